# Optimizing a Trainium2 kernel written in Bass

```python
import math
import jax, jax.numpy as jnp
from jax import lax
import numpy as np

D_MODEL = 2048
BATCH = 2
SEQ = 4096
DEPTH = 2
DEC_BATCH = 16
DEC_SEQ = 16
PAST_LEN = 2048

CHUNK = 64
N_MIXERS = 2
N_CONV_LAYERS = (DEPTH + 1) // 2
N_ATTN_LAYERS = DEPTH // 2
CONV_WIDTH = 31
CONV_STATE = CONV_WIDTH - 1
HEAD_DIM = 64
N_HEADS = D_MODEL // HEAD_DIM
N_KV_HEADS = N_HEADS // 8
GROUP = N_HEADS // N_KV_HEADS
QKV_DIM = (N_HEADS + 2 * N_KV_HEADS) * HEAD_DIM
WINDOW = 128
WIN_CHUNKS = WINDOW // CHUNK
BAND = WINDOW + CHUNK
N_BUCKETS = 32
MAX_DISTANCE = 128
D_FF = -(-8 * D_MODEL // (3 * 256)) * 256
EPS = 1e-6

kernel_name = "streaming_conformer_swa_sink_hybrid_step"


def rms_norm(x, g):
    xf = x.astype(jnp.float32)
    y = xf * lax.rsqrt(jnp.mean(xf * xf, axis=-1, keepdims=True) + EPS)
    return (y * g.astype(jnp.float32)).astype(x.dtype)


def layer_norm(x, g, b):
    xf = x.astype(jnp.float32)
    mu = jnp.mean(xf, axis=-1, keepdims=True)
    xc = xf - mu
    y = xc * lax.rsqrt(jnp.mean(xc * xc, axis=-1, keepdims=True) + EPS)
    return (y * g.astype(jnp.float32) + b.astype(jnp.float32)).astype(x.dtype)


def modulate(h, shift, scale):
    return h * (1 + scale[:, None, :]) + shift[:, None, :]


def t5_bucket(rel):
    nb = N_BUCKETS // 2
    max_exact = nb // 2
    ret = jnp.where(rel > 0, nb, 0)
    n = jnp.abs(rel)
    nf = jnp.maximum(n, 1).astype(jnp.float32)
    large = max_exact + (jnp.log(nf / max_exact) / math.log(MAX_DISTANCE / max_exact)
                         * (nb - max_exact)).astype(jnp.int32)
    large = jnp.minimum(large, nb - 1)
    return ret + jnp.where(n < max_exact, n, large)


def rel_bias(table, q_pos, k_pos):
    bucket = t5_bucket(k_pos[None, :] - q_pos[:, None])
    b = jnp.take(table.astype(jnp.float32), bucket, axis=1)
    return b.reshape(N_KV_HEADS, GROUP, q_pos.shape[0], k_pos.shape[0])


def sink_attention(q, k, v, bias, valid, sinks):
    s = jnp.einsum('bnqhgd,bnkhd->bnhgqk', q, k,
                   preferred_element_type=jnp.float32) * (HEAD_DIM ** -0.5)
    s = s + bias
    if valid is not None:
        s = jnp.where(valid, s, -jnp.inf)
    sink = jnp.broadcast_to(sinks.astype(jnp.float32).reshape(N_KV_HEADS, GROUP, 1, 1),
                            s.shape[:-1] + (1,))
    p = jax.nn.softmax(jnp.concatenate([s, sink], axis=-1), axis=-1)[..., :-1]
    return jnp.einsum('bnhgqk,bnkhd->bnqhgd', p.astype(v.dtype), v)


def qkv_proj(h, w_qkv, b_qkv):
    B, T, _ = h.shape
    qkv = h @ w_qkv + b_qkv
    qd = N_HEADS * HEAD_DIM
    kd = N_KV_HEADS * HEAD_DIM
    q = qkv[..., :qd].reshape(B, T, N_KV_HEADS, GROUP, HEAD_DIM)
    k = qkv[..., qd:qd + kd].reshape(B, T, N_KV_HEADS, HEAD_DIM)
    v = qkv[..., qd + kd:].reshape(B, T, N_KV_HEADS, HEAD_DIM)
    return q, k, v


def attn_prompt(h, w_qkv, b_qkv, w_o, b_o, sinks, table):
    B, S, _ = h.shape
    NC = S // CHUNK
    q, k, v = qkv_proj(h, w_qkv, b_qkv)
    qc = q.reshape(B, NC, CHUNK, N_KV_HEADS, GROUP, HEAD_DIM)
    pad = ((0, 0), (WIN_CHUNKS, 0), (0, 0), (0, 0), (0, 0))
    kp = jnp.pad(k.reshape(B, NC, CHUNK, N_KV_HEADS, HEAD_DIM), pad)
    vp = jnp.pad(v.reshape(B, NC, CHUNK, N_KV_HEADS, HEAD_DIM), pad)
    kb = jnp.concatenate([kp[:, j:j + NC] for j in range(WIN_CHUNKS + 1)], axis=2)
    vb = jnp.concatenate([vp[:, j:j + NC] for j in range(WIN_CHUNKS + 1)], axis=2)
    q_pos = jnp.arange(CHUNK)
    k_pos = jnp.arange(BAND) - WINDOW
    bias = rel_bias(table, q_pos, k_pos)
    key_abs = jnp.arange(NC)[:, None] * CHUNK + k_pos[None, :]
    valid = (key_abs >= 0).reshape(1, NC, 1, 1, 1, BAND)
    o = sink_attention(qc, kb, vb, bias, valid, sinks)
    y = o.reshape(B, S, N_HEADS * HEAD_DIM) @ w_o + b_o
    return y, k[:, S - WINDOW:], v[:, S - WINDOW:]


def attn_sample(h, cache_k, cache_v, w_qkv, b_qkv, w_o, b_o, sinks, table):
    B, T, _ = h.shape
    q, k, v = qkv_proj(h, w_qkv, b_qkv)
    k_all = jnp.concatenate([cache_k.astype(k.dtype), k], axis=1)
    v_all = jnp.concatenate([cache_v.astype(v.dtype), v], axis=1)
    q_pos = jnp.arange(T)
    k_pos = jnp.arange(WINDOW + T) - WINDOW
    bias = rel_bias(table, q_pos, k_pos)
    o = sink_attention(q[:, None], k_all[:, None], v_all[:, None], bias, None, sinks)
    y = o.reshape(B, T, N_HEADS * HEAD_DIM) @ w_o + b_o
    return y, k_all[:, T:], v_all[:, T:]


def conv_module(h, state, w_pw1, b_pw1, w_dw, b_dw, ln_g, ln_b, w_pw2, b_pw2):
    T = h.shape[1]
    a, gt = jnp.split(h @ w_pw1 + b_pw1, 2, axis=-1)
    u = a * jax.nn.sigmoid(gt)
    up = jnp.concatenate([state.astype(u.dtype), u], axis=1)
    z = lax.conv_general_dilated(up, w_dw[:, None, :].astype(up.dtype), window_strides=(1,),
                                 padding='VALID', dimension_numbers=('NWC', 'WIO', 'NWC'),
                                 feature_group_count=D_MODEL) + b_dw
    z = jax.nn.silu(layer_norm(z, ln_g, ln_b))
    return z @ w_pw2 + b_pw2, up[:, T:]


def swiglu(h, w_gu, w_down):
    g, u = jnp.split(h @ w_gu, 2, axis=-1)
    return (jax.nn.silu(g) * u) @ w_down


def run_trunk(x, c, conv_states, win_k, win_v, p):
    new_conv, new_k, new_v = [], [], []
    for i in range(DEPTH):
        mod = jax.nn.silu(c) @ p['w_mod'][i] + p['b_mod'][i]
        sh1, sc1, g1, sh2, sc2, g2 = jnp.split(mod, 6, axis=-1)
        h = modulate(rms_norm(x, p['norm_mix'][i]), sh1, sc1)
        j = i // N_MIXERS
        if i % N_MIXERS == 0:
            st = (jnp.zeros((x.shape[0], CONV_STATE, D_MODEL), x.dtype)
                  if conv_states is None else conv_states[j])
            y, s_new = conv_module(h, st, p['w_pw1'][j], p['b_pw1'][j], p['w_dw'][j], p['b_dw'][j],
                                   p['conv_ln_g'][j], p['conv_ln_b'][j], p['w_pw2'][j], p['b_pw2'][j])
            new_conv.append(s_new)
        else:
            if win_k is None:
                y, k_new, v_new = attn_prompt(h, p['w_qkv'][j], p['b_qkv'][j], p['w_o'][j],
                                              p['b_o'][j], p['attn_sinks'][j], p['rel_bias_table'])
            else:
                y, k_new, v_new = attn_sample(h, win_k[j], win_v[j], p['w_qkv'][j], p['b_qkv'][j],
                                              p['w_o'][j], p['b_o'][j], p['attn_sinks'][j],
                                              p['rel_bias_table'])
            new_k.append(k_new)
            new_v.append(v_new)
        x = x + g1[:, None, :] * y
        h = modulate(rms_norm(x, p['norm_ffn'][i]), sh2, sc2)
        x = x + g2[:, None, :] * swiglu(h, p['w_gu'][i], p['w_down'][i])
    return rms_norm(x, p['norm_out']), jnp.stack(new_conv), jnp.stack(new_k), jnp.stack(new_v)


def setup_inputs(seed: int = 0) -> dict:
    key = jax.random.key(seed)
    ks = jax.random.split(key, 32)

    def nrm(k, shape, scale):
        return jax.random.normal(k, shape, jnp.float32) * scale

    D = D_MODEL
    return {
        "x_prompt": nrm(ks[0], (BATCH, SEQ, D), 1.0),
        "x_sample": nrm(ks[1], (DEC_BATCH, DEC_SEQ, D), 1.0),
        "c_prompt": nrm(ks[2], (BATCH, D), 1.0),
        "c_sample": nrm(ks[3], (DEC_BATCH, D), 1.0),
        "state_conv": nrm(ks[4], (N_CONV_LAYERS, DEC_BATCH, CONV_STATE, D), 0.5),
        "cache_win_k": nrm(ks[5], (N_ATTN_LAYERS, DEC_BATCH, WINDOW, N_KV_HEADS, HEAD_DIM), 1.0),
        "cache_win_v": nrm(ks[6], (N_ATTN_LAYERS, DEC_BATCH, WINDOW, N_KV_HEADS, HEAD_DIM), 1.0),
        "w_mod": nrm(ks[7], (DEPTH, D, 6 * D), 0.5 * D ** -0.5),
        "b_mod": nrm(ks[8], (DEPTH, 6 * D), 0.02),
        "norm_mix": 1.0 + nrm(ks[9], (DEPTH, D), 0.02),
        "norm_ffn": 1.0 + nrm(ks[10], (DEPTH, D), 0.02),
        "w_pw1": nrm(ks[11], (N_CONV_LAYERS, D, 2 * D), D ** -0.5),
        "b_pw1": nrm(ks[12], (N_CONV_LAYERS, 2 * D), 0.02),
        "w_dw": nrm(ks[13], (N_CONV_LAYERS, CONV_WIDTH, D), CONV_WIDTH ** -0.5),
        "b_dw": nrm(ks[14], (N_CONV_LAYERS, D), 0.02),
        "conv_ln_g": 1.0 + nrm(ks[15], (N_CONV_LAYERS, D), 0.02),
        "conv_ln_b": nrm(ks[16], (N_CONV_LAYERS, D), 0.02),
        "w_pw2": nrm(ks[17], (N_CONV_LAYERS, D, D), D ** -0.5),
        "b_pw2": nrm(ks[18], (N_CONV_LAYERS, D), 0.02),
        "w_qkv": nrm(ks[19], (N_ATTN_LAYERS, D, QKV_DIM), D ** -0.5),
        "b_qkv": nrm(ks[20], (N_ATTN_LAYERS, QKV_DIM), 0.02),
        "w_o": nrm(ks[21], (N_ATTN_LAYERS, N_HEADS * HEAD_DIM, D), (N_HEADS * HEAD_DIM) ** -0.5),
        "b_o": nrm(ks[22], (N_ATTN_LAYERS, D), 0.02),
        "attn_sinks": nrm(ks[23], (N_ATTN_LAYERS, N_HEADS), 1.0),
        "rel_bias_table": nrm(ks[24], (N_HEADS, N_BUCKETS), 0.5),
        "w_gu": nrm(ks[25], (DEPTH, D, 2 * D_FF), D ** -0.5),
        "w_down": nrm(ks[26], (DEPTH, D_FF, D), D_FF ** -0.5),
        "norm_out": 1.0 + nrm(ks[27], (D,), 0.02),
    }


def reference(x_prompt, x_sample, c_prompt, c_sample, state_conv, cache_win_k, cache_win_v,
              w_mod, b_mod, norm_mix, norm_ffn, w_pw1, b_pw1, w_dw, b_dw, conv_ln_g, conv_ln_b,
              w_pw2, b_pw2, w_qkv, b_qkv, w_o, b_o, attn_sinks, rel_bias_table, w_gu, w_down,
              norm_out):
    p = {
        'w_mod': w_mod, 'b_mod': b_mod, 'norm_mix': norm_mix, 'norm_ffn': norm_ffn,
        'w_pw1': w_pw1, 'b_pw1': b_pw1, 'w_dw': w_dw, 'b_dw': b_dw,
        'conv_ln_g': conv_ln_g, 'conv_ln_b': conv_ln_b, 'w_pw2': w_pw2, 'b_pw2': b_pw2,
        'w_qkv': w_qkv, 'b_qkv': b_qkv, 'w_o': w_o, 'b_o': b_o, 'attn_sinks': attn_sinks,
        'rel_bias_table': rel_bias_table, 'w_gu': w_gu, 'w_down': w_down, 'norm_out': norm_out,
    }
    y_prompt, conv_prompt, win_k_prompt, win_v_prompt = run_trunk(x_prompt, c_prompt, None, None, None, p)
    y_sample, conv_sample, win_k_sample, win_v_sample = run_trunk(
        x_sample, c_sample, state_conv, cache_win_k, cache_win_v, p)
    return (y_prompt, y_sample, conv_prompt, win_k_prompt, win_v_prompt,
            conv_sample, win_k_sample, win_v_sample)
```

```python
import numpy as np
import concourse.bass as bass
import concourse.mybir as mybir
from concourse.bass_utils import run_bass_kernel_spmd
from contextlib import ExitStack, suppress

F32 = mybir.dt.float32
BF16 = mybir.dt.bfloat16
U8 = mybir.dt.uint8
AF = mybir.ActivationFunctionType
ALU = mybir.AluOpType

D = 2048
KC = 16
DFF = 5632
NFF = 44
SEQ = 4096
NCORES = 8
EPS = 1e-6
NX = 1184
NH = 1216
NU = 1276
NQ = 1056
NKT = 1440
GRP = 8

VO = {}
_o = 0
for _n, _c in [("bmod", 192), ("nmix", 32), ("nffn", 32), ("bpa", 16), ("bpg", 16), ("bdw", 16), ("lng", 16),
               ("lnb", 16), ("bpw2", 16), ("bq", 16), ("bkd", 4), ("bo", 16), ("nout", 16)]:
    VO[_n] = _o
    _o += _c
NVEC = _o


def t5_bucket_np(rel):
    nb = 16
    max_exact = 8
    ret = np.where(rel > 0, nb, 0)
    n = np.abs(rel)
    nf = np.maximum(n, 1).astype(np.float32)
    large = max_exact + (np.log(nf / max_exact) / np.float32(np.log(128 / max_exact)) * (nb - max_exact)).astype(np.int32)
    large = np.minimum(large, nb - 1)
    return ret + np.where(n < max_exact, n, large)


STOP = None


class _StopBuild(Exception):
    pass


class Eng:
    def __init__(self, nc, eng, name, es):
        self.e = eng
        self.sem = es.enter_context(nc.semaphore(name))
        self.n = 0
        self.last = None
        self.marked = True
        self.seen = {}

    def emit(self, inst):
        self.last = inst
        self.marked = False
        return inst

    def tick(self):
        if not self.marked:
            self.n += 1
            self.last.then_inc(self.sem, 1)
            self.marked = True
        return (self.sem, self.n)

    def wait(self, *ts):
        for t in ts:
            if t is None:
                continue
            sem, v = t
            if v <= 0:
                continue
            k = id(sem)
            if self.seen.get(k, 0) >= v:
                continue
            self.e.wait_ge(sem, v)
            self.seen[k] = v


def build_nc():
    nc = bass.Bass("TRN2", target_bir_lowering=False)
    dt_in = lambda n, s: nc.dram_tensor(n, list(s), F32, kind="ExternalInput").ap()
    dt_out = lambda n, s: nc.dram_tensor(n, list(s), F32, kind="ExternalOutput").ap()
    xT = dt_in("xT", (D, NH))
    cT = dt_in("cT", (128, KC, 3))
    flag_d = dt_in("flag", (128, 1))
    stT = dt_in("stT", (2, D, 30))
    ckT = dt_in("ckT", (2, 4, 128, 128))
    ck = dt_in("ck", (2, 128, 256))
    cv = dt_in("cv", (2, 128, 256))
    vecsT = dt_in("vecsT", (128, NVEC))
    wdwT = dt_in("wdwT", (128, 31 * 16))
    bkv_d = dt_in("bkv", (1, 512))
    ohrel = dt_in("ohrel", (32, 255))
    tableT = dt_in("tableT", (32, 32))
    sinks_d = dt_in("sinks", (1, 32))
    identf = dt_in("identf", (128, 128))
    wmod = dt_in("wmod", (96, 128, KC, 256))
    wpw1 = dt_in("wpw1", (16, 128, KC, 256))
    wpw2 = dt_in("wpw2", (8, 128, KC, 256))
    wqkv = dt_in("wqkv", (10, 128, KC, 256))
    wo = dt_in("wo", (8, 128, KC, 256))
    wgu = dt_in("wgu", (88, 128, KC, 256))
    wdn = dt_in("wdn", (96, 128, GRP, 256))

    yT = dt_out("yT", (D, NQ))
    ucT = dt_out("ucT", (D, 62))
    cs_head = dt_out("cs_head", (2, D, 14))
    kp_o = dt_out("kp_o", (128, 256))
    vp_o = dt_out("vp_o", (128, 256))
    ks_o = dt_out("ks_o", (2, 128, 256))
    vs_o = dt_out("vs_o", (2, 128, 256))

    es = ExitStack()
    with es, suppress(_StopBuild):
        sb = lambda n, s, d: es.enter_context(nc.sbuf_tensor(n, list(s), d))
        X = sb("X", (128, KC, NX), F32)
        HBraw = sb("HBraw", (128, KC * NH * 2), U8)
        UBraw = sb("UBraw", (128, KC * NU * 2), U8)
        RING = sb("RING", (128, 3, KC, 256), BF16)
        SCR = sb("SCR", (128, 23808), U8)
        VTs = sb("VTs", (128, NVEC), F32)
        MOD = sb("MOD", (128, 2, 96, 3), F32)
        AM = sb("AM", (128, KC, 3), F32)
        G1B = sb("G1B", (128, KC, 3), F32)
        IDB = sb("IDB", (128, 128), BF16)
        IDF = sb("IDF", (128, 128), F32)
        ONESB = sb("ONESB", (128, 128), BF16)
        FLAG = sb("FLAG", (128, 1), F32)
        FLAGB = sb("FLAGB", (128, 64), BF16)
        CTs = sb("CTs", (128, KC, 3), F32)
        SCT = sb("SCT", (128, KC, 3), BF16)
        UFs = sb("UFs", (128, 2, 62), F32)
        ESK = sb("ESK", (1, 2, 32), BF16)
        ESF = sb("ESF", (1, 3, 32), F32)
        ES64 = sb("ES64", (128, 32), F32)
        FLAGH = sb("FLAGH", (128, 64), BF16)
        ONE1 = sb("ONE1", (1, 64), BF16)
        EPSC = sb("EPSC", (128, 1), F32)
        PS = [es.enter_context(nc.psum_tensor(f"ps{i}", [128, 512], F32)) for i in range(8)]

        def carve(raw, off, shape, dtype):
            n = int(np.prod(shape[1:]))
            bs = 2 if dtype == BF16 else 4
            v = raw[:, off:off + n * bs].bitcast(dtype)
            if len(shape) == 3:
                v = v.rearrange("p (a b) -> p a b", b=shape[2])
            elif len(shape) == 4:
                v = v.rearrange("p (a b c) -> p a b c", b=shape[2], c=shape[3])
            return v

        HB = carve(HBraw, 0, (128, KC, NH), BF16)
        UB = carve(UBraw, 0, (128, KC, NU), BF16)
        QT = carve(UBraw, 0, (128, KC, NQ), BF16)
        ACTR = carve(UBraw, 0, (128, 16, NX), BF16)
        KVST = carve(UBraw, 33792, (128, 2, 256), F32)
        BKVB = carve(UBraw, 33792 + 2048, (128, 512), F32)
        XP = carve(SCR, 0, (128, KC, 32), F32)
        RSTD = carve(SCR, 2048, (128, NH), F32)
        T1 = carve(SCR, 6912, (128, NH), F32)
        T1B = carve(SCR, 16640, (128, NH), F32)
        SQ = carve(SCR, 11776, (128, 2, NH), BF16)
        SG = carve(SCR, 16640, (128, 2, 512), F32)
        TMPE = carve(SCR, 20736, (128, 512), F32)
        KT = carve(SCR, 0, (128, 4, NKT), BF16)
        VTM = carve(SCR, 11520, (128, 22, 256), BF16)
        YST = carve(SCR, 0, (128, 2, NQ), F32)
        Z = carve(HBraw, 0, (128, KC, 416), F32)
        ZB = carve(HBraw, 26624, (128, 2, 416), BF16)
        ZSQ = carve(HBraw, 28288, (128, 2, 416), BF16)
        MEAN = carve(SCR, 0, (128, 416), F32)
        MSQ = carve(SCR, 1664, (128, 416), F32)
        RSC = carve(SCR, 3328, (128, 416), F32)
        WDW = carve(SCR, 4992, (128, 31, 16), F32)
        DG = carve(SCR, 7168, (128, 2, 31, 128), BF16)
        BIAS = carve(HBraw, 0, (128, 4, 2, 512), F32)
        PT = carve(HBraw, 16384, (128, 6, 512), BF16)
        SS = carve(HBraw, 22528, (128, 3, 512), F32)
        RDEN = carve(HBraw, 28672, (128, 2, 256), F32)
        OHS = carve(HBraw, 30720, (128, 255), F32)
        TBT = carve(HBraw, 31744, (128, 32), F32)

        PE = Eng(nc, nc.tensor, "s_pe", es)
        ACT = Eng(nc, nc.scalar, "s_act", es)
        DVE = Eng(nc, nc.vector, "s_dve", es)
        SP = Eng(nc, nc.sync, "s_sp", es)
        POOL = Eng(nc, nc.gpsimd, "s_pool", es)
        ld_sem = es.enter_context(nc.semaphore("ld"))
        ld2_sem = es.enter_context(nc.semaphore("ld2"))
        st_sem = es.enter_context(nc.semaphore("st"))
        kvst_sem = [es.enter_context(nc.semaphore(f"kvst{i}")) for i in range(2)]
        uf_sem = [es.enter_context(nc.semaphore(f"uf{i}")) for i in range(2)]
        yst_sem = [es.enter_context(nc.semaphore(f"yst{i}")) for i in range(2)]
        slot_sem = [es.enter_context(nc.semaphore(f"slot{i}")) for i in range(3)]
        cnt = {}

        def dma(engw, out, in_, sem):
            engw.e.dma_start(out=out, in_=in_).then_inc(sem, 16)
            cnt[id(sem)] = cnt.get(id(sem), 0) + 1
            return (sem, 16 * cnt[id(sem)])

        def mm(out, lhsT, rhs, start, stop, **kw):
            return PE.emit(nc.tensor.matmul(out, lhsT=lhsT, rhs=rhs, start=start, stop=stop, **kw))

        def act(out, in_, func, **kw):
            return ACT.emit(nc.scalar.activation(out=out, in_=in_, func=func, **kw))

        def barrier():
            ts = [PE.tick(), ACT.tick(), DVE.tick()]
            for e in (PE, ACT, DVE):
                e.wait(*ts)
            return ts

        def checkpoint(stage):
            if STOP == stage:
                barrier()
                SP.wait(PE.tick(), ACT.tick(), DVE.tick())
                POOL.wait(PE.tick())
                for sem in [st_sem, ld_sem, ld2_sem] + uf_sem + kvst_sem + yst_sem + slot_sem:
                    if cnt.get(id(sem)):
                        SP.wait((sem, 16 * cnt[id(sem)]))
                raise _StopBuild()

        plan = []

        def wblock(name, Wt, blk, nk):
            plan.append((name, nk, 256, Wt[blk, :, 0:nk, :]))

        for b in range(16):
            wblock(f"mod0_{b}", wmod, b, KC)
        for c in range(16):
            wblock(f"pw1_{c}", wpw1, c, KC)
            wblock(f"mod0_{16 + c}", wmod, 16 + c, KC)
        for b in range(32, 48):
            wblock(f"mod0_{b}", wmod, b, KC)
        for b in range(8):
            wblock(f"pw2_{b}", wpw2, b, KC)
        for li in range(2):
            if li == 1:
                wblock("kv_k", wqkv, 8, KC)
                wblock("kv_v", wqkv, 9, KC)
                for b in range(8):
                    wblock(f"q_{b}", wqkv, b, KC)
                for b in range(8):
                    wblock(f"wo_{b}", wo, b, KC)
            ngrp = (NFF + GRP - 1) // GRP
            for G in range(ngrp):
                j0, j1 = G * GRP, min(NFF, (G + 1) * GRP)
                for j in range(j0, j1):
                    wblock(f"gu{li}_{j}", wgu, 44 * li + j, KC)
                    if li == 0:
                        wblock(f"mod1_{j}", wmod, 48 + j, KC)
                        if j == NFF - 1:
                            for b in range(NFF, 48):
                                wblock(f"mod1_{b}", wmod, 48 + b, KC)
                for b in range(8):
                    wblock(f"dn{li}_{G}_{b}", wdn, (li * 6 + G) * 8 + b, j1 - j0)

        ring = {"next_pf": 0, "next_use": 0, "free": [None, None, None], "load": {}, "hold": None}

        def ring_prefetch(i):
            name, nk, ncols, src = plan[i]
            s = i % 3
            POOL.wait(ring["free"][s])
            if ring["hold"] is not None and name == "kv_k":
                POOL.wait(*ring["hold"])
            ring["load"][i] = dma(POOL, RING[:, s, 0:nk, 0:ncols], src, slot_sem[s])

        def ring_use(name):
            i = ring["next_use"]
            assert plan[i][0] == name, (plan[i][0], name)
            while ring["next_pf"] <= min(i + 2, len(plan) - 1):
                ring_prefetch(ring["next_pf"])
                ring["next_pf"] += 1
            PE.wait(ring["load"][i])
            ring["next_use"] += 1
            return RING[:, i % 3], i

        def ring_release(i):
            ring["free"][i % 3] = PE.tick()

        xv = xT.rearrange("(c p) t -> p c t", p=128)
        for q in range(4):
            dma(SP, X[:, 4 * q:4 * q + 4, :], xv[:, 4 * q:4 * q + 4, 32:NH], ld_sem)
        dma(SP, XP, xv[:, :, 0:32], ld_sem)
        dma(SP, CTs[:], cT, ld_sem)
        dma(SP, FLAG[:], flag_d, ld_sem)
        dma(SP, VTs[:], vecsT, ld_sem)
        dma(SP, IDF[:], identf, ld_sem)
        t_ld = (ld_sem, 16 * cnt[id(ld_sem)])
        stv = stT.rearrange("b (c p) w -> b p c w", p=128)
        t_st = None
        for b in range(2):
            t_st = dma(POOL, UB[:, :, 1184 + 46 * b:1184 + 46 * b + 30], stv[b], ld2_sem)
        for b in range(2):
            dma(SP, cs_head[b], stT[b, :, 16:30], st_sem)
        for b in range(2):
            dma(SP, ks_o[b, 0:112, :], ck[b, 16:128, :], st_sem)
            dma(SP, vs_o[b, 0:112, :], cv[b, 16:128, :], st_sem)

        DVE.wait(t_ld)
        ACT.wait(t_ld)
        DVE.emit(nc.vector.memset(ONESB[:], 1.0))
        DVE.emit(nc.vector.memset(ONE1[:], 1.0))
        DVE.emit(nc.vector.memset(EPSC[:], EPS))
        DVE.emit(nc.vector.tensor_copy(out=IDB[:], in_=IDF[:]))
        DVE.emit(nc.vector.tensor_copy(out=FLAGB[:], in_=FLAG[:, 0:1].to_broadcast([128, 64])))
        DVE.emit(nc.vector.memset(FLAGH[64:128, :], 1.0))
        DVE.emit(nc.vector.tensor_copy(out=FLAGH[0:64, :], in_=FLAG[0:64, 0:1].to_broadcast([64, 64])))
        act(SCT[:], CTs[:], AF.Silu)
        barrier()
        checkpoint(0)

        def mod_block(li, b, pbase=4, defer=False):
            slot, bi = ring_use(f"mod{li}_{b}")
            bank = PS[pbase + (mod_block.k % 2)]
            PE.wait(mod_block.free[mod_block.k % 2])
            for h in range(2):
                for k in range(KC):
                    mm(bank[:, 4 * h:4 * h + 3], slot[:, k, 128 * h:128 * h + 128], SCT[:, k, :], k == 0, k == KC - 1)
            ring_release(bi)
            tp = PE.tick()
            kslot = mod_block.k % 2
            mod_block.k += 1

            def evac():
                DVE.wait(tp)
                for h in range(2):
                    n = 2 * b + h
                    DVE.emit(nc.vector.tensor_scalar_add(out=MOD[:, li, n, :], in0=bank[:, 4 * h:4 * h + 3],
                                                         scalar1=VTs[:, VO["bmod"] + 96 * li + n:VO["bmod"] + 96 * li + n + 1]))
                mod_block.free[kslot] = DVE.tick()
            if defer:
                return evac
            evac()
            return None
        mod_block.k = 0
        mod_block.free = [None, None]

        SEGH = [(0, 1184, 0), (1184, 1200, 1), (1200, 1216, 2)]

        def segs(lo, hi, off=0):
            out = []
            for (a, b_, bi) in [(0, 1152, 0), (1152, 1168, 1), (1168, 1184, 2)]:
                a2, b2 = max(lo, a + off if False else a), min(hi, b_)
                if a2 < b2:
                    out.append((a2, b2, bi))
            return out

        def rms_modulate(li, which, with_pre):
            nv = VO["nmix"] if which == 0 else VO["nffn"]
            sh0, sc0 = (0, 16) if which == 0 else (48, 64)
            c0 = 0 if with_pre else 32
            tiles = [(c0, 512), (512, 1024), (1024, NH)]
            banks = [PS[5], PS[6], PS[7]]
            for c in range(KC):
                sqb = SQ[:, c % 2, :]
                ACT.wait(rms_modulate.sq_free[c % 2])
                if with_pre:
                    act(sqb[:, 0:32], XP[:, c, :], AF.Square)
                act(sqb[:, 32:NH], X[:, c, :], AF.Square)
                ta = ACT.tick()
                PE.wait(ta)
                for ti, (a, b_) in enumerate(tiles):
                    mm(banks[ti][:, 0:b_ - a], ONESB[:], sqb[:, a:b_], c == 0, c == KC - 1)
                rms_modulate.sq_free[c % 2] = PE.tick()
            tp = PE.tick()
            ACT.wait(tp)
            for ti, (a, b_) in enumerate(tiles):
                act(RSTD[:, a:b_], banks[ti][:, 0:b_ - a], AF.Sqrt, bias=EPSC[:, 0:1], scale=1.0 / D)
            ta = ACT.tick()
            DVE.wait(ta)
            DVE.emit(nc.vector.reciprocal(out=RSTD[:, c0:NH], in_=RSTD[:, c0:NH]))
            for c in range(KC):
                DVE.emit(nc.vector.tensor_scalar(out=AM[:, c, :], in0=MOD[:, li, sc0 + c, :], scalar1=1.0,
                                                 scalar2=VTs[:, nv + 16 * li + c:nv + 16 * li + c + 1],
                                                 op0=ALU.add, op1=ALU.mult))
            td = DVE.tick()
            DVE.wait(td)
            ACT.wait(td)
            tprev = [None, None]
            for c in range(KC):
                Tc = T1 if c % 2 == 0 else T1B
                DVE.wait(tprev[c % 2])
                if with_pre:
                    DVE.emit(nc.vector.tensor_tensor(out=Tc[:, 0:32], in0=XP[:, c, :], in1=RSTD[:, 0:32], op=ALU.mult))
                DVE.emit(nc.vector.tensor_tensor(out=Tc[:, 32:NH], in0=X[:, c, :], in1=RSTD[:, 32:NH], op=ALU.mult))
                td = DVE.tick()
                ACT.wait(td)
                for (a, b_, bi) in SEGH:
                    a2 = max(a, c0)
                    act(HB[:, c, a2:b_], Tc[:, a2:b_], AF.Identity, scale=AM[:, c, bi:bi + 1],
                        bias=MOD[:, li, sh0 + c, bi:bi + 1])
                tprev[c % 2] = ACT.tick()
        rms_modulate.sq_free = [None, None]

        def resid_epilogue(bank, ncols, n, xlo, gsrc, bvec):
            ACT.wait(resid_epilogue.tfree)
            for (a, b_, bi) in segs(xlo, xlo + ncols):
                if bvec is None:
                    act(TMPE[:, a - xlo:b_ - xlo], bank[:, a - xlo:b_ - xlo], AF.Identity, scale=gsrc[:, n, bi:bi + 1])
                else:
                    act(TMPE[:, a - xlo:b_ - xlo], bank[:, a - xlo:b_ - xlo], AF.Identity,
                        scale=gsrc[:, n, bi:bi + 1], bias=bvec[:, n, bi:bi + 1])
            ta = ACT.tick()
            DVE.wait(ta)
            DVE.emit(nc.vector.tensor_tensor(out=X[:, n, xlo:xlo + ncols], in0=X[:, n, xlo:xlo + ncols],
                                             in1=TMPE[:, 0:ncols], op=ALU.add))
            resid_epilogue.tfree = DVE.tick()
            return resid_epilogue.tfree
        resid_epilogue.tfree = None

        def proj_resid(wname, rhs_of, tiles, gch, bname, li):
            for n in range(KC):
                DVE.emit(nc.vector.tensor_scalar_mul(out=G1B[:, n, :], in0=MOD[:, li, gch + n, :],
                                                     scalar1=VTs[:, VO[bname] + n:VO[bname] + n + 1]))
            tg = DVE.tick()
            ACT.wait(tg)
            bfree = [None, None]
            kk = 0
            for b in range(8):
                slot, bi = ring_use(f"{wname}_{b}")
                for h in range(2):
                    n = 2 * b + h
                    for (xlo, ncols) in tiles:
                        bank = PS[kk % 2]
                        PE.wait(bfree[kk % 2])
                        for k in range(KC):
                            mm(bank[:, 0:ncols], slot[:, k, 128 * h:128 * h + 128], rhs_of(k, xlo, ncols), k == 0, k == KC - 1)
                        tp = PE.tick()
                        ACT.wait(tp)
                        resid_epilogue(bank, ncols, n, xlo, MOD[:, li, gch:gch + 16, :], G1B)
                        bfree[kk % 2] = ACT.tick()
                        kk += 1
                ring_release(bi)

        def ffn(li, xlo_all, tiles):
            gch = 80
            ngrp = (NFF + GRP - 1) // GRP
            gfree = [None, None]
            ufree = [None, None]
            sgfree = [None, None]
            dfree = [None, None]
            kk = 0
            dk = 0
            for G in range(ngrp):
                j0, j1 = G * GRP, min(NFF, (G + 1) * GRP)
                for j in range(j0, j1):
                    slot, bi = ring_use(f"gu{li}_{j}")
                    aslot = (G % 2) * GRP + (j - j0)
                    for (xlo, ncols) in tiles:
                        bg, bu = PS[kk % 2], PS[2 + kk % 2]
                        PE.wait(gfree[kk % 2], ufree[kk % 2])
                        for k in range(KC):
                            mm(bg[:, 0:ncols], slot[:, k, 0:128], HB[:, k, 32 + xlo:32 + xlo + ncols], k == 0, k == KC - 1)
                        tpg = PE.tick()
                        for k in range(KC):
                            mm(bu[:, 0:ncols], slot[:, k, 128:256], HB[:, k, 32 + xlo:32 + xlo + ncols], k == 0, k == KC - 1)
                        tpu = PE.tick()
                        ACT.wait(tpg, sgfree[kk % 2])
                        act(SG[:, kk % 2, 0:ncols], bg[:, 0:ncols], AF.Silu)
                        gfree[kk % 2] = ACT.tick()
                        DVE.wait(gfree[kk % 2], tpu)
                        DVE.emit(nc.vector.tensor_tensor(out=ACTR[:, aslot, xlo:xlo + ncols], in0=bu[:, 0:ncols],
                                                         in1=SG[:, kk % 2, 0:ncols], op=ALU.mult))
                        ufree[kk % 2] = DVE.tick()
                        sgfree[kk % 2] = ufree[kk % 2]
                        kk += 1
                    ring_release(bi)
                    if li == 0:
                        mod_block(1, j)
                        if j == NFF - 1:
                            for b in range(NFF, 48):
                                mod_block(1, b)
                t_act = DVE.tick()
                PE.wait(t_act)
                for b in range(8):
                    slot, bi = ring_use(f"dn{li}_{G}_{b}")
                    for h in range(2):
                        n = 2 * b + h
                        for (xlo, ncols) in tiles:
                            bank = PS[6 + dk % 2]
                            PE.wait(dfree[dk % 2])
                            for jj in range(j1 - j0):
                                mm(bank[:, 0:ncols], slot[:, jj, 128 * h:128 * h + 128],
                                   ACTR[:, (G % 2) * GRP + jj, xlo:xlo + ncols], jj == 0, jj == j1 - j0 - 1)
                            tp = PE.tick()
                            ACT.wait(tp)
                            resid_epilogue(bank, ncols, n, xlo, MOD[:, li, gch:gch + 16, :], None)
                            dfree[dk % 2] = ACT.tick()
                            dk += 1
                    ring_release(bi)

        for b in range(16):
            mod_block(0, b)
        barrier()
        checkpoint(1)
        rms_modulate(0, 0, True)
        barrier()
        checkpoint(2)

        DVE.wait(t_st)
        tiles1 = [(0, 512), (512, 1024), (1024, NH)]
        afree = [None, None]
        gfree = [None, None]
        sgfree = [None, None]
        uf_t = [None, None]
        kk = 0
        for c in range(KC):
            slot, bi = ring_use(f"pw1_{c}")
            for (a, b_) in tiles1:
                ncols = b_ - a
                ba, bgt = PS[kk % 2], PS[2 + kk % 2]
                PE.wait(afree[kk % 2], gfree[kk % 2])
                for k in range(KC):
                    mm(ba[:, 0:ncols], slot[:, k, 0:128], HB[:, k, a:b_], k == 0, k == KC - 1)
                tpa = PE.tick()
                for k in range(KC):
                    mm(bgt[:, 0:ncols], slot[:, k, 128:256], HB[:, k, a:b_], k == 0, k == KC - 1)
                tpg = PE.tick()
                ACT.wait(tpg, sgfree[kk % 2])
                act(SG[:, kk % 2, 0:ncols], bgt[:, 0:ncols], AF.Sigmoid, bias=VTs[:, VO["bpg"] + c:VO["bpg"] + c + 1])
                gfree[kk % 2] = ACT.tick()
                DVE.wait(gfree[kk % 2], tpa)
                ba_s = VTs[:, VO["bpa"] + c:VO["bpa"] + c + 1]

                def glu(out, lo, hi):
                    DVE.emit(nc.vector.scalar_tensor_tensor(out=out, in0=ba[:, lo:hi], scalar=ba_s,
                                                            in1=SG[:, kk % 2, lo:hi], op0=ALU.add, op1=ALU.mult))
                if a < 1024:
                    glu(UB[:, c, a:b_], 0, ncols)
                else:
                    glu(UB[:, c, 1024:1184], 0, 160)
                    glu(UB[:, c, 1214:1230], 160, 176)
                    glu(UB[:, c, 1260:1276], 176, 192)
                    DVE.wait(uf_t[c % 2])
                    glu(UFs[:, c % 2, 0:30], 130, 160)
                    glu(UFs[:, c % 2, 30:62], 160, 192)
                    tu = DVE.tick()
                    SP.wait(tu)
                    uf_t[c % 2] = dma(SP, ucT[128 * c:128 * c + 128, :], UFs[:, c % 2, :], uf_sem[c % 2])
                afree[kk % 2] = DVE.tick()
                sgfree[kk % 2] = afree[kk % 2]
                if a == 0:
                    DVE.wait(afree[kk % 2])
                    DVE.emit(nc.vector.tensor_scalar_mul(out=UB[:, c, 0:160], in0=UB[:, c, 0:160], scalar1=FLAG[:, 0:1]))
                kk += 1
            ring_release(bi)
            mod_block(0, 16 + c)
        barrier()
        checkpoint(3)

        SP.wait(PE.tick(), ACT.tick(), DVE.tick())
        tw = dma(SP, WDW, wdwT.rearrange("p (w c) -> p w c", c=16), ld_sem)
        DVE.wait(tw)
        ACT.wait(tw)
        ctiles = [(0, 384), (384, 384), (768, 416)]
        units = [(zlo, zn, c) for (zlo, zn) in ctiles for c in range(KC)]
        dgfree = [None, None]
        cfree = [None, None, None, None]
        zdone = [None] * KC
        zbfree = [None, None]
        conv_t = {}
        st_read = [None]
        NDV = 26

        def emit_conv(u):
            zlo, zn, c = units[u]
            b = u % 2
            samp = zn > 384
            DVE.wait(dgfree[b])
            ACT.wait(dgfree[b])
            DVE.emit(nc.vector.tensor_tensor(out=DG[:, b, 0:NDV, :], in0=IDB[:].unsqueeze(1).to_broadcast([128, NDV, 128]),
                                             in1=WDW[:, 0:NDV, c:c + 1].to_broadcast([128, NDV, 128]), op=ALU.mult))
            td = DVE.tick()
            for w in range(NDV, 31):
                act(DG[:, b, w, :], IDB[:], AF.Identity, scale=WDW[:, w, c:c + 1])
            ta = ACT.tick()
            bank = PS[u % 4]
            PE.wait(td, ta, cfree[u % 4])
            for w in range(31):
                mm(bank[:, 0:384], DG[:, b, w, :], UB[:, c, zlo + 2 + w:zlo + 2 + w + 384], w == 0, w == 30)
            if samp:
                for w in range(31):
                    mm(bank[:, 384:416], DG[:, b, w, :],
                       UB[:, c, 1184:1276].rearrange("p (b t) -> p b t", t=46)[:, :, w:w + 16], w == 0, w == 30)
            conv_t[u] = PE.tick()
            dgfree[b] = conv_t[u]

        def emit_post(u):
            zlo, zn, c = units[u]
            b = u % 2
            bank = PS[u % 4]
            bdw = VTs[:, VO["bdw"] + c:VO["bdw"] + c + 1]
            ACT.wait(conv_t[u], zdone[c])
            act(Z[:, c, 0:zn], bank[:, 0:zn], AF.Identity, bias=bdw)
            tz = ACT.tick()
            cfree[u % 4] = tz
            DVE.wait(tz, zbfree[b])
            DVE.emit(nc.vector.tensor_copy(out=ZB[:, b, 0:zn], in_=Z[:, c, 0:zn]))
            td = DVE.tick()
            ACT.wait(tz, zbfree[b])
            act(ZSQ[:, b, 0:zn], Z[:, c, 0:zn], AF.Square)
            ta = ACT.tick()
            PE.wait(td, ta)
            if c == 0:
                PE.wait(st_read[0])
            mm(PS[4][:, 0:zn], ONESB[:], ZB[:, b, 0:zn], c == 0, c == KC - 1)
            mm(PS[5][:, 0:zn], ONESB[:], ZSQ[:, b, 0:zn], c == 0, c == KC - 1)
            zbfree[b] = PE.tick()
            if c != KC - 1:
                return
            ts = zbfree[b]
            DVE.wait(ts)
            DVE.emit(nc.vector.tensor_scalar_mul(out=MEAN[:, 0:zn], in0=PS[4][:, 0:zn], scalar1=1.0 / D))
            t1_ = DVE.tick()
            DVE.wait(t1_)
            DVE.emit(nc.vector.tensor_tensor(out=MSQ[:, 0:zn], in0=MEAN[:, 0:zn], in1=MEAN[:, 0:zn], op=ALU.mult))
            t2_ = DVE.tick()
            DVE.wait(t2_)
            DVE.emit(nc.vector.scalar_tensor_tensor(out=RSC[:, 0:zn], in0=PS[5][:, 0:zn], scalar=1.0 / D, in1=MSQ[:, 0:zn],
                                                    op0=ALU.mult, op1=ALU.subtract))
            t3_ = DVE.tick()
            st_read[0] = t3_
            ACT.wait(t3_)
            act(RSC[:, 0:zn], RSC[:, 0:zn], AF.Sqrt, bias=EPSC[:, 0:1], scale=1.0)
            t4_ = ACT.tick()
            DVE.wait(t4_)
            DVE.emit(nc.vector.reciprocal(out=RSC[:, 0:zn], in_=RSC[:, 0:zn]))
            t5_ = DVE.tick()
            DVE.wait(t5_)
            for cc in range(KC):
                DVE.emit(nc.vector.tensor_tensor(out=Z[:, cc, 0:zn], in0=Z[:, cc, 0:zn], in1=MEAN[:, 0:zn], op=ALU.subtract))
                tq = DVE.tick()
                DVE.wait(tq)
                DVE.emit(nc.vector.tensor_tensor(out=Z[:, cc, 0:zn], in0=Z[:, cc, 0:zn], in1=RSC[:, 0:zn], op=ALU.mult))
                tq = DVE.tick()
                ACT.wait(tq)
                act(UB[:, cc, zlo:zlo + zn], Z[:, cc, 0:zn], AF.Silu, scale=VTs[:, VO["lng"] + cc:VO["lng"] + cc + 1],
                    bias=VTs[:, VO["lnb"] + cc:VO["lnb"] + cc + 1])
                zdone[cc] = ACT.tick()

        emit_conv(0)
        nmod = 32
        pending = None
        for u in range(len(units)):
            if u + 1 < len(units):
                emit_conv(u + 1)
            if pending is not None:
                pending()
                pending = None
            emit_post(u)
            if u % 2 == 1 and nmod < 48:
                pending = mod_block(0, nmod, pbase=6, defer=True)
                nmod += 1
        if pending is not None:
            pending()
        while nmod < 48:
            mod_block(0, nmod, pbase=6)
            nmod += 1
        barrier()
        checkpoint(4)

        tilesX = [(0, 400), (400, 400), (800, 384)]
        proj_resid("pw2", lambda k, xlo, n_: UB[:, k, xlo:xlo + n_], tilesX, 32, "bpw2", 0)
        barrier()
        checkpoint(5)

        rms_modulate(0, 1, False)
        barrier()
        ffn(0, 0, tilesX)
        barrier()
        checkpoint(6)

        rms_modulate(1, 0, False)
        tb = barrier()
        checkpoint(7)
        ring["hold"] = tb
        POOL.wait(*tb)
        t_c = None
        for b in range(2):
            dma(POOL, KT[:, :, 1152 + 144 * b:1152 + 144 * b + 128], ckT[b].rearrange("g p k -> p g k"), ld2_sem)
            t_c = dma(POOL, VTM[:, 18 + 2 * b, :], cv[b], ld2_sem)
        SP.wait(*tb)
        t_b = dma(SP, BKVB[:, :], bkv_d.partition_broadcast(128), ld_sem)

        tilesK = [(0, 512), (512, 512), (1024, 160)]
        slot, bi = ring_use("kv_k")
        kfree = [None, None]
        kk = 0

        def ktcol(xlo):
            return xlo if xlo < 1152 else (1280 if xlo < 1168 else 1424)
        for g in range(4):
            for (xlo, ncols) in tilesK:
                bank = PS[kk % 2]
                PE.wait(kfree[kk % 2])
                for half in range(2):
                    for k in range(KC):
                        mm(bank[64 * half:64 * half + 64, 0:ncols], slot[:, k, 64 * g:64 * g + 64],
                           HB[:, k, 32 + xlo:32 + xlo + ncols], k == 0, k == KC - 1, tile_position=(0, 64 * half))
                tp = PE.tick()
                ACT.wait(tp)
                bk = VTs[:, VO["bkd"] + g:VO["bkd"] + g + 1]
                if xlo < 1024:
                    act(KT[:, g, xlo:xlo + ncols], bank[:, 0:ncols], AF.Identity, bias=bk)
                else:
                    act(KT[:, g, 1024:1152], bank[:, 0:128], AF.Identity, bias=bk)
                    act(KT[:, g, 1280:1296], bank[:, 128:144], AF.Identity, bias=bk)
                    act(KT[:, g, 1424:1440], bank[:, 144:160], AF.Identity, bias=bk)
                kfree[kk % 2] = ACT.tick()
                kk += 1
        DVE.wait(t_b)
        kv_t = [None, None]
        so = 0

        def tm_out(slot_, which, items):
            nonlocal so
            for (hcol, rows, dst) in items:
                bank = PS[2 + so % 2]
                PE.wait(tm_out.bfree[so % 2])
                for k in range(KC):
                    mm(bank[0:rows, 0:256], HB[:, k, hcol:hcol + rows], slot_[:, k, 0:256], k == 0, k == KC - 1)
                tp = PE.tick()
                DVE.wait(tp, kv_t[so % 2])
                DVE.emit(nc.vector.tensor_tensor(out=KVST[0:rows, so % 2, :], in0=bank[0:rows, 0:256],
                                                 in1=BKVB[0:rows, 256 * which:256 * which + 256], op=ALU.add))
                td = DVE.tick()
                tm_out.bfree[so % 2] = td
                SP.wait(td)
                kv_t[so % 2] = dma(SP, dst, KVST[0:rows, so % 2, :], kvst_sem[so % 2])
                so += 1
        tm_out.bfree = [None, None]
        tm_out(slot, 0, [(32 + 1024, 64, kp_o[0:64, :]), (32 + 1088, 64, kp_o[64:128, :]),
                         (32 + 1152, 16, ks_o[0, 112:128, :]), (32 + 1168, 16, ks_o[1, 112:128, :])])
        ring_release(bi)
        slot, bi = ring_use("kv_v")
        vfree = [None, None]
        vchunks = [(32 + 64 * i, 128 if i < 17 else 64, i) for i in range(18)] + [(32 + 1152, 16, 19), (32 + 1168, 16, 21)]
        for vi, (hcol, rows, vs) in enumerate(vchunks):
            bank = PS[4 + vi % 2]
            PE.wait(vfree[vi % 2])
            for k in range(KC):
                mm(bank[0:rows, 0:256], HB[:, k, hcol:hcol + rows], slot[:, k, 0:256], k == 0, k == KC - 1)
            tp = PE.tick()
            DVE.wait(tp)
            DVE.emit(nc.vector.tensor_tensor(out=VTM[0:rows, vs, :], in0=bank[0:rows, 0:256], in1=BKVB[0:rows, 256:512],
                                             op=ALU.add))
            if vs < 2:
                mr = 128 if vs == 0 else 64
                tq = DVE.tick()
                DVE.wait(tq)
                DVE.emit(nc.vector.tensor_scalar_mul(out=VTM[0:mr, vs, :], in0=VTM[0:mr, vs, :], scalar1=FLAG[0:mr, 0:1]))
            vfree[vi % 2] = DVE.tick()
        tm_out(slot, 1, [(32 + 1024, 64, vp_o[0:64, :]), (32 + 1088, 64, vp_o[64:128, :]),
                         (32 + 1152, 16, vs_o[0, 112:128, :]), (32 + 1168, 16, vs_o[1, 112:128, :])])
        ring_release(bi)
        barrier()
        checkpoint(8)
        tilesQ = [(128, 352), (480, 352), (832, 352)]
        qfree = [None, None]
        kk = 0
        for b in range(8):
            slot, bi = ring_use(f"q_{b}")
            for h in range(2):
                n = 2 * b + h
                for (xlo, ncols) in tilesQ:
                    bank = PS[kk % 2]
                    PE.wait(qfree[kk % 2])
                    for k in range(KC):
                        mm(bank[:, 0:ncols], slot[:, k, 128 * h:128 * h + 128], HB[:, k, 32 + xlo:32 + xlo + ncols],
                           k == 0, k == KC - 1)
                    tp = PE.tick()
                    ACT.wait(tp)
                    act(QT[:, n, xlo - 128:xlo - 128 + ncols], bank[:, 0:ncols], AF.Identity,
                        bias=VTs[:, VO["bq"] + n:VO["bq"] + n + 1])
                    qfree[kk % 2] = ACT.tick()
                    kk += 1
            ring_release(bi)
        barrier()

        checkpoint(9)
        SP.wait(PE.tick(), ACT.tick(), DVE.tick())
        t1 = dma(SP, OHS[0:32, :], ohrel, ld_sem)
        t2 = dma(SP, TBT[0:32, :], tableT, ld_sem)
        t3 = dma(SP, ES64[:, :], sinks_d.partition_broadcast(128), ld_sem)
        ACT.wait(t3)
        act(ES64[:, :], ES64[:, :], AF.Exp)
        ta = ACT.tick()
        DVE.wait(ta)
        PE.wait(t3, t_c)

        qblocks = []
        for qc in range(16):
            m01 = FLAGB if qc == 0 else (FLAGH if qc == 1 else ONESB)
            qblocks.append((64 * qc, 64, [(64 * qc, qc, 128, m01), (64 * qc + 128, qc + 2, 64, ONESB)]))
        for b in range(2):
            qblocks.append((1024 + 16 * b, 16, [(1152 + 144 * b, 18 + 2 * b, 128, ONESB),
                                               (1280 + 144 * b, 19 + 2 * b, 16, ONESB)]))
        barrier()
        for kt, (j0, nj) in enumerate([(0, 128), (128, 64)]):
            for q in range(64):
                mm(PS[q // 16][0:nj, 32 * (q % 16):32 * (q % 16) + 32], OHS[0:32, j0 + 63 - q:j0 + 63 - q + nj],
                   TBT[0:32, 0:32], True, True)
            tp = PE.tick()
            DVE.wait(tp)
            for g in range(4):
                for bq in range(4):
                    DVE.emit(nc.vector.tensor_copy(
                        out=BIAS[0:nj, g, kt, :].rearrange("j (par pair q) -> j par pair q", par=2, pair=4)[:, :, :, 16 * bq:16 * bq + 16],
                        in_=PS[bq][0:nj, :].rearrange("j (q h) -> j q h", h=32)[:, :, 8 * g:8 * g + 8]
                            .rearrange("j q (pair par) -> j par pair q", par=2)))
            td = DVE.tick()
            PE.wait(td)
        barrier()
        sfree = [None, None, None]
        ofree = [None, None]
        ssfree = [None, None, None]
        ptfree = {}
        exp_t = {}
        gi = 0
        for g in range(4):
            DVE.wait(PE.tick())
            for par in range(2):
                for b3 in range(3):
                    DVE.emit(nc.vector.tensor_copy(
                        out=PT[64:65, 3 * par + b3, 256:512].rearrange("o (a q) -> o a q", q=64),
                        in_=ES64[64:65, 8 * g + par:8 * g + 8:2].unsqueeze(2).to_broadcast([1, 4, 64])))
            tsk = DVE.tick()
            PE.wait(tsk)
            checkpoint(50)
            its = [(qcol, nq, keys, par) for (qcol, nq, keys) in qblocks for par in range(2)]

            st = {}

            def S_mm(i):
                qcol, nq, keys, par = its[i]
                b3 = (gi + i) % 3
                pp = slice(64 * par, 64 * par + 64)
                bS = PS[b3]
                PE.wait(sfree[b3])
                for kt, (kcol, vs, nk, msk) in enumerate(keys):
                    mm(bS[0:nk, 256 * kt:256 * kt + 4 * nq], KT[pp, g, kcol:kcol + nk], QT[pp, 4 * g:4 * g + 4, qcol:qcol + nq],
                       True, True, tile_position=(64 * par, 0))
                st[("s", i)] = PE.tick()

            def S_bias(i):
                qcol, nq, keys, par = its[i]
                b3 = (gi + i) % 3
                bS = PS[b3]
                DVE.wait(st[("s", i)], ssfree[b3])
                for kt, (kcol, vs, nk, msk) in enumerate(keys):
                    DVE.emit(nc.vector.scalar_tensor_tensor(
                        out=SS[0:nk, b3, 256 * kt:256 * kt + 4 * nq].rearrange("j (a q) -> j a q", q=nq),
                        in0=bS[0:nk, 256 * kt:256 * kt + 4 * nq].rearrange("j (a q) -> j a q", q=nq), scalar=0.125,
                        in1=BIAS[0:nk, g, kt, 256 * par:256 * par + 256].rearrange("j (a q) -> j a q", q=64)[:, :, 0:nq],
                        op0=ALU.mult, op1=ALU.add))
                td = DVE.tick()
                sfree[b3] = td
                st[("b", i)] = td

            def S_exp(i):
                qcol, nq, keys, par = its[i]
                b3 = (gi + i) % 3
                pb = 3 * par + b3
                ACT.wait(st[("b", i)], ptfree.get(pb))
                for kt, (kcol, vs, nk, msk) in enumerate(keys):
                    act(PT[0:nk, pb, 256 * kt:256 * kt + 4 * nq], SS[0:nk, b3, 256 * kt:256 * kt + 4 * nq], AF.Exp)
                ta = ACT.tick()
                ssfree[b3] = ta
                exp_t[i] = ta

            def PV_mm(i):
                qcol, nq, keys, par = its[i]
                b3 = (gi + i) % 3
                pb = 3 * par + b3
                b2 = (gi + i) % 2
                pp = slice(64 * par, 64 * par + 64)
                bO = PS[3 + b2]
                PE.wait(exp_t[i], ofree[b2])
                for kt, (kcol, vs, nk, msk) in enumerate(keys):
                    mm(bO[pp, 0:4 * nq], VTM[0:nk, vs, 64 * g:64 * g + 64], PT[0:nk, pb, 256 * kt:256 * kt + 4 * nq],
                       kt == 0, kt == 1, tile_position=(0, 64 * par))
                (kcol, vs, nk, msk) = keys[0]
                mm(bO[pp, 256:256 + 4 * nq], msk[0:nk, 0:64], PT[0:nk, pb, 0:4 * nq], True, False, tile_position=(0, 64 * par))
                (kcol, vs, nk, msk) = keys[1]
                if nq == 64:
                    mm(bO[pp, 256:512], ONESB[0:65, 0:64], PT[0:65, pb, 256:512], False, True, tile_position=(0, 64 * par))
                else:
                    mm(bO[pp, 256:256 + 4 * nq], ONESB[0:nk, 0:64], PT[0:nk, pb, 256:256 + 4 * nq], False, True,
                       tile_position=(0, 64 * par))
                tp = PE.tick()
                ptfree[pb] = tp
                st[("pv", i)] = tp

            def PV_ln(i):
                qcol, nq, keys, par = its[i]
                b2 = (gi + i) % 2
                pp = slice(64 * par, 64 * par + 64)
                bO = PS[3 + b2]
                tp = st[("pv", i)]
                if nq == 64:
                    ACT.wait(tp, ofree[b2])
                    act(RDEN[pp, b2, 0:4 * nq], bO[pp, 256:256 + 4 * nq], AF.Ln)
                else:
                    DVE.wait(tp, ofree[b2])
                    DVE.emit(nc.vector.tensor_tensor(
                        out=RDEN[pp, b2, 0:4 * nq].rearrange("d (a q) -> d a q", q=nq),
                        in0=bO[pp, 256:256 + 4 * nq].rearrange("d (a q) -> d a q", q=nq),
                        in1=ES64[pp, 8 * g + par:8 * g + 8:2].unsqueeze(2).to_broadcast([64, 4, nq]), op=ALU.add))
                    tdd = DVE.tick()
                    ACT.wait(tdd)
                    act(RDEN[pp, b2, 0:4 * nq], RDEN[pp, b2, 0:4 * nq], AF.Ln)
                ta = ACT.tick()
                ACT.wait(ta)
                act(RDEN[pp, b2, 0:4 * nq], RDEN[pp, b2, 0:4 * nq], AF.Exp, scale=-1.0)
                st[("ln", i)] = ACT.tick()

            def PV_mult(i):
                qcol, nq, keys, par = its[i]
                b2 = (gi + i) % 2
                pp = slice(64 * par, 64 * par + 64)
                bO = PS[3 + b2]
                DVE.wait(st[("ln", i)], st[("pv", i)])
                DVE.emit(nc.vector.tensor_tensor(
                    out=QT[pp, 4 * g:4 * g + 4, qcol:qcol + nq],
                    in0=bO[pp, 0:4 * nq].rearrange("d (a q) -> d a q", q=nq),
                    in1=RDEN[pp, b2, 0:4 * nq].rearrange("d (a q) -> d a q", q=nq), op=ALU.mult))
                ofree[b2] = DVE.tick()

            n_it = len(its)
            for i in range(2):
                S_mm(i)
                S_bias(i)
                S_exp(i)
            for i in range(n_it):
                PV_mm(i)
                nxt = i + 2 < n_it
                if nxt:
                    S_mm(i + 2)
                    S_bias(i + 2)
                PV_ln(i)
                PV_mult(i)
                if nxt:
                    S_exp(i + 2)
            gi += n_it
        barrier()

        checkpoint(10)
        proj_resid("wo", lambda k, xlo, n_: QT[:, k, xlo - 128:xlo - 128 + n_], tilesQ, 32, "bo", 1)
        barrier()

        checkpoint(11)
        rms_modulate(1, 1, False)
        barrier()
        DVE.wait(kv_t[0], kv_t[1])
        ffn(1, 128, tilesQ)
        barrier()

        checkpoint(12)
        tilesF = [(128, 512), (640, 512), (1152, 32)]
        banks = [PS[5], PS[6], PS[7]]
        sqf = [None, None]
        for c in range(KC):
            ACT.wait(sqf[c % 2])
            act(SQ[:, c % 2, 32:NH], X[:, c, :], AF.Square)
            ta = ACT.tick()
            PE.wait(ta)
            for ti, (a, n_) in enumerate(tilesF):
                mm(banks[ti][:, 0:n_], ONESB[:], SQ[:, c % 2, 32 + a:32 + a + n_], c == 0, c == KC - 1)
            sqf[c % 2] = PE.tick()
        tp = PE.tick()
        ACT.wait(tp)
        for ti, (a, n_) in enumerate(tilesF):
            act(RSTD[:, 32 + a:32 + a + n_], banks[ti][:, 0:n_], AF.Sqrt, bias=EPSC[:, 0:1], scale=1.0 / D)
        ta = ACT.tick()
        DVE.wait(ta)
        DVE.emit(nc.vector.reciprocal(out=RSTD[:, 160:NH], in_=RSTD[:, 160:NH]))
        td = DVE.tick()
        barrier()
        YS = carve(SCR, 11776, (128, 2, NQ), F32)
        y_t = [None, None]
        for c in range(KC):
            DVE.wait(y_t[c % 2])
            DVE.emit(nc.vector.scalar_tensor_tensor(out=YS[:, c % 2, :], in0=X[:, c, 128:NX],
                                                    scalar=VTs[:, VO["nout"] + c:VO["nout"] + c + 1],
                                                    in1=RSTD[:, 160:NH], op0=ALU.mult, op1=ALU.mult))
            td = DVE.tick()
            SP.wait(td)
            y_t[c % 2] = dma(SP, yT[128 * c:128 * c + 128, :], YS[:, c % 2, :], yst_sem[c % 2])
        SP.wait(y_t[0], y_t[1], uf_t[0], uf_t[1], kv_t[0], kv_t[1], (st_sem, 16 * cnt[id(st_sem)]))
    return nc


_NC_CACHE = {}


def _host_inputs(inp):
    f = lambda a: np.ascontiguousarray(np.asarray(a, dtype=np.float32))
    xp, xs = f(inp["x_prompt"]), f(inp["x_sample"])
    cp, cs = f(inp["c_prompt"]), f(inp["c_sample"])
    stc, ckc, cvc = f(inp["state_conv"])[0], f(inp["cache_win_k"])[0], f(inp["cache_win_v"])[0]
    w_pw1 = f(inp["w_pw1"])[0]
    wpw1 = np.ascontiguousarray(np.stack([w_pw1[:, :D].reshape(D, 16, 128), w_pw1[:, D:].reshape(D, 16, 128)], axis=2).reshape(D, 2 * D))
    w_gu = f(inp["w_gu"])
    wgu = np.ascontiguousarray(np.stack([w_gu[:, :, :DFF].reshape(2, D, NFF, 128), w_gu[:, :, DFF:].reshape(2, D, NFF, 128)], axis=3).reshape(2, D, 2 * DFF))
    b_qkv = f(inp["b_qkv"])[0]
    b_pw1 = f(inp["b_pw1"])[0]
    bkd = np.stack([np.concatenate([b_qkv[2048 + 64 * g:2048 + 64 * g + 64]] * 2) for g in range(4)])
    rows = [f(inp["b_mod"]).reshape(192, 128), f(inp["norm_mix"]).reshape(32, 128), f(inp["norm_ffn"]).reshape(32, 128),
            b_pw1[:D].reshape(16, 128), b_pw1[D:].reshape(16, 128), f(inp["b_dw"]).reshape(16, 128),
            f(inp["conv_ln_g"]).reshape(16, 128), f(inp["conv_ln_b"]).reshape(16, 128), f(inp["b_pw2"]).reshape(16, 128),
            b_qkv[:2048].reshape(16, 128), bkd, f(inp["b_o"]).reshape(16, 128), f(inp["norm_out"]).reshape(16, 128)]
    vecs = np.concatenate(rows, axis=0)
    assert vecs.shape[0] == NVEC
    vecsT = np.ascontiguousarray(vecs.T)
    wdwT = np.ascontiguousarray(f(inp["w_dw"])[0].reshape(31, 16, 128).transpose(2, 0, 1).reshape(128, 31 * 16))
    rel = np.arange(255) - 191
    bk = t5_bucket_np(rel)
    ohrel = np.zeros((32, 255), np.float32)
    ohrel[bk, np.arange(255)] = 1.0
    def tile_w(W, nl):
        N = W.shape[1]
        return np.ascontiguousarray(W.reshape(nl, KC, 128, N // 256, 256).transpose(0, 3, 2, 1, 4).reshape(nl * (N // 256), 128, KC, 256))

    w_dn = f(inp["w_down"])
    wdn_t = np.zeros((2, 6, 8, 128, GRP, 256), np.float32)
    for li in range(2):
        for G in range(6):
            j0, j1 = G * GRP, min(NFF, (G + 1) * GRP)
            blk = w_dn[li, j0 * 128:j1 * 128, :].reshape(j1 - j0, 128, 8, 256)
            wdn_t[li, G, :, :, 0:j1 - j0, :] = blk.transpose(2, 1, 0, 3)
    wdn_t = wdn_t.reshape(96, 128, GRP, 256)
    shared = {
        "vecsT": vecsT, "wdwT": wdwT, "bkv": np.ascontiguousarray(b_qkv[2048:].reshape(1, 512)), "ohrel": ohrel,
        "tableT": np.ascontiguousarray(f(inp["rel_bias_table"]).T), "sinks": f(inp["attn_sinks"]).reshape(1, 32),
        "identf": np.eye(128, dtype=np.float32),
        "wmod": tile_w(f(inp["w_mod"]).reshape(2 * D, 6 * D), 2),
        "wpw1": tile_w(wpw1, 1), "wpw2": tile_w(f(inp["w_pw2"])[0], 1), "wqkv": tile_w(f(inp["w_qkv"])[0], 1),
        "wo": tile_w(f(inp["w_o"])[0], 1), "wgu": tile_w(wgu.reshape(2 * D, 2 * DFF), 2), "wdn": wdn_t,
    }
    maps = []
    for i in range(NCORES):
        b, seg = i // 4, i % 4
        t0 = seg * 1024
        xin = np.zeros((NH, D), np.float32)
        lo = t0 - 160
        if lo >= 0:
            xin[0:160] = xp[b, lo:t0]
        xin[160:1184] = xp[b, t0:t0 + 1024]
        xin[1184:1200] = xs[2 * i]
        xin[1200:1216] = xs[2 * i + 1]
        cvec = np.stack([cp[b], cs[2 * i], cs[2 * i + 1]])
        m = dict(shared)
        m["xT"] = np.ascontiguousarray(xin.T)
        m["cT"] = np.ascontiguousarray(cvec.reshape(3, 16, 128).transpose(2, 1, 0))
        m["flag"] = np.full((128, 1), 1.0 if seg > 0 else 0.0, np.float32)
        m["stT"] = np.ascontiguousarray(stc[2 * i:2 * i + 2].transpose(0, 2, 1))
        kk = ckc[2 * i:2 * i + 2]
        kT = kk.transpose(0, 2, 3, 1)
        m["ckT"] = np.ascontiguousarray(np.concatenate([kT, kT], axis=2))
        m["ck"] = np.ascontiguousarray(kk.reshape(2, 128, 256))
        m["cv"] = np.ascontiguousarray(cvc[2 * i:2 * i + 2].reshape(2, 128, 256))
        maps.append(m)
    return maps


def kernel(**inputs):
    if "nc" not in _NC_CACHE:
        _NC_CACHE["nc"] = build_nc()
    nc = _NC_CACHE["nc"]
    maps = _host_inputs(inputs)
    res = run_bass_kernel_spmd(nc, maps, core_ids=list(range(NCORES)))
    R = res.results
    stc = np.asarray(inputs["state_conv"], dtype=np.float32)
    y_prompt = np.zeros((2, SEQ, D), np.float32)
    y_sample = np.zeros((16, 16, D), np.float32)
    conv_prompt = np.zeros((1, 2, 30, D), np.float32)
    wk_p = np.zeros((1, 2, 128, 4, 64), np.float32)
    wv_p = np.zeros((1, 2, 128, 4, 64), np.float32)
    conv_sample = np.zeros((1, 16, 30, D), np.float32)
    wk_s = np.zeros((1, 16, 128, 4, 64), np.float32)
    wv_s = np.zeros((1, 16, 128, 4, 64), np.float32)
    for i in range(NCORES):
        b, seg = i // 4, i % 4
        r = R[i]
        yT = np.asarray(r["yT"])
        y_prompt[b, seg * 1024:(seg + 1) * 1024] = yT[:, 0:1024].T
        y_sample[2 * i] = yT[:, 1024:1040].T
        y_sample[2 * i + 1] = yT[:, 1040:1056].T
        uc = np.asarray(r["ucT"])
        hd = np.asarray(r["cs_head"])
        for j in range(2):
            conv_sample[0, 2 * i + j, 0:14] = hd[j].T
            conv_sample[0, 2 * i + j, 14:30] = uc[:, 30 + 16 * j:46 + 16 * j].T
            wk_s[0, 2 * i + j] = np.asarray(r["ks_o"])[j].reshape(128, 4, 64)
            wv_s[0, 2 * i + j] = np.asarray(r["vs_o"])[j].reshape(128, 4, 64)
        if seg == 3:
            conv_prompt[0, b] = uc[:, 0:30].T
            wk_p[0, b] = np.asarray(r["kp_o"]).reshape(128, 4, 64)
            wv_p[0, b] = np.asarray(r["vp_o"]).reshape(128, 4, 64)
    return (y_prompt, y_sample, conv_prompt, wk_p, wv_p, conv_sample, wk_s, wv_s)
```

```python
import numpy as np
import concourse.bass as bass
import concourse.mybir as mybir
from concourse.bass_utils import run_bass_kernel_spmd
from contextlib import ExitStack, suppress

F32 = mybir.dt.float32
BF16 = mybir.dt.bfloat16
U8 = mybir.dt.uint8
AF = mybir.ActivationFunctionType
ALU = mybir.AluOpType

D = 2048
KC = 16
DFF = 5632
NFF = 44
SEQ = 4096
NCORES = 8
EPS = 1e-6
NX = 1184
NH = 1216
NU = 1276
NQ = 1056
NKT = 1440
GRP = 8

VO = {}
_o = 0
for _n, _c in [("bmod", 192), ("nmix", 32), ("nffn", 32), ("bpa", 16), ("bpg", 16), ("bdw", 16), ("lng", 16),
               ("lnb", 16), ("bpw2", 16), ("bq", 16), ("bkd", 4), ("bo", 16), ("nout", 16)]:
    VO[_n] = _o
    _o += _c
NVEC = _o


def t5_bucket_np(rel):
    nb = 16
    max_exact = 8
    ret = np.where(rel > 0, nb, 0)
    n = np.abs(rel)
    nf = np.maximum(n, 1).astype(np.float32)
    large = max_exact + (np.log(nf / max_exact) / np.float32(np.log(128 / max_exact)) * (nb - max_exact)).astype(np.int32)
    large = np.minimum(large, nb - 1)
    return ret + np.where(n < max_exact, n, large)


STOP = None


class _StopBuild(Exception):
    pass


class Eng:
    def __init__(self, nc, eng, name, es):
        self.e = eng
        self.sem = es.enter_context(nc.semaphore(name))
        self.n = 0
        self.last = None
        self.marked = True
        self.seen = {}

    def emit(self, inst):
        self.last = inst
        self.marked = False
        return inst

    def tick(self):
        if not self.marked:
            self.n += 1
            self.last.then_inc(self.sem, 1)
            self.marked = True
        return (self.sem, self.n)

    def wait(self, *ts):
        for t in ts:
            if t is None:
                continue
            sem, v = t
            if v <= 0:
                continue
            k = id(sem)
            if self.seen.get(k, 0) >= v:
                continue
            self.e.wait_ge(sem, v)
            self.seen[k] = v


def build_nc():
    nc = bass.Bass("TRN2", target_bir_lowering=False)
    dt_in = lambda n, s: nc.dram_tensor(n, list(s), F32, kind="ExternalInput").ap()
    dt_out = lambda n, s: nc.dram_tensor(n, list(s), F32, kind="ExternalOutput").ap()
    xT = dt_in("xT", (D, NH))
    cT = dt_in("cT", (128, KC, 3))
    flag_d = dt_in("flag", (128, 1))
    stT = dt_in("stT", (2, D, 30))
    ckT = dt_in("ckT", (2, 4, 128, 128))
    ck = dt_in("ck", (2, 128, 256))
    cv = dt_in("cv", (2, 128, 256))
    vecsT = dt_in("vecsT", (128, NVEC))
    wdwT = dt_in("wdwT", (128, 31 * 16))
    bkv_d = dt_in("bkv", (1, 512))
    ohrel = dt_in("ohrel", (32, 255))
    tableT = dt_in("tableT", (32, 32))
    sinks_d = dt_in("sinks", (1, 32))
    identf = dt_in("identf", (128, 128))
    wmod = dt_in("wmod", (96, 128, KC, 256))
    wpw1 = dt_in("wpw1", (16, 128, KC, 256))
    wpw2 = dt_in("wpw2", (8, 128, KC, 256))
    wqkv = dt_in("wqkv", (10, 128, KC, 256))
    wo = dt_in("wo", (8, 128, KC, 256))
    wgu = dt_in("wgu", (88, 128, KC, 256))
    wdn = dt_in("wdn", (96, 128, GRP, 256))

    yT = dt_out("yT", (D, NQ))
    ucT = dt_out("ucT", (D, 62))
    cs_head = dt_out("cs_head", (2, D, 14))
    kp_o = dt_out("kp_o", (128, 256))
    vp_o = dt_out("vp_o", (128, 256))
    ks_o = dt_out("ks_o", (2, 128, 256))
    vs_o = dt_out("vs_o", (2, 128, 256))

    es = ExitStack()
    with es, suppress(_StopBuild):
        sb = lambda n, s, d: es.enter_context(nc.sbuf_tensor(n, list(s), d))
        X = sb("X", (128, KC, NX), F32)
        HBraw = sb("HBraw", (128, KC * NH * 2), U8)
        UBraw = sb("UBraw", (128, KC * NU * 2), U8)
        RING = sb("RING", (128, 3, KC, 256), BF16)
        SCR = sb("SCR", (128, 23808), U8)
        VTs = sb("VTs", (128, NVEC), F32)
        MOD = sb("MOD", (128, 2, 96, 3), F32)
        AM = sb("AM", (128, KC, 3), F32)
        G1B = sb("G1B", (128, KC, 3), F32)
        IDB = sb("IDB", (128, 128), BF16)
        IDF = sb("IDF", (128, 128), F32)
        ONESB = sb("ONESB", (128, 128), BF16)
        FLAG = sb("FLAG", (128, 1), F32)
        FLAGB = sb("FLAGB", (128, 64), BF16)
        CTs = sb("CTs", (128, KC, 3), F32)
        SCT = sb("SCT", (128, KC, 3), BF16)
        UFs = sb("UFs", (128, 2, 62), F32)
        ESK = sb("ESK", (1, 2, 32), BF16)
        ESF = sb("ESF", (1, 3, 32), F32)
        ES64 = sb("ES64", (128, 32), F32)
        FLAGH = sb("FLAGH", (128, 64), BF16)
        ONE1 = sb("ONE1", (1, 64), BF16)
        EPSC = sb("EPSC", (128, 1), F32)
        PS = [es.enter_context(nc.psum_tensor(f"ps{i}", [128, 512], F32)) for i in range(8)]

        def carve(raw, off, shape, dtype):
            n = int(np.prod(shape[1:]))
            bs = 2 if dtype == BF16 else 4
            v = raw[:, off:off + n * bs].bitcast(dtype)
            if len(shape) == 3:
                v = v.rearrange("p (a b) -> p a b", b=shape[2])
            elif len(shape) == 4:
                v = v.rearrange("p (a b c) -> p a b c", b=shape[2], c=shape[3])
            return v

        HB = carve(HBraw, 0, (128, KC, NH), BF16)
        UB = carve(UBraw, 0, (128, KC, NU), BF16)
        QT = carve(UBraw, 0, (128, KC, NQ), BF16)
        ACTR = carve(UBraw, 0, (128, 16, NX), BF16)
        KVST = carve(UBraw, 33792, (128, 2, 256), F32)
        BKVB = carve(UBraw, 33792 + 2048, (128, 512), F32)
        XP = carve(SCR, 0, (128, KC, 32), F32)
        RSTD = carve(SCR, 2048, (128, NH), F32)
        T1 = carve(SCR, 6912, (128, NH), F32)
        T1B = carve(SCR, 16640, (128, NH), F32)
        SQ = carve(SCR, 11776, (128, 2, NH), BF16)
        SG = carve(SCR, 16640, (128, 2, 512), F32)
        TMPE = carve(SCR, 20736, (128, 512), F32)
        KT = carve(SCR, 0, (128, 4, NKT), BF16)
        VTM = carve(SCR, 11520, (128, 22, 256), BF16)
        YST = carve(SCR, 0, (128, 2, NQ), F32)
        Z = carve(HBraw, 0, (128, KC, 416), F32)
        ZB = carve(HBraw, 26624, (128, 2, 416), BF16)
        ZSQ = carve(HBraw, 28288, (128, 2, 416), BF16)
        MEAN = carve(SCR, 0, (128, 416), F32)
        MSQ = carve(SCR, 1664, (128, 416), F32)
        RSC = carve(SCR, 3328, (128, 416), F32)
        WDW = carve(SCR, 4992, (128, 31, 16), F32)
        DG = carve(SCR, 7168, (128, 2, 31, 128), BF16)
        BIAS = carve(HBraw, 0, (128, 4, 2, 512), F32)
        PT = carve(HBraw, 16384, (128, 6, 512), BF16)
        SS = carve(HBraw, 22528, (128, 3, 512), F32)
        RDEN = carve(HBraw, 28672, (128, 2, 256), F32)
        OHS = carve(HBraw, 30720, (128, 255), F32)
        TBT = carve(HBraw, 31744, (128, 32), F32)

        PE = Eng(nc, nc.tensor, "s_pe", es)
        ACT = Eng(nc, nc.scalar, "s_act", es)
        DVE = Eng(nc, nc.vector, "s_dve", es)
        SP = Eng(nc, nc.sync, "s_sp", es)
        POOL = Eng(nc, nc.gpsimd, "s_pool", es)
        ld_sem = es.enter_context(nc.semaphore("ld"))
        ld2_sem = es.enter_context(nc.semaphore("ld2"))
        st_sem = es.enter_context(nc.semaphore("st"))
        kvst_sem = [es.enter_context(nc.semaphore(f"kvst{i}")) for i in range(2)]
        uf_sem = [es.enter_context(nc.semaphore(f"uf{i}")) for i in range(2)]
        yst_sem = [es.enter_context(nc.semaphore(f"yst{i}")) for i in range(2)]
        slot_sem = [es.enter_context(nc.semaphore(f"slot{i}")) for i in range(3)]
        cnt = {}

        def dma(engw, out, in_, sem):
            engw.e.dma_start(out=out, in_=in_).then_inc(sem, 16)
            cnt[id(sem)] = cnt.get(id(sem), 0) + 1
            return (sem, 16 * cnt[id(sem)])

        def mm(out, lhsT, rhs, start, stop, **kw):
            return PE.emit(nc.tensor.matmul(out, lhsT=lhsT, rhs=rhs, start=start, stop=stop, **kw))

        def act(out, in_, func, **kw):
            return ACT.emit(nc.scalar.activation(out=out, in_=in_, func=func, **kw))

        def barrier():
            ts = [PE.tick(), ACT.tick(), DVE.tick()]
            for e in (PE, ACT, DVE):
                e.wait(*ts)
            return ts

        def checkpoint(stage):
            if STOP == stage:
                barrier()
                SP.wait(PE.tick(), ACT.tick(), DVE.tick())
                POOL.wait(PE.tick())
                for sem in [st_sem, ld_sem, ld2_sem] + uf_sem + kvst_sem + yst_sem + slot_sem:
                    if cnt.get(id(sem)):
                        SP.wait((sem, 16 * cnt[id(sem)]))
                raise _StopBuild()

        plan = []

        def wblock(name, Wt, blk, nk):
            plan.append((name, nk, 256, Wt[blk, :, 0:nk, :]))

        for b in range(16):
            wblock(f"mod0_{b}", wmod, b, KC)
        for c in range(16):
            wblock(f"pw1_{c}", wpw1, c, KC)
            wblock(f"mod0_{16 + c}", wmod, 16 + c, KC)
        for b in range(32, 48):
            wblock(f"mod0_{b}", wmod, b, KC)
        for b in range(8):
            wblock(f"pw2_{b}", wpw2, b, KC)
        for li in range(2):
            if li == 1:
                wblock("kv_k", wqkv, 8, KC)
                wblock("kv_v", wqkv, 9, KC)
                for b in range(8):
                    wblock(f"q_{b}", wqkv, b, KC)
                for b in range(8):
                    wblock(f"wo_{b}", wo, b, KC)
            ngrp = (NFF + GRP - 1) // GRP
            for G in range(ngrp):
                j0, j1 = G * GRP, min(NFF, (G + 1) * GRP)
                for j in range(j0, j1):
                    wblock(f"gu{li}_{j}", wgu, 44 * li + j, KC)
                    if li == 0:
                        wblock(f"mod1_{j}", wmod, 48 + j, KC)
                        if j == NFF - 1:
                            for b in range(NFF, 48):
                                wblock(f"mod1_{b}", wmod, 48 + b, KC)
                for b in range(8):
                    wblock(f"dn{li}_{G}_{b}", wdn, (li * 6 + G) * 8 + b, j1 - j0)

        ring = {"next_pf": 0, "next_use": 0, "free": [None, None, None], "load": {}, "hold": None}

        def ring_prefetch(i):
            name, nk, ncols, src = plan[i]
            s = i % 3
            POOL.wait(ring["free"][s])
            if ring["hold"] is not None and name == "kv_k":
                POOL.wait(*ring["hold"])
            ring["load"][i] = dma(POOL, RING[:, s, 0:nk, 0:ncols], src, slot_sem[s])

        def ring_use(name):
            i = ring["next_use"]
            assert plan[i][0] == name, (plan[i][0], name)
            while ring["next_pf"] <= min(i + 2, len(plan) - 1):
                ring_prefetch(ring["next_pf"])
                ring["next_pf"] += 1
            PE.wait(ring["load"][i])
            ring["next_use"] += 1
            return RING[:, i % 3], i

        def ring_release(i):
            ring["free"][i % 3] = PE.tick()

        xv = xT.rearrange("(c p) t -> p c t", p=128)
        for q in range(4):
            dma(SP, X[:, 4 * q:4 * q + 4, :], xv[:, 4 * q:4 * q + 4, 32:NH], ld_sem)
        dma(SP, XP, xv[:, :, 0:32], ld_sem)
        dma(SP, CTs[:], cT, ld_sem)
        dma(SP, FLAG[:], flag_d, ld_sem)
        dma(SP, VTs[:], vecsT, ld_sem)
        dma(SP, IDF[:], identf, ld_sem)
        t_ld = (ld_sem, 16 * cnt[id(ld_sem)])
        stv = stT.rearrange("b (c p) w -> b p c w", p=128)
        t_st = None
        for b in range(2):
            t_st = dma(POOL, UB[:, :, 1184 + 46 * b:1184 + 46 * b + 30], stv[b], ld2_sem)
        for b in range(2):
            dma(SP, cs_head[b], stT[b, :, 16:30], st_sem)
        for b in range(2):
            dma(SP, ks_o[b, 0:112, :], ck[b, 16:128, :], st_sem)
            dma(SP, vs_o[b, 0:112, :], cv[b, 16:128, :], st_sem)

        DVE.wait(t_ld)
        ACT.wait(t_ld)
        DVE.emit(nc.vector.memset(ONESB[:], 1.0))
        DVE.emit(nc.vector.memset(ONE1[:], 1.0))
        DVE.emit(nc.vector.memset(EPSC[:], EPS))
        DVE.emit(nc.vector.tensor_copy(out=IDB[:], in_=IDF[:]))
        DVE.emit(nc.vector.tensor_copy(out=FLAGB[:], in_=FLAG[:, 0:1].to_broadcast([128, 64])))
        DVE.emit(nc.vector.memset(FLAGH[64:128, :], 1.0))
        DVE.emit(nc.vector.tensor_copy(out=FLAGH[0:64, :], in_=FLAG[0:64, 0:1].to_broadcast([64, 64])))
        act(SCT[:], CTs[:], AF.Silu)
        barrier()
        checkpoint(0)

        def mod_block(li, b, pbase=4, defer=False):
            slot, bi = ring_use(f"mod{li}_{b}")
            bank = PS[pbase + (mod_block.k % 2)]
            PE.wait(mod_block.free[mod_block.k % 2])
            for h in range(2):
                for k in range(KC):
                    mm(bank[:, 4 * h:4 * h + 3], slot[:, k, 128 * h:128 * h + 128], SCT[:, k, :], k == 0, k == KC - 1)
            ring_release(bi)
            tp = PE.tick()
            kslot = mod_block.k % 2
            mod_block.k += 1

            def evac():
                DVE.wait(tp)
                for h in range(2):
                    n = 2 * b + h
                    DVE.emit(nc.vector.tensor_scalar_add(out=MOD[:, li, n, :], in0=bank[:, 4 * h:4 * h + 3],
                                                         scalar1=VTs[:, VO["bmod"] + 96 * li + n:VO["bmod"] + 96 * li + n + 1]))
                mod_block.free[kslot] = DVE.tick()
            if defer:
                return evac
            evac()
            return None
        mod_block.k = 0
        mod_block.free = [None, None]

        SEGH = [(0, 1184, 0), (1184, 1200, 1), (1200, 1216, 2)]

        def segs(lo, hi, off=0):
            out = []
            for (a, b_, bi) in [(0, 1152, 0), (1152, 1168, 1), (1168, 1184, 2)]:
                a2, b2 = max(lo, a + off if False else a), min(hi, b_)
                if a2 < b2:
                    out.append((a2, b2, bi))
            return out

        def rms_modulate(li, which, with_pre):
            nv = VO["nmix"] if which == 0 else VO["nffn"]
            sh0, sc0 = (0, 16) if which == 0 else (48, 64)
            c0 = 0 if with_pre else 32
            tiles = [(c0, 512), (512, 1024), (1024, NH)]
            banks = [PS[5], PS[6], PS[7]]
            for c in range(KC):
                sqb = SQ[:, c % 2, :]
                ACT.wait(rms_modulate.sq_free[c % 2])
                if with_pre:
                    act(sqb[:, 0:32], XP[:, c, :], AF.Square)
                act(sqb[:, 32:NH], X[:, c, :], AF.Square)
                ta = ACT.tick()
                PE.wait(ta)
                for ti, (a, b_) in enumerate(tiles):
                    mm(banks[ti][:, 0:b_ - a], ONESB[:], sqb[:, a:b_], c == 0, c == KC - 1)
                rms_modulate.sq_free[c % 2] = PE.tick()
            tp = PE.tick()
            ACT.wait(tp)
            for ti, (a, b_) in enumerate(tiles):
                act(RSTD[:, a:b_], banks[ti][:, 0:b_ - a], AF.Sqrt, bias=EPSC[:, 0:1], scale=1.0 / D)
            ta = ACT.tick()
            DVE.wait(ta)
            DVE.emit(nc.vector.reciprocal(out=RSTD[:, c0:NH], in_=RSTD[:, c0:NH]))
            for c in range(KC):
                DVE.emit(nc.vector.tensor_scalar(out=AM[:, c, :], in0=MOD[:, li, sc0 + c, :], scalar1=1.0,
                                                 scalar2=VTs[:, nv + 16 * li + c:nv + 16 * li + c + 1],
                                                 op0=ALU.add, op1=ALU.mult))
            td = DVE.tick()
            DVE.wait(td)
            ACT.wait(td)
            tprev = [None, None]
            for c in range(KC):
                Tc = T1 if c % 2 == 0 else T1B
                DVE.wait(tprev[c % 2])
                if with_pre:
                    DVE.emit(nc.vector.tensor_tensor(out=Tc[:, 0:32], in0=XP[:, c, :], in1=RSTD[:, 0:32], op=ALU.mult))
                DVE.emit(nc.vector.tensor_tensor(out=Tc[:, 32:NH], in0=X[:, c, :], in1=RSTD[:, 32:NH], op=ALU.mult))
                td = DVE.tick()
                ACT.wait(td)
                for (a, b_, bi) in SEGH:
                    a2 = max(a, c0)
                    act(HB[:, c, a2:b_], Tc[:, a2:b_], AF.Identity, scale=AM[:, c, bi:bi + 1],
                        bias=MOD[:, li, sh0 + c, bi:bi + 1])
                tprev[c % 2] = ACT.tick()
        rms_modulate.sq_free = [None, None]

        def resid_epilogue(bank, ncols, n, xlo, gsrc, bvec):
            ACT.wait(resid_epilogue.tfree)
            for (a, b_, bi) in segs(xlo, xlo + ncols):
                if bvec is None:
                    act(TMPE[:, a - xlo:b_ - xlo], bank[:, a - xlo:b_ - xlo], AF.Identity, scale=gsrc[:, n, bi:bi + 1])
                else:
                    act(TMPE[:, a - xlo:b_ - xlo], bank[:, a - xlo:b_ - xlo], AF.Identity,
                        scale=gsrc[:, n, bi:bi + 1], bias=bvec[:, n, bi:bi + 1])
            ta = ACT.tick()
            DVE.wait(ta)
            DVE.emit(nc.vector.tensor_tensor(out=X[:, n, xlo:xlo + ncols], in0=X[:, n, xlo:xlo + ncols],
                                             in1=TMPE[:, 0:ncols], op=ALU.add))
            resid_epilogue.tfree = DVE.tick()
            return resid_epilogue.tfree
        resid_epilogue.tfree = None

        def proj_resid(wname, rhs_of, tiles, gch, bname, li):
            for n in range(KC):
                DVE.emit(nc.vector.tensor_scalar_mul(out=G1B[:, n, :], in0=MOD[:, li, gch + n, :],
                                                     scalar1=VTs[:, VO[bname] + n:VO[bname] + n + 1]))
            tg = DVE.tick()
            ACT.wait(tg)
            bfree = [None, None]
            kk = 0
            for b in range(8):
                slot, bi = ring_use(f"{wname}_{b}")
                for h in range(2):
                    n = 2 * b + h
                    for (xlo, ncols) in tiles:
                        bank = PS[kk % 2]
                        PE.wait(bfree[kk % 2])
                        for k in range(KC):
                            mm(bank[:, 0:ncols], slot[:, k, 128 * h:128 * h + 128], rhs_of(k, xlo, ncols), k == 0, k == KC - 1)
                        tp = PE.tick()
                        ACT.wait(tp)
                        resid_epilogue(bank, ncols, n, xlo, MOD[:, li, gch:gch + 16, :], G1B)
                        bfree[kk % 2] = ACT.tick()
                        kk += 1
                ring_release(bi)

        def ffn(li, xlo_all, tiles):
            gch = 80
            ngrp = (NFF + GRP - 1) // GRP
            gfree = [None, None]
            ufree = [None, None]
            sgfree = [None, None]
            dfree = [None, None]
            kk = 0
            dk = 0
            for G in range(ngrp):
                j0, j1 = G * GRP, min(NFF, (G + 1) * GRP)
                for j in range(j0, j1):
                    slot, bi = ring_use(f"gu{li}_{j}")
                    aslot = (G % 2) * GRP + (j - j0)
                    for (xlo, ncols) in tiles:
                        bg, bu = PS[kk % 2], PS[2 + kk % 2]
                        PE.wait(gfree[kk % 2], ufree[kk % 2])
                        for k in range(KC):
                            mm(bg[:, 0:ncols], slot[:, k, 0:128], HB[:, k, 32 + xlo:32 + xlo + ncols], k == 0, k == KC - 1)
                        tpg = PE.tick()
                        for k in range(KC):
                            mm(bu[:, 0:ncols], slot[:, k, 128:256], HB[:, k, 32 + xlo:32 + xlo + ncols], k == 0, k == KC - 1)
                        tpu = PE.tick()
                        ACT.wait(tpg, sgfree[kk % 2])
                        act(SG[:, kk % 2, 0:ncols], bg[:, 0:ncols], AF.Silu)
                        gfree[kk % 2] = ACT.tick()
                        DVE.wait(gfree[kk % 2], tpu)
                        DVE.emit(nc.vector.tensor_tensor(out=ACTR[:, aslot, xlo:xlo + ncols], in0=bu[:, 0:ncols],
                                                         in1=SG[:, kk % 2, 0:ncols], op=ALU.mult))
                        ufree[kk % 2] = DVE.tick()
                        sgfree[kk % 2] = ufree[kk % 2]
                        kk += 1
                    ring_release(bi)
                    if li == 0:
                        mod_block(1, j)
                        if j == NFF - 1:
                            for b in range(NFF, 48):
                                mod_block(1, b)
                t_act = DVE.tick()
                PE.wait(t_act)
                for b in range(8):
                    slot, bi = ring_use(f"dn{li}_{G}_{b}")
                    for h in range(2):
                        n = 2 * b + h
                        for (xlo, ncols) in tiles:
                            bank = PS[6 + dk % 2]
                            PE.wait(dfree[dk % 2])
                            for jj in range(j1 - j0):
                                mm(bank[:, 0:ncols], slot[:, jj, 128 * h:128 * h + 128],
                                   ACTR[:, (G % 2) * GRP + jj, xlo:xlo + ncols], jj == 0, jj == j1 - j0 - 1)
                            tp = PE.tick()
                            ACT.wait(tp)
                            resid_epilogue(bank, ncols, n, xlo, MOD[:, li, gch:gch + 16, :], None)
                            dfree[dk % 2] = ACT.tick()
                            dk += 1
                    ring_release(bi)

        for b in range(16):
            mod_block(0, b)
        barrier()
        checkpoint(1)
        rms_modulate(0, 0, True)
        barrier()
        checkpoint(2)

        DVE.wait(t_st)
        tiles1 = [(0, 512), (512, 1024), (1024, NH)]
        afree = [None, None]
        gfree = [None, None]
        sgfree = [None, None]
        uf_t = [None, None]
        kk = 0
        for c in range(KC):
            slot, bi = ring_use(f"pw1_{c}")
            for (a, b_) in tiles1:
                ncols = b_ - a
                ba, bgt = PS[kk % 2], PS[2 + kk % 2]
                PE.wait(afree[kk % 2], gfree[kk % 2])
                for k in range(KC):
                    mm(ba[:, 0:ncols], slot[:, k, 0:128], HB[:, k, a:b_], k == 0, k == KC - 1)
                tpa = PE.tick()
                for k in range(KC):
                    mm(bgt[:, 0:ncols], slot[:, k, 128:256], HB[:, k, a:b_], k == 0, k == KC - 1)
                tpg = PE.tick()
                ACT.wait(tpg, sgfree[kk % 2])
                act(SG[:, kk % 2, 0:ncols], bgt[:, 0:ncols], AF.Sigmoid, bias=VTs[:, VO["bpg"] + c:VO["bpg"] + c + 1])
                gfree[kk % 2] = ACT.tick()
                DVE.wait(gfree[kk % 2], tpa)
                ba_s = VTs[:, VO["bpa"] + c:VO["bpa"] + c + 1]

                def glu(out, lo, hi):
                    DVE.emit(nc.vector.scalar_tensor_tensor(out=out, in0=ba[:, lo:hi], scalar=ba_s,
                                                            in1=SG[:, kk % 2, lo:hi], op0=ALU.add, op1=ALU.mult))
                if a < 1024:
                    glu(UB[:, c, a:b_], 0, ncols)
                else:
                    glu(UB[:, c, 1024:1184], 0, 160)
                    glu(UB[:, c, 1214:1230], 160, 176)
                    glu(UB[:, c, 1260:1276], 176, 192)
                    DVE.wait(uf_t[c % 2])
                    glu(UFs[:, c % 2, 0:30], 130, 160)
                    glu(UFs[:, c % 2, 30:62], 160, 192)
                    tu = DVE.tick()
                    SP.wait(tu)
                    uf_t[c % 2] = dma(SP, ucT[128 * c:128 * c + 128, :], UFs[:, c % 2, :], uf_sem[c % 2])
                afree[kk % 2] = DVE.tick()
                sgfree[kk % 2] = afree[kk % 2]
                if a == 0:
                    DVE.wait(afree[kk % 2])
                    DVE.emit(nc.vector.tensor_scalar_mul(out=UB[:, c, 0:160], in0=UB[:, c, 0:160], scalar1=FLAG[:, 0:1]))
                kk += 1
            ring_release(bi)
            mod_block(0, 16 + c)
        barrier()
        checkpoint(3)

        SP.wait(PE.tick(), ACT.tick(), DVE.tick())
        tw = dma(SP, WDW, wdwT.rearrange("p (w c) -> p w c", c=16), ld_sem)
        DVE.wait(tw)
        ACT.wait(tw)
        ctiles = [(0, 384), (384, 384), (768, 416)]
        units = [(zlo, zn, c) for (zlo, zn) in ctiles for c in range(KC)]
        dgfree = [None, None]
        cfree = [None, None, None, None]
        zdone = [None] * KC
        zbfree = [None, None]
        conv_t = {}
        st_read = [None]
        NDV = 26

        def emit_conv(u):
            zlo, zn, c = units[u]
            b = u % 2
            samp = zn > 384
            DVE.wait(dgfree[b])
            ACT.wait(dgfree[b])
            DVE.emit(nc.vector.tensor_tensor(out=DG[:, b, 0:NDV, :], in0=IDB[:].unsqueeze(1).to_broadcast([128, NDV, 128]),
                                             in1=WDW[:, 0:NDV, c:c + 1].to_broadcast([128, NDV, 128]), op=ALU.mult))
            td = DVE.tick()
            for w in range(NDV, 31):
                act(DG[:, b, w, :], IDB[:], AF.Identity, scale=WDW[:, w, c:c + 1])
            ta = ACT.tick()
            bank = PS[u % 4]
            PE.wait(td, ta, cfree[u % 4])
            for w in range(31):
                mm(bank[:, 0:384], DG[:, b, w, :], UB[:, c, zlo + 2 + w:zlo + 2 + w + 384], w == 0, w == 30)
            if samp:
                for w in range(31):
                    mm(bank[:, 384:416], DG[:, b, w, :],
                       UB[:, c, 1184:1276].rearrange("p (b t) -> p b t", t=46)[:, :, w:w + 16], w == 0, w == 30)
            conv_t[u] = PE.tick()
            dgfree[b] = conv_t[u]

        def emit_post(u):
            zlo, zn, c = units[u]
            b = u % 2
            bank = PS[u % 4]
            bdw = VTs[:, VO["bdw"] + c:VO["bdw"] + c + 1]
            ACT.wait(conv_t[u], zdone[c])
            act(Z[:, c, 0:zn], bank[:, 0:zn], AF.Identity, bias=bdw)
            tz = ACT.tick()
            cfree[u % 4] = tz
            DVE.wait(tz, zbfree[b])
            DVE.emit(nc.vector.tensor_copy(out=ZB[:, b, 0:zn], in_=Z[:, c, 0:zn]))
            td = DVE.tick()
            ACT.wait(tz, zbfree[b])
            act(ZSQ[:, b, 0:zn], Z[:, c, 0:zn], AF.Square)
            ta = ACT.tick()
            PE.wait(td, ta)
            if c == 0:
                PE.wait(st_read[0])
            mm(PS[4][:, 0:zn], ONESB[:], ZB[:, b, 0:zn], c == 0, c == KC - 1)
            mm(PS[5][:, 0:zn], ONESB[:], ZSQ[:, b, 0:zn], c == 0, c == KC - 1)
            zbfree[b] = PE.tick()
            if c != KC - 1:
                return
            ts = zbfree[b]
            DVE.wait(ts)
            DVE.emit(nc.vector.tensor_scalar_mul(out=MEAN[:, 0:zn], in0=PS[4][:, 0:zn], scalar1=1.0 / D))
            t1_ = DVE.tick()
            DVE.wait(t1_)
            DVE.emit(nc.vector.tensor_tensor(out=MSQ[:, 0:zn], in0=MEAN[:, 0:zn], in1=MEAN[:, 0:zn], op=ALU.mult))
            t2_ = DVE.tick()
            DVE.wait(t2_)
            DVE.emit(nc.vector.scalar_tensor_tensor(out=RSC[:, 0:zn], in0=PS[5][:, 0:zn], scalar=1.0 / D, in1=MSQ[:, 0:zn],
                                                    op0=ALU.mult, op1=ALU.subtract))
            t3_ = DVE.tick()
            st_read[0] = t3_
            ACT.wait(t3_)
            act(RSC[:, 0:zn], RSC[:, 0:zn], AF.Sqrt, bias=EPSC[:, 0:1], scale=1.0)
            t4_ = ACT.tick()
            DVE.wait(t4_)
            DVE.emit(nc.vector.reciprocal(out=RSC[:, 0:zn], in_=RSC[:, 0:zn]))
            t5_ = DVE.tick()
            DVE.wait(t5_)
            for cc in range(KC):
                DVE.emit(nc.vector.tensor_tensor(out=Z[:, cc, 0:zn], in0=Z[:, cc, 0:zn], in1=MEAN[:, 0:zn], op=ALU.subtract))
                tq = DVE.tick()
                DVE.wait(tq)
                DVE.emit(nc.vector.tensor_tensor(out=Z[:, cc, 0:zn], in0=Z[:, cc, 0:zn], in1=RSC[:, 0:zn], op=ALU.mult))
                tq = DVE.tick()
                ACT.wait(tq)
                act(UB[:, cc, zlo:zlo + zn], Z[:, cc, 0:zn], AF.Silu, scale=VTs[:, VO["lng"] + cc:VO["lng"] + cc + 1],
                    bias=VTs[:, VO["lnb"] + cc:VO["lnb"] + cc + 1])
                zdone[cc] = ACT.tick()

        emit_conv(0)
        nmod = 32
        pending = None
        for u in range(len(units)):
            if u + 1 < len(units):
                emit_conv(u + 1)
            if pending is not None:
                pending()
                pending = None
            emit_post(u)
            if u % 2 == 1 and nmod < 48:
                pending = mod_block(0, nmod, pbase=6, defer=True)
                nmod += 1
        if pending is not None:
            pending()
        while nmod < 48:
            mod_block(0, nmod, pbase=6)
            nmod += 1
        barrier()
        checkpoint(4)

        tilesX = [(0, 400), (400, 400), (800, 384)]
        proj_resid("pw2", lambda k, xlo, n_: UB[:, k, xlo:xlo + n_], tilesX, 32, "bpw2", 0)
        barrier()
        checkpoint(5)

        rms_modulate(0, 1, False)
        barrier()
        ffn(0, 0, tilesX)
        barrier()
        checkpoint(6)

        rms_modulate(1, 0, False)
        tb = barrier()
        checkpoint(7)
        ring["hold"] = tb
        POOL.wait(*tb)
        t_c = None
        for b in range(2):
            dma(POOL, KT[:, :, 1152 + 144 * b:1152 + 144 * b + 128], ckT[b].rearrange("g p k -> p g k"), ld2_sem)
            t_c = dma(POOL, VTM[:, 18 + 2 * b, :], cv[b], ld2_sem)
        SP.wait(*tb)
        t_b = dma(SP, BKVB[:, :], bkv_d.partition_broadcast(128), ld_sem)

        tilesK = [(0, 512), (512, 512), (1024, 160)]
        slot, bi = ring_use("kv_k")
        kfree = [None, None]
        kk = 0

        def ktcol(xlo):
            return xlo if xlo < 1152 else (1280 if xlo < 1168 else 1424)
        for g in range(4):
            for (xlo, ncols) in tilesK:
                bank = PS[kk % 2]
                PE.wait(kfree[kk % 2])
                for half in range(2):
                    for k in range(KC):
                        mm(bank[64 * half:64 * half + 64, 0:ncols], slot[:, k, 64 * g:64 * g + 64],
                           HB[:, k, 32 + xlo:32 + xlo + ncols], k == 0, k == KC - 1, tile_position=(0, 64 * half))
                tp = PE.tick()
                ACT.wait(tp)
                bk = VTs[:, VO["bkd"] + g:VO["bkd"] + g + 1]
                if xlo < 1024:
                    act(KT[:, g, xlo:xlo + ncols], bank[:, 0:ncols], AF.Identity, bias=bk)
                else:
                    act(KT[:, g, 1024:1152], bank[:, 0:128], AF.Identity, bias=bk)
                    act(KT[:, g, 1280:1296], bank[:, 128:144], AF.Identity, bias=bk)
                    act(KT[:, g, 1424:1440], bank[:, 144:160], AF.Identity, bias=bk)
                kfree[kk % 2] = ACT.tick()
                kk += 1
        DVE.wait(t_b)
        kv_t = [None, None]
        so = 0

        def tm_out(slot_, which, items):
            nonlocal so
            for (hcol, rows, dst) in items:
                bank = PS[2 + so % 2]
                PE.wait(tm_out.bfree[so % 2])
                for k in range(KC):
                    mm(bank[0:rows, 0:256], HB[:, k, hcol:hcol + rows], slot_[:, k, 0:256], k == 0, k == KC - 1)
                tp = PE.tick()
                DVE.wait(tp, kv_t[so % 2])
                DVE.emit(nc.vector.tensor_tensor(out=KVST[0:rows, so % 2, :], in0=bank[0:rows, 0:256],
                                                 in1=BKVB[0:rows, 256 * which:256 * which + 256], op=ALU.add))
                td = DVE.tick()
                tm_out.bfree[so % 2] = td
                SP.wait(td)
                kv_t[so % 2] = dma(SP, dst, KVST[0:rows, so % 2, :], kvst_sem[so % 2])
                so += 1
        tm_out.bfree = [None, None]
        tm_out(slot, 0, [(32 + 1024, 64, kp_o[0:64, :]), (32 + 1088, 64, kp_o[64:128, :]),
                         (32 + 1152, 16, ks_o[0, 112:128, :]), (32 + 1168, 16, ks_o[1, 112:128, :])])
        ring_release(bi)
        slot, bi = ring_use("kv_v")
        vfree = [None, None]
        vchunks = [(32 + 64 * i, 128 if i < 17 else 64, i) for i in range(18)] + [(32 + 1152, 16, 19), (32 + 1168, 16, 21)]
        for vi, (hcol, rows, vs) in enumerate(vchunks):
            bank = PS[4 + vi % 2]
            PE.wait(vfree[vi % 2])
            for k in range(KC):
                mm(bank[0:rows, 0:256], HB[:, k, hcol:hcol + rows], slot[:, k, 0:256], k == 0, k == KC - 1)
            tp = PE.tick()
            DVE.wait(tp)
            DVE.emit(nc.vector.tensor_tensor(out=VTM[0:rows, vs, :], in0=bank[0:rows, 0:256], in1=BKVB[0:rows, 256:512],
                                             op=ALU.add))
            if vs < 2:
                mr = 128 if vs == 0 else 64
                tq = DVE.tick()
                DVE.wait(tq)
                DVE.emit(nc.vector.tensor_scalar_mul(out=VTM[0:mr, vs, :], in0=VTM[0:mr, vs, :], scalar1=FLAG[0:mr, 0:1]))
            vfree[vi % 2] = DVE.tick()
        tm_out(slot, 1, [(32 + 1024, 64, vp_o[0:64, :]), (32 + 1088, 64, vp_o[64:128, :]),
                         (32 + 1152, 16, vs_o[0, 112:128, :]), (32 + 1168, 16, vs_o[1, 112:128, :])])
        ring_release(bi)
        barrier()
        checkpoint(8)
        tilesQ = [(128, 352), (480, 352), (832, 352)]
        qfree = [None, None]
        kk = 0
        for b in range(8):
            slot, bi = ring_use(f"q_{b}")
            for h in range(2):
                n = 2 * b + h
                for (xlo, ncols) in tilesQ:
                    bank = PS[kk % 2]
                    PE.wait(qfree[kk % 2])
                    for k in range(KC):
                        mm(bank[:, 0:ncols], slot[:, k, 128 * h:128 * h + 128], HB[:, k, 32 + xlo:32 + xlo + ncols],
                           k == 0, k == KC - 1)
                    tp = PE.tick()
                    ACT.wait(tp)
                    act(QT[:, n, xlo - 128:xlo - 128 + ncols], bank[:, 0:ncols], AF.Identity,
                        bias=VTs[:, VO["bq"] + n:VO["bq"] + n + 1])
                    qfree[kk % 2] = ACT.tick()
                    kk += 1
            ring_release(bi)
        barrier()

        checkpoint(9)
        SP.wait(PE.tick(), ACT.tick(), DVE.tick())
        t1 = dma(SP, OHS[0:32, :], ohrel, ld_sem)
        t2 = dma(SP, TBT[0:32, :], tableT, ld_sem)
        t3 = dma(SP, ES64[:, :], sinks_d.partition_broadcast(128), ld_sem)
        ACT.wait(t3)
        act(ES64[:, :], ES64[:, :], AF.Exp)
        ta = ACT.tick()
        DVE.wait(ta)
        PE.wait(t3, t_c)

        qblocks = []
        for qc in range(16):
            m01 = FLAGB if qc == 0 else (FLAGH if qc == 1 else ONESB)
            qblocks.append((64 * qc, 64, [(64 * qc, qc, 128, m01), (64 * qc + 128, qc + 2, 64, ONESB)]))
        for b in range(2):
            qblocks.append((1024 + 16 * b, 16, [(1152 + 144 * b, 18 + 2 * b, 128, ONESB),
                                               (1280 + 144 * b, 19 + 2 * b, 16, ONESB)]))
        barrier()
        for kt, (j0, nj) in enumerate([(0, 128), (128, 64)]):
            for q in range(64):
                mm(PS[q // 16][0:nj, 32 * (q % 16):32 * (q % 16) + 32], OHS[0:32, j0 + 63 - q:j0 + 63 - q + nj],
                   TBT[0:32, 0:32], True, True)
            tp = PE.tick()
            DVE.wait(tp)
            for g in range(4):
                for bq in range(4):
                    DVE.emit(nc.vector.tensor_copy(
                        out=BIAS[0:nj, g, kt, :].rearrange("j (par pair q) -> j par pair q", par=2, pair=4)[:, :, :, 16 * bq:16 * bq + 16],
                        in_=PS[bq][0:nj, :].rearrange("j (q h) -> j q h", h=32)[:, :, 8 * g:8 * g + 8]
                            .rearrange("j q (pair par) -> j par pair q", par=2)))
            td = DVE.tick()
            PE.wait(td)
        barrier()
        sfree = [None, None, None]
        ofree = [None, None]
        ssfree = [None, None, None]
        ptfree = {}
        exp_t = {}
        gi = 0
        for g in range(4):
            DVE.wait(PE.tick())
            for par in range(2):
                for b3 in range(3):
                    DVE.emit(nc.vector.tensor_copy(
                        out=PT[64:65, 3 * par + b3, 256:512].rearrange("o (a q) -> o a q", q=64),
                        in_=ES64[64:65, 8 * g + par:8 * g + 8:2].unsqueeze(2).to_broadcast([1, 4, 64])))
            tsk = DVE.tick()
            PE.wait(tsk)
            checkpoint(50)
            its = [(qcol, nq, keys, par) for (qcol, nq, keys) in qblocks for par in range(2)]

            st = {}

            def S_mm(i):
                qcol, nq, keys, par = its[i]
                b3 = (gi + i) % 3
                pp = slice(64 * par, 64 * par + 64)
                bS = PS[b3]
                PE.wait(sfree[b3])
                for kt, (kcol, vs, nk, msk) in enumerate(keys):
                    mm(bS[0:nk, 256 * kt:256 * kt + 4 * nq], KT[pp, g, kcol:kcol + nk], QT[pp, 4 * g:4 * g + 4, qcol:qcol + nq],
                       True, True, tile_position=(64 * par, 0))
                st[("s", i)] = PE.tick()

            def S_bias(i):
                qcol, nq, keys, par = its[i]
                b3 = (gi + i) % 3
                bS = PS[b3]
                DVE.wait(st[("s", i)], ssfree[b3])
                for kt, (kcol, vs, nk, msk) in enumerate(keys):
                    DVE.emit(nc.vector.scalar_tensor_tensor(
                        out=SS[0:nk, b3, 256 * kt:256 * kt + 4 * nq].rearrange("j (a q) -> j a q", q=nq),
                        in0=bS[0:nk, 256 * kt:256 * kt + 4 * nq].rearrange("j (a q) -> j a q", q=nq), scalar=0.125,
                        in1=BIAS[0:nk, g, kt, 256 * par:256 * par + 256].rearrange("j (a q) -> j a q", q=64)[:, :, 0:nq],
                        op0=ALU.mult, op1=ALU.add))
                td = DVE.tick()
                sfree[b3] = td
                st[("b", i)] = td

            def S_exp(i):
                qcol, nq, keys, par = its[i]
                b3 = (gi + i) % 3
                pb = 3 * par + b3
                ACT.wait(st[("b", i)], ptfree.get(pb))
                for kt, (kcol, vs, nk, msk) in enumerate(keys):
                    act(PT[0:nk, pb, 256 * kt:256 * kt + 4 * nq], SS[0:nk, b3, 256 * kt:256 * kt + 4 * nq], AF.Exp)
                ta = ACT.tick()
                ssfree[b3] = ta
                exp_t[i] = ta

            def PV_mm(i0):
                b2 = ((gi + i0) // 2) % 2
                bO = PS[3 + b2]
                PE.wait(exp_t[i0], exp_t[i0 + 1], ofree[b2])
                for di in range(2):
                    i = i0 + di
                    qcol, nq, keys, par = its[i]
                    pb = 3 * par + (gi + i) % 3
                    pp = slice(64 * par, 64 * par + 64)
                    for kt, (kcol, vs, nk, msk) in enumerate(keys):
                        mm(bO[pp, 0:4 * nq], VTM[0:nk, vs, 64 * g:64 * g + 64], PT[0:nk, pb, 256 * kt:256 * kt + 4 * nq],
                           kt == 0, kt == 1, tile_position=(0, 64 * par))
                for di in range(2):
                    i = i0 + di
                    qcol, nq, keys, par = its[i]
                    pb = 3 * par + (gi + i) % 3
                    pp = slice(64 * par, 64 * par + 64)
                    (kcol, vs, nk, msk) = keys[0]
                    mm(bO[pp, 256:256 + 4 * nq], msk[0:nk, 0:64], PT[0:nk, pb, 0:4 * nq], True, False, tile_position=(0, 64 * par))
                    (kcol, vs, nk, msk) = keys[1]
                    if nq == 64:
                        mm(bO[pp, 256:512], ONESB[0:65, 0:64], PT[0:65, pb, 256:512], False, True, tile_position=(0, 64 * par))
                    else:
                        mm(bO[pp, 256:256 + 4 * nq], ONESB[0:nk, 0:64], PT[0:nk, pb, 256:256 + 4 * nq], False, True,
                           tile_position=(0, 64 * par))
                tp = PE.tick()
                for di in range(2):
                    par = its[i0 + di][3]
                    ptfree[3 * par + (gi + i0 + di) % 3] = tp
                st[("pv", i0)] = tp

            def PV_ln(i0):
                qcol, nq, keys, _ = its[i0]
                b2 = ((gi + i0) // 2) % 2
                bO = PS[3 + b2]
                tp = st[("pv", i0)]
                if nq == 64:
                    ACT.wait(tp, ofree[b2])
                    act(RDEN[:, b2, 0:4 * nq], bO[:, 256:256 + 4 * nq], AF.Ln)
                else:
                    DVE.wait(tp, ofree[b2])
                    for par in range(2):
                        pp = slice(64 * par, 64 * par + 64)
                        DVE.emit(nc.vector.tensor_tensor(
                            out=RDEN[pp, b2, 0:4 * nq].rearrange("d (a q) -> d a q", q=nq),
                            in0=bO[pp, 256:256 + 4 * nq].rearrange("d (a q) -> d a q", q=nq),
                            in1=ES64[pp, 8 * g + par:8 * g + 8:2].unsqueeze(2).to_broadcast([64, 4, nq]), op=ALU.add))
                    tdd = DVE.tick()
                    ACT.wait(tdd)
                    act(RDEN[:, b2, 0:4 * nq], RDEN[:, b2, 0:4 * nq], AF.Ln)
                ta = ACT.tick()
                ACT.wait(ta)
                act(RDEN[:, b2, 0:4 * nq], RDEN[:, b2, 0:4 * nq], AF.Exp, scale=-1.0)
                st[("ln", i0)] = ACT.tick()

            def PV_mult(i0):
                qcol, nq, keys, _ = its[i0]
                b2 = ((gi + i0) // 2) % 2
                bO = PS[3 + b2]
                DVE.wait(st[("ln", i0)], st[("pv", i0)])
                DVE.emit(nc.vector.tensor_tensor(
                    out=QT[:, 4 * g:4 * g + 4, qcol:qcol + nq],
                    in0=bO[:, 0:4 * nq].rearrange("d (a q) -> d a q", q=nq),
                    in1=RDEN[:, b2, 0:4 * nq].rearrange("d (a q) -> d a q", q=nq), op=ALU.mult))
                ofree[b2] = DVE.tick()

            n_it = len(its)
            assert n_it % 2 == 0 and gi % 2 == 0
            for i in range(2):
                S_mm(i)
            for i in range(2):
                S_bias(i)
                S_exp(i)
            for i0 in range(0, n_it, 2):
                PV_mm(i0)
                nxt = i0 + 2 < n_it
                if nxt:
                    S_mm(i0 + 2)
                    S_mm(i0 + 3)
                    S_bias(i0 + 2)
                    S_bias(i0 + 3)
                PV_ln(i0)
                PV_mult(i0)
                if nxt:
                    S_exp(i0 + 2)
                    S_exp(i0 + 3)
            gi += n_it
        barrier()

        checkpoint(10)
        proj_resid("wo", lambda k, xlo, n_: QT[:, k, xlo - 128:xlo - 128 + n_], tilesQ, 32, "bo", 1)
        barrier()

        checkpoint(11)
        rms_modulate(1, 1, False)
        barrier()
        DVE.wait(kv_t[0], kv_t[1])
        ffn(1, 128, tilesQ)
        barrier()

        checkpoint(12)
        tilesF = [(128, 512), (640, 512), (1152, 32)]
        banks = [PS[5], PS[6], PS[7]]
        sqf = [None, None]
        for c in range(KC):
            ACT.wait(sqf[c % 2])
            act(SQ[:, c % 2, 32:NH], X[:, c, :], AF.Square)
            ta = ACT.tick()
            PE.wait(ta)
            for ti, (a, n_) in enumerate(tilesF):
                mm(banks[ti][:, 0:n_], ONESB[:], SQ[:, c % 2, 32 + a:32 + a + n_], c == 0, c == KC - 1)
            sqf[c % 2] = PE.tick()
        tp = PE.tick()
        ACT.wait(tp)
        for ti, (a, n_) in enumerate(tilesF):
            act(RSTD[:, 32 + a:32 + a + n_], banks[ti][:, 0:n_], AF.Sqrt, bias=EPSC[:, 0:1], scale=1.0 / D)
        ta = ACT.tick()
        DVE.wait(ta)
        DVE.emit(nc.vector.reciprocal(out=RSTD[:, 160:NH], in_=RSTD[:, 160:NH]))
        td = DVE.tick()
        barrier()
        YS = carve(SCR, 11776, (128, 2, NQ), F32)
        y_t = [None, None]
        for c in range(KC):
            DVE.wait(y_t[c % 2])
            DVE.emit(nc.vector.scalar_tensor_tensor(out=YS[:, c % 2, :], in0=X[:, c, 128:NX],
                                                    scalar=VTs[:, VO["nout"] + c:VO["nout"] + c + 1],
                                                    in1=RSTD[:, 160:NH], op0=ALU.mult, op1=ALU.mult))
            td = DVE.tick()
            SP.wait(td)
            y_t[c % 2] = dma(SP, yT[128 * c:128 * c + 128, :], YS[:, c % 2, :], yst_sem[c % 2])
        SP.wait(y_t[0], y_t[1], uf_t[0], uf_t[1], kv_t[0], kv_t[1], (st_sem, 16 * cnt[id(st_sem)]))
    return nc


_NC_CACHE = {}


def _host_inputs(inp):
    f = lambda a: np.ascontiguousarray(np.asarray(a, dtype=np.float32))
    xp, xs = f(inp["x_prompt"]), f(inp["x_sample"])
    cp, cs = f(inp["c_prompt"]), f(inp["c_sample"])
    stc, ckc, cvc = f(inp["state_conv"])[0], f(inp["cache_win_k"])[0], f(inp["cache_win_v"])[0]
    w_pw1 = f(inp["w_pw1"])[0]
    wpw1 = np.ascontiguousarray(np.stack([w_pw1[:, :D].reshape(D, 16, 128), w_pw1[:, D:].reshape(D, 16, 128)], axis=2).reshape(D, 2 * D))
    w_gu = f(inp["w_gu"])
    wgu = np.ascontiguousarray(np.stack([w_gu[:, :, :DFF].reshape(2, D, NFF, 128), w_gu[:, :, DFF:].reshape(2, D, NFF, 128)], axis=3).reshape(2, D, 2 * DFF))
    b_qkv = f(inp["b_qkv"])[0]
    b_pw1 = f(inp["b_pw1"])[0]
    bkd = np.stack([np.concatenate([b_qkv[2048 + 64 * g:2048 + 64 * g + 64]] * 2) for g in range(4)])
    rows = [f(inp["b_mod"]).reshape(192, 128), f(inp["norm_mix"]).reshape(32, 128), f(inp["norm_ffn"]).reshape(32, 128),
            b_pw1[:D].reshape(16, 128), b_pw1[D:].reshape(16, 128), f(inp["b_dw"]).reshape(16, 128),
            f(inp["conv_ln_g"]).reshape(16, 128), f(inp["conv_ln_b"]).reshape(16, 128), f(inp["b_pw2"]).reshape(16, 128),
            b_qkv[:2048].reshape(16, 128), bkd, f(inp["b_o"]).reshape(16, 128), f(inp["norm_out"]).reshape(16, 128)]
    vecs = np.concatenate(rows, axis=0)
    assert vecs.shape[0] == NVEC
    vecsT = np.ascontiguousarray(vecs.T)
    wdwT = np.ascontiguousarray(f(inp["w_dw"])[0].reshape(31, 16, 128).transpose(2, 0, 1).reshape(128, 31 * 16))
    rel = np.arange(255) - 191
    bk = t5_bucket_np(rel)
    ohrel = np.zeros((32, 255), np.float32)
    ohrel[bk, np.arange(255)] = 1.0
    def tile_w(W, nl):
        N = W.shape[1]
        return np.ascontiguousarray(W.reshape(nl, KC, 128, N // 256, 256).transpose(0, 3, 2, 1, 4).reshape(nl * (N // 256), 128, KC, 256))

    w_dn = f(inp["w_down"])
    wdn_t = np.zeros((2, 6, 8, 128, GRP, 256), np.float32)
    for li in range(2):
        for G in range(6):
            j0, j1 = G * GRP, min(NFF, (G + 1) * GRP)
            blk = w_dn[li, j0 * 128:j1 * 128, :].reshape(j1 - j0, 128, 8, 256)
            wdn_t[li, G, :, :, 0:j1 - j0, :] = blk.transpose(2, 1, 0, 3)
    wdn_t = wdn_t.reshape(96, 128, GRP, 256)
    shared = {
        "vecsT": vecsT, "wdwT": wdwT, "bkv": np.ascontiguousarray(b_qkv[2048:].reshape(1, 512)), "ohrel": ohrel,
        "tableT": np.ascontiguousarray(f(inp["rel_bias_table"]).T), "sinks": f(inp["attn_sinks"]).reshape(1, 32),
        "identf": np.eye(128, dtype=np.float32),
        "wmod": tile_w(f(inp["w_mod"]).reshape(2 * D, 6 * D), 2),
        "wpw1": tile_w(wpw1, 1), "wpw2": tile_w(f(inp["w_pw2"])[0], 1), "wqkv": tile_w(f(inp["w_qkv"])[0], 1),
        "wo": tile_w(f(inp["w_o"])[0], 1), "wgu": tile_w(wgu.reshape(2 * D, 2 * DFF), 2), "wdn": wdn_t,
    }
    maps = []
    for i in range(NCORES):
        b, seg = i // 4, i % 4
        t0 = seg * 1024
        xin = np.zeros((NH, D), np.float32)
        lo = t0 - 160
        if lo >= 0:
            xin[0:160] = xp[b, lo:t0]
        xin[160:1184] = xp[b, t0:t0 + 1024]
        xin[1184:1200] = xs[2 * i]
        xin[1200:1216] = xs[2 * i + 1]
        cvec = np.stack([cp[b], cs[2 * i], cs[2 * i + 1]])
        m = dict(shared)
        m["xT"] = np.ascontiguousarray(xin.T)
        m["cT"] = np.ascontiguousarray(cvec.reshape(3, 16, 128).transpose(2, 1, 0))
        m["flag"] = np.full((128, 1), 1.0 if seg > 0 else 0.0, np.float32)
        m["stT"] = np.ascontiguousarray(stc[2 * i:2 * i + 2].transpose(0, 2, 1))
        kk = ckc[2 * i:2 * i + 2]
        kT = kk.transpose(0, 2, 3, 1)
        m["ckT"] = np.ascontiguousarray(np.concatenate([kT, kT], axis=2))
        m["ck"] = np.ascontiguousarray(kk.reshape(2, 128, 256))
        m["cv"] = np.ascontiguousarray(cvc[2 * i:2 * i + 2].reshape(2, 128, 256))
        maps.append(m)
    return maps


def kernel(**inputs):
    if "nc" not in _NC_CACHE:
        _NC_CACHE["nc"] = build_nc()
    nc = _NC_CACHE["nc"]
    maps = _host_inputs(inputs)
    res = run_bass_kernel_spmd(nc, maps, core_ids=list(range(NCORES)))
    R = res.results
    stc = np.asarray(inputs["state_conv"], dtype=np.float32)
    y_prompt = np.zeros((2, SEQ, D), np.float32)
    y_sample = np.zeros((16, 16, D), np.float32)
    conv_prompt = np.zeros((1, 2, 30, D), np.float32)
    wk_p = np.zeros((1, 2, 128, 4, 64), np.float32)
    wv_p = np.zeros((1, 2, 128, 4, 64), np.float32)
    conv_sample = np.zeros((1, 16, 30, D), np.float32)
    wk_s = np.zeros((1, 16, 128, 4, 64), np.float32)
    wv_s = np.zeros((1, 16, 128, 4, 64), np.float32)
    for i in range(NCORES):
        b, seg = i // 4, i % 4
        r = R[i]
        yT = np.asarray(r["yT"])
        y_prompt[b, seg * 1024:(seg + 1) * 1024] = yT[:, 0:1024].T
        y_sample[2 * i] = yT[:, 1024:1040].T
        y_sample[2 * i + 1] = yT[:, 1040:1056].T
        uc = np.asarray(r["ucT"])
        hd = np.asarray(r["cs_head"])
        for j in range(2):
            conv_sample[0, 2 * i + j, 0:14] = hd[j].T
            conv_sample[0, 2 * i + j, 14:30] = uc[:, 30 + 16 * j:46 + 16 * j].T
            wk_s[0, 2 * i + j] = np.asarray(r["ks_o"])[j].reshape(128, 4, 64)
            wv_s[0, 2 * i + j] = np.asarray(r["vs_o"])[j].reshape(128, 4, 64)
        if seg == 3:
            conv_prompt[0, b] = uc[:, 0:30].T
            wk_p[0, b] = np.asarray(r["kp_o"]).reshape(128, 4, 64)
            wv_p[0, b] = np.asarray(r["vp_o"]).reshape(128, 4, 64)
    return (y_prompt, y_sample, conv_prompt, wk_p, wv_p, conv_sample, wk_s, wv_s)
```

```python
import numpy as np
import concourse.bass as bass
import concourse.mybir as mybir
from concourse.bass_utils import run_bass_kernel_spmd
from contextlib import ExitStack, suppress

F32 = mybir.dt.float32
BF16 = mybir.dt.bfloat16
U8 = mybir.dt.uint8
AF = mybir.ActivationFunctionType
ALU = mybir.AluOpType

D = 2048
KC = 16
DFF = 5632
NFF = 44
SEQ = 4096
NCORES = 8
EPS = 1e-6
NX = 1184
NH = 1216
NU = 1276
NQ = 1056
NKT = 1440
GRP = 8

VO = {}
_o = 0
for _n, _c in [("bmod", 192), ("nmix", 32), ("nffn", 32), ("bpa", 16), ("bpg", 16), ("bdw", 16), ("lng", 16),
               ("lnb", 16), ("bpw2", 16), ("bq", 16), ("bkd", 4), ("bo", 16), ("nout", 16)]:
    VO[_n] = _o
    _o += _c
NVEC = _o


def t5_bucket_np(rel):
    nb = 16
    max_exact = 8
    ret = np.where(rel > 0, nb, 0)
    n = np.abs(rel)
    nf = np.maximum(n, 1).astype(np.float32)
    large = max_exact + (np.log(nf / max_exact) / np.float32(np.log(128 / max_exact)) * (nb - max_exact)).astype(np.int32)
    large = np.minimum(large, nb - 1)
    return ret + np.where(n < max_exact, n, large)


STOP = None


class _StopBuild(Exception):
    pass


class Eng:
    def __init__(self, nc, eng, name, es):
        self.e = eng
        self.sem = es.enter_context(nc.semaphore(name))
        self.n = 0
        self.last = None
        self.marked = True
        self.seen = {}

    def emit(self, inst):
        self.last = inst
        self.marked = False
        return inst

    def tick(self):
        if not self.marked:
            self.n += 1
            self.last.then_inc(self.sem, 1)
            self.marked = True
        return (self.sem, self.n)

    def wait(self, *ts):
        for t in ts:
            if t is None:
                continue
            sem, v = t
            if v <= 0:
                continue
            k = id(sem)
            if self.seen.get(k, 0) >= v:
                continue
            self.e.wait_ge(sem, v)
            self.seen[k] = v


def build_nc():
    nc = bass.Bass("TRN2", target_bir_lowering=False)
    dt_in = lambda n, s: nc.dram_tensor(n, list(s), F32, kind="ExternalInput").ap()
    dt_out = lambda n, s: nc.dram_tensor(n, list(s), F32, kind="ExternalOutput").ap()
    xT = dt_in("xT", (D, NH))
    cT = dt_in("cT", (128, KC, 3))
    flag_d = dt_in("flag", (128, 1))
    stT = dt_in("stT", (2, D, 30))
    ckT = dt_in("ckT", (2, 4, 128, 128))
    ck = dt_in("ck", (2, 128, 256))
    cv = dt_in("cv", (2, 128, 256))
    vecsT = dt_in("vecsT", (128, NVEC))
    wdwT = dt_in("wdwT", (128, 31 * 16))
    bkv_d = dt_in("bkv", (1, 512))
    ohrel = dt_in("ohrel", (32, 255))
    tableT = dt_in("tableT", (32, 32))
    sinks_d = dt_in("sinks", (1, 32))
    identf = dt_in("identf", (128, 128))
    wmod = dt_in("wmod", (96, 128, KC, 256))
    wpw1 = dt_in("wpw1", (16, 128, KC, 256))
    wpw2 = dt_in("wpw2", (8, 128, KC, 256))
    wqkv = dt_in("wqkv", (10, 128, KC, 256))
    wo = dt_in("wo", (8, 128, KC, 256))
    wgu = dt_in("wgu", (88, 128, KC, 256))
    wdn = dt_in("wdn", (96, 128, GRP, 256))

    yT = dt_out("yT", (D, NQ))
    ucT = dt_out("ucT", (D, 62))
    cs_head = dt_out("cs_head", (2, D, 14))
    kp_o = dt_out("kp_o", (128, 256))
    vp_o = dt_out("vp_o", (128, 256))
    ks_o = dt_out("ks_o", (2, 128, 256))
    vs_o = dt_out("vs_o", (2, 128, 256))

    es = ExitStack()
    with es, suppress(_StopBuild):
        sb = lambda n, s, d: es.enter_context(nc.sbuf_tensor(n, list(s), d))
        X = sb("X", (128, KC, NX), F32)
        HBraw = sb("HBraw", (128, KC * NH * 2), U8)
        UBraw = sb("UBraw", (128, KC * NU * 2), U8)
        RING = sb("RING", (128, 3, KC, 256), BF16)
        SCR = sb("SCR", (128, 23808), U8)
        VTs = sb("VTs", (128, NVEC), F32)
        MOD = sb("MOD", (128, 2, 96, 3), F32)
        AM = sb("AM", (128, KC, 3), F32)
        G1B = sb("G1B", (128, KC, 3), F32)
        IDB = sb("IDB", (128, 128), BF16)
        IDF = sb("IDF", (128, 128), F32)
        ONESB = sb("ONESB", (128, 128), BF16)
        FLAG = sb("FLAG", (128, 1), F32)
        FLAGB = sb("FLAGB", (128, 64), BF16)
        CTs = sb("CTs", (128, KC, 3), F32)
        SCT = sb("SCT", (128, KC, 3), BF16)
        UFs = sb("UFs", (128, 2, 62), F32)
        ESK = sb("ESK", (1, 2, 32), BF16)
        ESF = sb("ESF", (1, 3, 32), F32)
        ES64 = sb("ES64", (128, 32), F32)
        FLAGH = sb("FLAGH", (128, 64), BF16)
        ONE1 = sb("ONE1", (1, 64), BF16)
        EPSC = sb("EPSC", (128, 1), F32)
        PS = [es.enter_context(nc.psum_tensor(f"ps{i}", [128, 512], F32)) for i in range(8)]

        def carve(raw, off, shape, dtype):
            n = int(np.prod(shape[1:]))
            bs = 2 if dtype == BF16 else 4
            v = raw[:, off:off + n * bs].bitcast(dtype)
            if len(shape) == 3:
                v = v.rearrange("p (a b) -> p a b", b=shape[2])
            elif len(shape) == 4:
                v = v.rearrange("p (a b c) -> p a b c", b=shape[2], c=shape[3])
            return v

        HB = carve(HBraw, 0, (128, KC, NH), BF16)
        UB = carve(UBraw, 0, (128, KC, NU), BF16)
        QT = carve(UBraw, 0, (128, KC, NQ), BF16)
        ACTR = carve(UBraw, 0, (128, 16, NX), BF16)
        KVST = carve(UBraw, 33792, (128, 2, 256), F32)
        BKVB = carve(UBraw, 33792 + 2048, (128, 512), F32)
        XP = carve(SCR, 0, (128, KC, 32), F32)
        RSTD = carve(SCR, 2048, (128, NH), F32)
        T1 = carve(SCR, 6912, (128, NH), F32)
        T1B = carve(SCR, 16640, (128, NH), F32)
        SQ = carve(SCR, 11776, (128, 2, NH), BF16)
        SG = carve(SCR, 16640, (128, 2, 512), F32)
        TMPE = carve(SCR, 20736, (128, 512), F32)
        KT = carve(SCR, 0, (128, 4, NKT), BF16)
        VTM = carve(SCR, 11520, (128, 22, 256), BF16)
        YST = carve(SCR, 0, (128, 2, NQ), F32)
        Z = carve(HBraw, 0, (128, KC, 416), F32)
        ZB = carve(HBraw, 26624, (128, 2, 416), BF16)
        ZSQ = carve(HBraw, 28288, (128, 2, 416), BF16)
        MEAN = carve(SCR, 0, (128, 416), F32)
        MSQ = carve(SCR, 1664, (128, 416), F32)
        RSC = carve(SCR, 3328, (128, 416), F32)
        WDW = carve(SCR, 4992, (128, 31, 16), F32)
        DG = carve(SCR, 7168, (128, 2, 31, 128), BF16)
        BIAS = carve(HBraw, 0, (128, 4, 2, 512), F32)
        PT = carve(HBraw, 16384, (128, 6, 512), BF16)
        SS = carve(HBraw, 22528, (128, 3, 512), F32)
        RDEN = carve(HBraw, 28672, (128, 2, 256), F32)
        OHS = carve(HBraw, 30720, (128, 255), F32)
        TBT = carve(HBraw, 31744, (128, 32), F32)

        PE = Eng(nc, nc.tensor, "s_pe", es)
        ACT = Eng(nc, nc.scalar, "s_act", es)
        DVE = Eng(nc, nc.vector, "s_dve", es)
        SP = Eng(nc, nc.sync, "s_sp", es)
        POOL = Eng(nc, nc.gpsimd, "s_pool", es)
        ld_sem = es.enter_context(nc.semaphore("ld"))
        ld2_sem = es.enter_context(nc.semaphore("ld2"))
        st_sem = es.enter_context(nc.semaphore("st"))
        kvst_sem = [es.enter_context(nc.semaphore(f"kvst{i}")) for i in range(2)]
        uf_sem = [es.enter_context(nc.semaphore(f"uf{i}")) for i in range(2)]
        yst_sem = [es.enter_context(nc.semaphore(f"yst{i}")) for i in range(2)]
        slot_sem = [es.enter_context(nc.semaphore(f"slot{i}")) for i in range(3)]
        cnt = {}

        def dma(engw, out, in_, sem):
            engw.e.dma_start(out=out, in_=in_).then_inc(sem, 16)
            cnt[id(sem)] = cnt.get(id(sem), 0) + 1
            return (sem, 16 * cnt[id(sem)])

        def mm(out, lhsT, rhs, start, stop, **kw):
            return PE.emit(nc.tensor.matmul(out, lhsT=lhsT, rhs=rhs, start=start, stop=stop, **kw))

        def act(out, in_, func, **kw):
            return ACT.emit(nc.scalar.activation(out=out, in_=in_, func=func, **kw))

        def barrier():
            ts = [PE.tick(), ACT.tick(), DVE.tick()]
            for e in (PE, ACT, DVE):
                e.wait(*ts)
            return ts

        def checkpoint(stage):
            if STOP == stage:
                barrier()
                SP.wait(PE.tick(), ACT.tick(), DVE.tick())
                POOL.wait(PE.tick())
                for sem in [st_sem, ld_sem, ld2_sem] + uf_sem + kvst_sem + yst_sem + slot_sem:
                    if cnt.get(id(sem)):
                        SP.wait((sem, 16 * cnt[id(sem)]))
                raise _StopBuild()

        plan = []

        def wblock(name, Wt, blk, nk):
            plan.append((name, nk, 256, Wt[blk, :, 0:nk, :]))

        for b in range(16):
            wblock(f"mod0_{b}", wmod, b, KC)
        for c in range(16):
            wblock(f"pw1_{c}", wpw1, c, KC)
            wblock(f"mod0_{16 + c}", wmod, 16 + c, KC)
        for b in range(32, 48):
            wblock(f"mod0_{b}", wmod, b, KC)
        for b in range(8):
            wblock(f"pw2_{b}", wpw2, b, KC)
        for li in range(2):
            if li == 1:
                wblock("kv_k", wqkv, 8, KC)
                wblock("kv_v", wqkv, 9, KC)
                for b in range(8):
                    wblock(f"q_{b}", wqkv, b, KC)
                for b in range(8):
                    wblock(f"wo_{b}", wo, b, KC)
            ngrp = (NFF + GRP - 1) // GRP
            for G in range(ngrp):
                j0, j1 = G * GRP, min(NFF, (G + 1) * GRP)
                for j in range(j0, j1):
                    wblock(f"gu{li}_{j}", wgu, 44 * li + j, KC)
                    if li == 0:
                        wblock(f"mod1_{j}", wmod, 48 + j, KC)
                        if j == NFF - 1:
                            for b in range(NFF, 48):
                                wblock(f"mod1_{b}", wmod, 48 + b, KC)
                for b in range(8):
                    wblock(f"dn{li}_{G}_{b}", wdn, (li * 6 + G) * 8 + b, j1 - j0)

        ring = {"next_pf": 0, "next_use": 0, "free": [None, None, None], "load": {}, "hold": None}

        def ring_prefetch(i):
            name, nk, ncols, src = plan[i]
            s = i % 3
            POOL.wait(ring["free"][s])
            if ring["hold"] is not None and name == "kv_k":
                POOL.wait(*ring["hold"])
            ring["load"][i] = dma(POOL, RING[:, s, 0:nk, 0:ncols], src, slot_sem[s])

        def ring_use(name):
            i = ring["next_use"]
            assert plan[i][0] == name, (plan[i][0], name)
            while ring["next_pf"] <= min(i + 2, len(plan) - 1):
                ring_prefetch(ring["next_pf"])
                ring["next_pf"] += 1
            PE.wait(ring["load"][i])
            ring["next_use"] += 1
            return RING[:, i % 3], i

        def ring_release(i):
            ring["free"][i % 3] = PE.tick()

        xv = xT.rearrange("(c p) t -> p c t", p=128)
        for q in range(4):
            dma(SP, X[:, 4 * q:4 * q + 4, :], xv[:, 4 * q:4 * q + 4, 32:NH], ld_sem)
        dma(SP, XP, xv[:, :, 0:32], ld_sem)
        dma(SP, CTs[:], cT, ld_sem)
        dma(SP, FLAG[:], flag_d, ld_sem)
        dma(SP, VTs[:], vecsT, ld_sem)
        dma(SP, IDF[:], identf, ld_sem)
        t_ld = (ld_sem, 16 * cnt[id(ld_sem)])
        stv = stT.rearrange("b (c p) w -> b p c w", p=128)
        t_st = None
        for b in range(2):
            t_st = dma(POOL, UB[:, :, 1184 + 46 * b:1184 + 46 * b + 30], stv[b], ld2_sem)
        for b in range(2):
            dma(SP, cs_head[b], stT[b, :, 16:30], st_sem)
        for b in range(2):
            dma(SP, ks_o[b, 0:112, :], ck[b, 16:128, :], st_sem)
            dma(SP, vs_o[b, 0:112, :], cv[b, 16:128, :], st_sem)

        DVE.wait(t_ld)
        ACT.wait(t_ld)
        DVE.emit(nc.vector.memset(ONESB[:], 1.0))
        DVE.emit(nc.vector.memset(ONE1[:], 1.0))
        DVE.emit(nc.vector.memset(EPSC[:], EPS))
        DVE.emit(nc.vector.tensor_copy(out=IDB[:], in_=IDF[:]))
        DVE.emit(nc.vector.tensor_copy(out=FLAGB[:], in_=FLAG[:, 0:1].to_broadcast([128, 64])))
        DVE.emit(nc.vector.memset(FLAGH[64:128, :], 1.0))
        DVE.emit(nc.vector.tensor_copy(out=FLAGH[0:64, :], in_=FLAG[0:64, 0:1].to_broadcast([64, 64])))
        act(SCT[:], CTs[:], AF.Silu)
        barrier()
        checkpoint(0)

        def mod_block(li, b, pbase=4, defer=False):
            slot, bi = ring_use(f"mod{li}_{b}")
            bank = PS[pbase + (mod_block.k % 2)]
            PE.wait(mod_block.free[mod_block.k % 2])
            for h in range(2):
                for k in range(KC):
                    mm(bank[:, 4 * h:4 * h + 3], slot[:, k, 128 * h:128 * h + 128], SCT[:, k, :], k == 0, k == KC - 1)
            ring_release(bi)
            tp = PE.tick()
            kslot = mod_block.k % 2
            mod_block.k += 1

            def evac():
                DVE.wait(tp)
                for h in range(2):
                    n = 2 * b + h
                    DVE.emit(nc.vector.tensor_scalar_add(out=MOD[:, li, n, :], in0=bank[:, 4 * h:4 * h + 3],
                                                         scalar1=VTs[:, VO["bmod"] + 96 * li + n:VO["bmod"] + 96 * li + n + 1]))
                mod_block.free[kslot] = DVE.tick()
            if defer:
                return evac
            evac()
            return None
        mod_block.k = 0
        mod_block.free = [None, None]

        SEGH = [(0, 1184, 0), (1184, 1200, 1), (1200, 1216, 2)]

        def segs(lo, hi, off=0):
            out = []
            for (a, b_, bi) in [(0, 1152, 0), (1152, 1168, 1), (1168, 1184, 2)]:
                a2, b2 = max(lo, a + off if False else a), min(hi, b_)
                if a2 < b2:
                    out.append((a2, b2, bi))
            return out

        def rms_modulate(li, which, with_pre, part="all", banks=None):
            nv = VO["nmix"] if which == 0 else VO["nffn"]
            sh0, sc0 = (0, 16) if which == 0 else (48, 64)
            c0 = 0 if with_pre else 32
            tiles = [(c0, 512), (512, 1024), (1024, NH)]
            banks = banks or [PS[5], PS[6], PS[7]]
            for c in (range(KC) if part != "apply" else []):
                sqb = SQ[:, c % 2, :]
                ACT.wait(rms_modulate.sq_free[c % 2])
                if with_pre:
                    act(sqb[:, 0:32], XP[:, c, :], AF.Square)
                act(sqb[:, 32:NH], X[:, c, :], AF.Square)
                ta = ACT.tick()
                PE.wait(ta)
                for ti, (a, b_) in enumerate(tiles):
                    mm(banks[ti][:, 0:b_ - a], ONESB[:], sqb[:, a:b_], c == 0, c == KC - 1)
                rms_modulate.sq_free[c % 2] = PE.tick()
            if part != "apply":
                tp = PE.tick()
                ACT.wait(tp)
                for ti, (a, b_) in enumerate(tiles):
                    act(RSTD[:, a:b_], banks[ti][:, 0:b_ - a], AF.Sqrt, bias=EPSC[:, 0:1], scale=1.0 / D)
                ta = ACT.tick()
                DVE.wait(ta)
                DVE.emit(nc.vector.reciprocal(out=RSTD[:, c0:NH], in_=RSTD[:, c0:NH]))
            if part == "stats":
                return
            for c in range(KC):
                DVE.emit(nc.vector.tensor_scalar(out=AM[:, c, :], in0=MOD[:, li, sc0 + c, :], scalar1=1.0,
                                                 scalar2=VTs[:, nv + 16 * li + c:nv + 16 * li + c + 1],
                                                 op0=ALU.add, op1=ALU.mult))
            td = DVE.tick()
            DVE.wait(td)
            ACT.wait(td)
            tprev = [None, None]
            for c in range(KC):
                Tc = T1 if c % 2 == 0 else T1B
                DVE.wait(tprev[c % 2])
                if with_pre:
                    DVE.emit(nc.vector.tensor_tensor(out=Tc[:, 0:32], in0=XP[:, c, :], in1=RSTD[:, 0:32], op=ALU.mult))
                DVE.emit(nc.vector.tensor_tensor(out=Tc[:, 32:NH], in0=X[:, c, :], in1=RSTD[:, 32:NH], op=ALU.mult))
                td = DVE.tick()
                ACT.wait(td)
                for (a, b_, bi) in SEGH:
                    a2 = max(a, c0)
                    act(HB[:, c, a2:b_], Tc[:, a2:b_], AF.Identity, scale=AM[:, c, bi:bi + 1],
                        bias=MOD[:, li, sh0 + c, bi:bi + 1])
                tprev[c % 2] = ACT.tick()
        rms_modulate.sq_free = [None, None]

        def resid_epilogue(bank, ncols, n, xlo, gsrc, bvec):
            ACT.wait(resid_epilogue.tfree)
            for (a, b_, bi) in segs(xlo, xlo + ncols):
                if bvec is None:
                    act(TMPE[:, a - xlo:b_ - xlo], bank[:, a - xlo:b_ - xlo], AF.Identity, scale=gsrc[:, n, bi:bi + 1])
                else:
                    act(TMPE[:, a - xlo:b_ - xlo], bank[:, a - xlo:b_ - xlo], AF.Identity,
                        scale=gsrc[:, n, bi:bi + 1], bias=bvec[:, n, bi:bi + 1])
            ta = ACT.tick()
            DVE.wait(ta)
            DVE.emit(nc.vector.tensor_tensor(out=X[:, n, xlo:xlo + ncols], in0=X[:, n, xlo:xlo + ncols],
                                             in1=TMPE[:, 0:ncols], op=ALU.add))
            resid_epilogue.tfree = DVE.tick()
            return resid_epilogue.tfree
        resid_epilogue.tfree = None

        def proj_resid(wname, rhs_of, tiles, gch, bname, li):
            for n in range(KC):
                DVE.emit(nc.vector.tensor_scalar_mul(out=G1B[:, n, :], in0=MOD[:, li, gch + n, :],
                                                     scalar1=VTs[:, VO[bname] + n:VO[bname] + n + 1]))
            tg = DVE.tick()
            ACT.wait(tg)
            bfree = [None, None]
            kk = 0
            for b in range(8):
                slot, bi = ring_use(f"{wname}_{b}")
                for h in range(2):
                    n = 2 * b + h
                    for (xlo, ncols) in tiles:
                        bank = PS[kk % 2]
                        PE.wait(bfree[kk % 2])
                        for k in range(KC):
                            mm(bank[:, 0:ncols], slot[:, k, 128 * h:128 * h + 128], rhs_of(k, xlo, ncols), k == 0, k == KC - 1)
                        tp = PE.tick()
                        ACT.wait(tp)
                        resid_epilogue(bank, ncols, n, xlo, MOD[:, li, gch:gch + 16, :], G1B)
                        bfree[kk % 2] = ACT.tick()
                        kk += 1
                ring_release(bi)

        def ffn(li, xlo_all, tiles):
            gch = 80
            ngrp = (NFF + GRP - 1) // GRP
            gfree = [None, None]
            ufree = [None, None]
            sgfree = [None, None]
            dfree = [None, None]
            kk = 0
            dk = 0
            for G in range(ngrp):
                j0, j1 = G * GRP, min(NFF, (G + 1) * GRP)
                for j in range(j0, j1):
                    slot, bi = ring_use(f"gu{li}_{j}")
                    aslot = (G % 2) * GRP + (j - j0)
                    for (xlo, ncols) in tiles:
                        bg, bu = PS[kk % 2], PS[2 + kk % 2]
                        PE.wait(gfree[kk % 2], ufree[kk % 2])
                        for k in range(KC):
                            mm(bg[:, 0:ncols], slot[:, k, 0:128], HB[:, k, 32 + xlo:32 + xlo + ncols], k == 0, k == KC - 1)
                        tpg = PE.tick()
                        for k in range(KC):
                            mm(bu[:, 0:ncols], slot[:, k, 128:256], HB[:, k, 32 + xlo:32 + xlo + ncols], k == 0, k == KC - 1)
                        tpu = PE.tick()
                        ACT.wait(tpg, sgfree[kk % 2])
                        act(SG[:, kk % 2, 0:ncols], bg[:, 0:ncols], AF.Silu)
                        gfree[kk % 2] = ACT.tick()
                        DVE.wait(gfree[kk % 2], tpu)
                        DVE.emit(nc.vector.tensor_tensor(out=ACTR[:, aslot, xlo:xlo + ncols], in0=bu[:, 0:ncols],
                                                         in1=SG[:, kk % 2, 0:ncols], op=ALU.mult))
                        ufree[kk % 2] = DVE.tick()
                        sgfree[kk % 2] = ufree[kk % 2]
                        kk += 1
                    ring_release(bi)
                    if li == 0:
                        mod_block(1, j)
                        if j == NFF - 1:
                            for b in range(NFF, 48):
                                mod_block(1, b)
                t_act = DVE.tick()
                PE.wait(t_act)
                for b in range(8):
                    slot, bi = ring_use(f"dn{li}_{G}_{b}")
                    for h in range(2):
                        n = 2 * b + h
                        for (xlo, ncols) in tiles:
                            bank = PS[6 + dk % 2]
                            PE.wait(dfree[dk % 2])
                            for jj in range(j1 - j0):
                                mm(bank[:, 0:ncols], slot[:, jj, 128 * h:128 * h + 128],
                                   ACTR[:, (G % 2) * GRP + jj, xlo:xlo + ncols], jj == 0, jj == j1 - j0 - 1)
                            tp = PE.tick()
                            ACT.wait(tp)
                            resid_epilogue(bank, ncols, n, xlo, MOD[:, li, gch:gch + 16, :], None)
                            dfree[dk % 2] = ACT.tick()
                            dk += 1
                    ring_release(bi)

        rms_modulate(0, 0, True, part="stats", banks=[PS[1], PS[2], PS[3]])
        for b in range(16):
            mod_block(0, b)
        barrier()
        checkpoint(1)
        rms_modulate(0, 0, True, part="apply")
        barrier()
        checkpoint(2)

        DVE.wait(t_st)
        tiles1 = [(0, 512), (512, 1024), (1024, NH)]
        afree = [None, None]
        gfree = [None, None]
        sgfree = [None, None]
        uf_t = [None, None]
        kk = 0
        for c in range(KC):
            slot, bi = ring_use(f"pw1_{c}")
            for (a, b_) in tiles1:
                ncols = b_ - a
                ba, bgt = PS[kk % 2], PS[2 + kk % 2]
                PE.wait(afree[kk % 2], gfree[kk % 2])
                for k in range(KC):
                    mm(ba[:, 0:ncols], slot[:, k, 0:128], HB[:, k, a:b_], k == 0, k == KC - 1)
                tpa = PE.tick()
                for k in range(KC):
                    mm(bgt[:, 0:ncols], slot[:, k, 128:256], HB[:, k, a:b_], k == 0, k == KC - 1)
                tpg = PE.tick()
                ACT.wait(tpg, sgfree[kk % 2])
                act(SG[:, kk % 2, 0:ncols], bgt[:, 0:ncols], AF.Sigmoid, bias=VTs[:, VO["bpg"] + c:VO["bpg"] + c + 1])
                gfree[kk % 2] = ACT.tick()
                DVE.wait(gfree[kk % 2], tpa)
                ba_s = VTs[:, VO["bpa"] + c:VO["bpa"] + c + 1]

                def glu(out, lo, hi):
                    DVE.emit(nc.vector.scalar_tensor_tensor(out=out, in0=ba[:, lo:hi], scalar=ba_s,
                                                            in1=SG[:, kk % 2, lo:hi], op0=ALU.add, op1=ALU.mult))
                if a < 1024:
                    glu(UB[:, c, a:b_], 0, ncols)
                else:
                    glu(UB[:, c, 1024:1184], 0, 160)
                    glu(UB[:, c, 1214:1230], 160, 176)
                    glu(UB[:, c, 1260:1276], 176, 192)
                    DVE.wait(uf_t[c % 2])
                    glu(UFs[:, c % 2, 0:30], 130, 160)
                    glu(UFs[:, c % 2, 30:62], 160, 192)
                    tu = DVE.tick()
                    SP.wait(tu)
                    uf_t[c % 2] = dma(SP, ucT[128 * c:128 * c + 128, :], UFs[:, c % 2, :], uf_sem[c % 2])
                afree[kk % 2] = DVE.tick()
                sgfree[kk % 2] = afree[kk % 2]
                if a == 0:
                    DVE.wait(afree[kk % 2])
                    DVE.emit(nc.vector.tensor_scalar_mul(out=UB[:, c, 0:160], in0=UB[:, c, 0:160], scalar1=FLAG[:, 0:1]))
                kk += 1
            ring_release(bi)
            mod_block(0, 16 + c)
        barrier()
        checkpoint(3)

        SP.wait(PE.tick(), ACT.tick(), DVE.tick())
        tw = dma(SP, WDW, wdwT.rearrange("p (w c) -> p w c", c=16), ld_sem)
        DVE.wait(tw)
        ACT.wait(tw)
        ctiles = [(0, 384), (384, 384), (768, 416)]
        units = [(zlo, zn, c) for (zlo, zn) in ctiles for c in range(KC)]
        dgfree = [None, None]
        cfree = [None, None, None, None]
        zdone = [None] * KC
        zbfree = [None, None]
        conv_t = {}
        st_read = [None]
        NDV = 26

        def emit_conv(u):
            zlo, zn, c = units[u]
            b = u % 2
            samp = zn > 384
            DVE.wait(dgfree[b])
            ACT.wait(dgfree[b])
            DVE.emit(nc.vector.tensor_tensor(out=DG[:, b, 0:NDV, :], in0=IDB[:].unsqueeze(1).to_broadcast([128, NDV, 128]),
                                             in1=WDW[:, 0:NDV, c:c + 1].to_broadcast([128, NDV, 128]), op=ALU.mult))
            td = DVE.tick()
            for w in range(NDV, 31):
                act(DG[:, b, w, :], IDB[:], AF.Identity, scale=WDW[:, w, c:c + 1])
            ta = ACT.tick()
            bank = PS[u % 4]
            PE.wait(td, ta, cfree[u % 4])
            for w in range(31):
                mm(bank[:, 0:384], DG[:, b, w, :], UB[:, c, zlo + 2 + w:zlo + 2 + w + 384], w == 0, w == 30)
            if samp:
                for w in range(31):
                    mm(bank[:, 384:416], DG[:, b, w, :],
                       UB[:, c, 1184:1276].rearrange("p (b t) -> p b t", t=46)[:, :, w:w + 16], w == 0, w == 30)
            conv_t[u] = PE.tick()
            dgfree[b] = conv_t[u]

        def emit_post(u):
            zlo, zn, c = units[u]
            b = u % 2
            bank = PS[u % 4]
            bdw = VTs[:, VO["bdw"] + c:VO["bdw"] + c + 1]
            ACT.wait(conv_t[u], zdone[c])
            act(Z[:, c, 0:zn], bank[:, 0:zn], AF.Identity, bias=bdw)
            tz = ACT.tick()
            cfree[u % 4] = tz
            DVE.wait(tz, zbfree[b])
            DVE.emit(nc.vector.tensor_copy(out=ZB[:, b, 0:zn], in_=Z[:, c, 0:zn]))
            td = DVE.tick()
            ACT.wait(tz, zbfree[b])
            act(ZSQ[:, b, 0:zn], Z[:, c, 0:zn], AF.Square)
            ta = ACT.tick()
            PE.wait(td, ta)
            if c == 0:
                PE.wait(st_read[0])
            mm(PS[4][:, 0:zn], ONESB[:], ZB[:, b, 0:zn], c == 0, c == KC - 1)
            mm(PS[5][:, 0:zn], ONESB[:], ZSQ[:, b, 0:zn], c == 0, c == KC - 1)
            zbfree[b] = PE.tick()
            if c != KC - 1:
                return
            ts = zbfree[b]
            DVE.wait(ts)
            DVE.emit(nc.vector.tensor_scalar_mul(out=MEAN[:, 0:zn], in0=PS[4][:, 0:zn], scalar1=1.0 / D))
            t1_ = DVE.tick()
            DVE.wait(t1_)
            DVE.emit(nc.vector.tensor_tensor(out=MSQ[:, 0:zn], in0=MEAN[:, 0:zn], in1=MEAN[:, 0:zn], op=ALU.mult))
            t2_ = DVE.tick()
            DVE.wait(t2_)
            DVE.emit(nc.vector.scalar_tensor_tensor(out=RSC[:, 0:zn], in0=PS[5][:, 0:zn], scalar=1.0 / D, in1=MSQ[:, 0:zn],
                                                    op0=ALU.mult, op1=ALU.subtract))
            t3_ = DVE.tick()
            st_read[0] = t3_
            ACT.wait(t3_)
            act(RSC[:, 0:zn], RSC[:, 0:zn], AF.Sqrt, bias=EPSC[:, 0:1], scale=1.0)
            t4_ = ACT.tick()
            DVE.wait(t4_)
            DVE.emit(nc.vector.reciprocal(out=RSC[:, 0:zn], in_=RSC[:, 0:zn]))
            t5_ = DVE.tick()
            DVE.wait(t5_)
            for cc in range(KC):
                DVE.emit(nc.vector.tensor_tensor(out=Z[:, cc, 0:zn], in0=Z[:, cc, 0:zn], in1=MEAN[:, 0:zn], op=ALU.subtract))
                tq = DVE.tick()
                DVE.wait(tq)
                DVE.emit(nc.vector.tensor_tensor(out=Z[:, cc, 0:zn], in0=Z[:, cc, 0:zn], in1=RSC[:, 0:zn], op=ALU.mult))
                tq = DVE.tick()
                ACT.wait(tq)
                act(UB[:, cc, zlo:zlo + zn], Z[:, cc, 0:zn], AF.Silu, scale=VTs[:, VO["lng"] + cc:VO["lng"] + cc + 1],
                    bias=VTs[:, VO["lnb"] + cc:VO["lnb"] + cc + 1])
                zdone[cc] = ACT.tick()

        emit_conv(0)
        nmod = 32
        pending = None
        for u in range(len(units)):
            if u + 1 < len(units):
                emit_conv(u + 1)
            if pending is not None:
                pending()
                pending = None
            emit_post(u)
            if u % 2 == 1 and nmod < 48:
                pending = mod_block(0, nmod, pbase=6, defer=True)
                nmod += 1
        if pending is not None:
            pending()
        while nmod < 48:
            mod_block(0, nmod, pbase=6)
            nmod += 1
        barrier()
        checkpoint(4)

        tilesX = [(0, 512), (512, 512), (1024, 160)]
        proj_resid("pw2", lambda k, xlo, n_: UB[:, k, xlo:xlo + n_], tilesX, 32, "bpw2", 0)
        barrier()
        checkpoint(5)

        rms_modulate(0, 1, False)
        barrier()
        ffn(0, 0, tilesX)
        barrier()
        checkpoint(6)

        rms_modulate(1, 0, False)
        tb = barrier()
        checkpoint(7)
        ring["hold"] = tb
        POOL.wait(*tb)
        t_c = None
        for b in range(2):
            dma(POOL, KT[:, :, 1152 + 144 * b:1152 + 144 * b + 128], ckT[b].rearrange("g p k -> p g k"), ld2_sem)
            t_c = dma(POOL, VTM[:, 18 + 2 * b, :], cv[b], ld2_sem)
        SP.wait(*tb)
        t_b = dma(SP, BKVB[:, :], bkv_d.partition_broadcast(128), ld_sem)

        tilesK = [(0, 512), (512, 512), (1024, 160)]
        slot, bi = ring_use("kv_k")
        kfree = [None, None]
        kk = 0

        def ktcol(xlo):
            return xlo if xlo < 1152 else (1280 if xlo < 1168 else 1424)
        for g in range(4):
            for (xlo, ncols) in tilesK:
                bank = PS[kk % 2]
                PE.wait(kfree[kk % 2])
                for half in range(2):
                    for k in range(KC):
                        mm(bank[64 * half:64 * half + 64, 0:ncols], slot[:, k, 64 * g:64 * g + 64],
                           HB[:, k, 32 + xlo:32 + xlo + ncols], k == 0, k == KC - 1, tile_position=(0, 64 * half))
                tp = PE.tick()
                ACT.wait(tp)
                bk = VTs[:, VO["bkd"] + g:VO["bkd"] + g + 1]
                if xlo < 1024:
                    act(KT[:, g, xlo:xlo + ncols], bank[:, 0:ncols], AF.Identity, bias=bk)
                else:
                    act(KT[:, g, 1024:1152], bank[:, 0:128], AF.Identity, bias=bk)
                    act(KT[:, g, 1280:1296], bank[:, 128:144], AF.Identity, bias=bk)
                    act(KT[:, g, 1424:1440], bank[:, 144:160], AF.Identity, bias=bk)
                kfree[kk % 2] = ACT.tick()
                kk += 1
        DVE.wait(t_b)
        kv_t = [None, None]
        so = 0

        def tm_out(slot_, which, items):
            nonlocal so
            for (hcol, rows, dst) in items:
                bank = PS[2 + so % 2]
                PE.wait(tm_out.bfree[so % 2])
                for k in range(KC):
                    mm(bank[0:rows, 0:256], HB[:, k, hcol:hcol + rows], slot_[:, k, 0:256], k == 0, k == KC - 1)
                tp = PE.tick()
                DVE.wait(tp, kv_t[so % 2])
                DVE.emit(nc.vector.tensor_tensor(out=KVST[0:rows, so % 2, :], in0=bank[0:rows, 0:256],
                                                 in1=BKVB[0:rows, 256 * which:256 * which + 256], op=ALU.add))
                td = DVE.tick()
                tm_out.bfree[so % 2] = td
                SP.wait(td)
                kv_t[so % 2] = dma(SP, dst, KVST[0:rows, so % 2, :], kvst_sem[so % 2])
                so += 1
        tm_out.bfree = [None, None]
        tm_out(slot, 0, [(32 + 1024, 64, kp_o[0:64, :]), (32 + 1088, 64, kp_o[64:128, :]),
                         (32 + 1152, 16, ks_o[0, 112:128, :]), (32 + 1168, 16, ks_o[1, 112:128, :])])
        ring_release(bi)
        slot, bi = ring_use("kv_v")
        vfree = [None, None]
        vchunks = [(32 + 64 * i, 128 if i < 17 else 64, i) for i in range(18)] + [(32 + 1152, 16, 19), (32 + 1168, 16, 21)]
        for vi, (hcol, rows, vs) in enumerate(vchunks):
            bank = PS[4 + vi % 2]
            PE.wait(vfree[vi % 2])
            for k in range(KC):
                mm(bank[0:rows, 0:256], HB[:, k, hcol:hcol + rows], slot[:, k, 0:256], k == 0, k == KC - 1)
            tp = PE.tick()
            DVE.wait(tp)
            DVE.emit(nc.vector.tensor_tensor(out=VTM[0:rows, vs, :], in0=bank[0:rows, 0:256], in1=BKVB[0:rows, 256:512],
                                             op=ALU.add))
            if vs < 2:
                mr = 128 if vs == 0 else 64
                tq = DVE.tick()
                DVE.wait(tq)
                DVE.emit(nc.vector.tensor_scalar_mul(out=VTM[0:mr, vs, :], in0=VTM[0:mr, vs, :], scalar1=FLAG[0:mr, 0:1]))
            vfree[vi % 2] = DVE.tick()
        tm_out(slot, 1, [(32 + 1024, 64, vp_o[0:64, :]), (32 + 1088, 64, vp_o[64:128, :]),
                         (32 + 1152, 16, vs_o[0, 112:128, :]), (32 + 1168, 16, vs_o[1, 112:128, :])])
        ring_release(bi)
        barrier()
        checkpoint(8)
        tilesQ = [(128, 352), (480, 352), (832, 352)]
        qfree = [None, None]
        kk = 0
        for b in range(8):
            slot, bi = ring_use(f"q_{b}")
            for h in range(2):
                n = 2 * b + h
                for (xlo, ncols) in tilesQ:
                    bank = PS[kk % 2]
                    PE.wait(qfree[kk % 2])
                    for k in range(KC):
                        mm(bank[:, 0:ncols], slot[:, k, 128 * h:128 * h + 128], HB[:, k, 32 + xlo:32 + xlo + ncols],
                           k == 0, k == KC - 1)
                    tp = PE.tick()
                    ACT.wait(tp)
                    act(QT[:, n, xlo - 128:xlo - 128 + ncols], bank[:, 0:ncols], AF.Identity,
                        bias=VTs[:, VO["bq"] + n:VO["bq"] + n + 1])
                    qfree[kk % 2] = ACT.tick()
                    kk += 1
            ring_release(bi)
        barrier()

        checkpoint(9)
        SP.wait(PE.tick(), ACT.tick(), DVE.tick())
        t1 = dma(SP, OHS[0:32, :], ohrel, ld_sem)
        t2 = dma(SP, TBT[0:32, :], tableT, ld_sem)
        t3 = dma(SP, ES64[:, :], sinks_d.partition_broadcast(128), ld_sem)
        ACT.wait(t3)
        act(ES64[:, :], ES64[:, :], AF.Exp)
        ta = ACT.tick()
        DVE.wait(ta)
        PE.wait(t3, t_c)

        qblocks = []
        for qc in range(16):
            m01 = FLAGB if qc == 0 else (FLAGH if qc == 1 else ONESB)
            qblocks.append((64 * qc, 64, [(64 * qc, qc, 128, m01), (64 * qc + 128, qc + 2, 64, ONESB)]))
        for b in range(2):
            qblocks.append((1024 + 16 * b, 16, [(1152 + 144 * b, 18 + 2 * b, 128, ONESB),
                                               (1280 + 144 * b, 19 + 2 * b, 16, ONESB)]))
        barrier()
        for kt, (j0, nj) in enumerate([(0, 128), (128, 64)]):
            for q in range(64):
                mm(PS[q // 16][0:nj, 32 * (q % 16):32 * (q % 16) + 32], OHS[0:32, j0 + 63 - q:j0 + 63 - q + nj],
                   TBT[0:32, 0:32], True, True)
            tp = PE.tick()
            DVE.wait(tp)
            for g in range(4):
                for bq in range(4):
                    DVE.emit(nc.vector.tensor_copy(
                        out=BIAS[0:nj, g, kt, :].rearrange("j (par pair q) -> j par pair q", par=2, pair=4)[:, :, :, 16 * bq:16 * bq + 16],
                        in_=PS[bq][0:nj, :].rearrange("j (q h) -> j q h", h=32)[:, :, 8 * g:8 * g + 8]
                            .rearrange("j q (pair par) -> j par pair q", par=2)))
            td = DVE.tick()
            PE.wait(td)
        barrier()
        sfree = [None, None, None]
        ofree = [None, None]
        ssfree = [None, None, None]
        ptfree = {}
        exp_t = {}
        gi = 0
        for g in range(4):
            DVE.wait(PE.tick())
            for par in range(2):
                for b3 in range(3):
                    DVE.emit(nc.vector.tensor_copy(
                        out=PT[64:65, 3 * par + b3, 256:512].rearrange("o (a q) -> o a q", q=64),
                        in_=ES64[64:65, 8 * g + par:8 * g + 8:2].unsqueeze(2).to_broadcast([1, 4, 64])))
            tsk = DVE.tick()
            PE.wait(tsk)
            checkpoint(50)
            its = [(qcol, nq, keys, par) for (qcol, nq, keys) in qblocks for par in range(2)]

            st = {}

            def S_mm(i):
                qcol, nq, keys, par = its[i]
                b3 = (gi + i) % 3
                pp = slice(64 * par, 64 * par + 64)
                bS = PS[b3]
                PE.wait(sfree[b3])
                for kt, (kcol, vs, nk, msk) in enumerate(keys):
                    mm(bS[0:nk, 256 * kt:256 * kt + 4 * nq], KT[pp, g, kcol:kcol + nk], QT[pp, 4 * g:4 * g + 4, qcol:qcol + nq],
                       True, True, tile_position=(64 * par, 0))
                st[("s", i)] = PE.tick()

            def S_bias(i):
                qcol, nq, keys, par = its[i]
                b3 = (gi + i) % 3
                bS = PS[b3]
                DVE.wait(st[("s", i)], ssfree[b3])
                for kt, (kcol, vs, nk, msk) in enumerate(keys):
                    DVE.emit(nc.vector.scalar_tensor_tensor(
                        out=SS[0:nk, b3, 256 * kt:256 * kt + 4 * nq].rearrange("j (a q) -> j a q", q=nq),
                        in0=bS[0:nk, 256 * kt:256 * kt + 4 * nq].rearrange("j (a q) -> j a q", q=nq), scalar=0.125,
                        in1=BIAS[0:nk, g, kt, 256 * par:256 * par + 256].rearrange("j (a q) -> j a q", q=64)[:, :, 0:nq],
                        op0=ALU.mult, op1=ALU.add))
                td = DVE.tick()
                sfree[b3] = td
                st[("b", i)] = td

            def S_exp(i):
                qcol, nq, keys, par = its[i]
                b3 = (gi + i) % 3
                pb = 3 * par + b3
                ACT.wait(st[("b", i)], ptfree.get(pb))
                for kt, (kcol, vs, nk, msk) in enumerate(keys):
                    act(PT[0:nk, pb, 256 * kt:256 * kt + 4 * nq], SS[0:nk, b3, 256 * kt:256 * kt + 4 * nq], AF.Exp)
                ta = ACT.tick()
                ssfree[b3] = ta
                exp_t[i] = ta

            def PV_mm(i):
                qcol, nq, keys, par = its[i]
                b3 = (gi + i) % 3
                pb = 3 * par + b3
                b2 = (gi + i) % 2
                pp = slice(64 * par, 64 * par + 64)
                bO = PS[3 + b2]
                PE.wait(exp_t[i], ofree[b2])
                for kt, (kcol, vs, nk, msk) in enumerate(keys):
                    mm(bO[pp, 0:4 * nq], VTM[0:nk, vs, 64 * g:64 * g + 64], PT[0:nk, pb, 256 * kt:256 * kt + 4 * nq],
                       kt == 0, kt == 1, tile_position=(0, 64 * par))
                (kcol, vs, nk, msk) = keys[0]
                mm(bO[pp, 256:256 + 4 * nq], msk[0:nk, 0:64], PT[0:nk, pb, 0:4 * nq], True, False, tile_position=(0, 64 * par))
                (kcol, vs, nk, msk) = keys[1]
                if nq == 64:
                    mm(bO[pp, 256:512], ONESB[0:65, 0:64], PT[0:65, pb, 256:512], False, True, tile_position=(0, 64 * par))
                else:
                    mm(bO[pp, 256:256 + 4 * nq], ONESB[0:nk, 0:64], PT[0:nk, pb, 256:256 + 4 * nq], False, True,
                       tile_position=(0, 64 * par))
                tp = PE.tick()
                ptfree[pb] = tp
                st[("pv", i)] = tp

            def PV_ln(i):
                qcol, nq, keys, par = its[i]
                b2 = (gi + i) % 2
                pp = slice(64 * par, 64 * par + 64)
                bO = PS[3 + b2]
                tp = st[("pv", i)]
                if nq == 64:
                    ACT.wait(tp, ofree[b2])
                    act(RDEN[pp, b2, 0:4 * nq], bO[pp, 256:256 + 4 * nq], AF.Ln)
                else:
                    DVE.wait(tp, ofree[b2])
                    DVE.emit(nc.vector.tensor_tensor(
                        out=RDEN[pp, b2, 0:4 * nq].rearrange("d (a q) -> d a q", q=nq),
                        in0=bO[pp, 256:256 + 4 * nq].rearrange("d (a q) -> d a q", q=nq),
                        in1=ES64[pp, 8 * g + par:8 * g + 8:2].unsqueeze(2).to_broadcast([64, 4, nq]), op=ALU.add))
                    tdd = DVE.tick()
                    ACT.wait(tdd)
                    act(RDEN[pp, b2, 0:4 * nq], RDEN[pp, b2, 0:4 * nq], AF.Ln)
                ta = ACT.tick()
                ACT.wait(ta)
                act(RDEN[pp, b2, 0:4 * nq], RDEN[pp, b2, 0:4 * nq], AF.Exp, scale=-1.0)
                st[("ln", i)] = ACT.tick()

            def PV_mult(i):
                qcol, nq, keys, par = its[i]
                b2 = (gi + i) % 2
                pp = slice(64 * par, 64 * par + 64)
                bO = PS[3 + b2]
                DVE.wait(st[("ln", i)], st[("pv", i)])
                DVE.emit(nc.vector.tensor_tensor(
                    out=QT[pp, 4 * g:4 * g + 4, qcol:qcol + nq],
                    in0=bO[pp, 0:4 * nq].rearrange("d (a q) -> d a q", q=nq),
                    in1=RDEN[pp, b2, 0:4 * nq].rearrange("d (a q) -> d a q", q=nq), op=ALU.mult))
                ofree[b2] = DVE.tick()

            n_it = len(its)
            for i in range(2):
                S_mm(i)
                S_bias(i)
                S_exp(i)
            for i in range(n_it):
                PV_mm(i)
                nxt = i + 2 < n_it
                if nxt:
                    S_mm(i + 2)
                    S_bias(i + 2)
                PV_ln(i)
                PV_mult(i)
                if nxt:
                    S_exp(i + 2)
            gi += n_it
        barrier()

        checkpoint(10)
        proj_resid("wo", lambda k, xlo, n_: QT[:, k, xlo - 128:xlo - 128 + n_], tilesQ, 32, "bo", 1)
        barrier()

        checkpoint(11)
        rms_modulate(1, 1, False)
        barrier()
        DVE.wait(kv_t[0], kv_t[1])
        ffn(1, 128, tilesQ)
        barrier()

        checkpoint(12)
        tilesF = [(128, 512), (640, 512), (1152, 32)]
        banks = [PS[5], PS[6], PS[7]]
        sqf = [None, None]
        for c in range(KC):
            ACT.wait(sqf[c % 2])
            act(SQ[:, c % 2, 32:NH], X[:, c, :], AF.Square)
            ta = ACT.tick()
            PE.wait(ta)
            for ti, (a, n_) in enumerate(tilesF):
                mm(banks[ti][:, 0:n_], ONESB[:], SQ[:, c % 2, 32 + a:32 + a + n_], c == 0, c == KC - 1)
            sqf[c % 2] = PE.tick()
        tp = PE.tick()
        ACT.wait(tp)
        for ti, (a, n_) in enumerate(tilesF):
            act(RSTD[:, 32 + a:32 + a + n_], banks[ti][:, 0:n_], AF.Sqrt, bias=EPSC[:, 0:1], scale=1.0 / D)
        ta = ACT.tick()
        DVE.wait(ta)
        DVE.emit(nc.vector.reciprocal(out=RSTD[:, 160:NH], in_=RSTD[:, 160:NH]))
        td = DVE.tick()
        barrier()
        YS = carve(SCR, 11776, (128, 2, NQ), F32)
        y_t = [None, None]
        for c in range(KC):
            DVE.wait(y_t[c % 2])
            DVE.emit(nc.vector.scalar_tensor_tensor(out=YS[:, c % 2, :], in0=X[:, c, 128:NX],
                                                    scalar=VTs[:, VO["nout"] + c:VO["nout"] + c + 1],
                                                    in1=RSTD[:, 160:NH], op0=ALU.mult, op1=ALU.mult))
            td = DVE.tick()
            SP.wait(td)
            y_t[c % 2] = dma(SP, yT[128 * c:128 * c + 128, :], YS[:, c % 2, :], yst_sem[c % 2])
        SP.wait(y_t[0], y_t[1], uf_t[0], uf_t[1], kv_t[0], kv_t[1], (st_sem, 16 * cnt[id(st_sem)]))
    return nc


_NC_CACHE = {}


def _host_inputs(inp):
    f = lambda a: np.ascontiguousarray(np.asarray(a, dtype=np.float32))
    xp, xs = f(inp["x_prompt"]), f(inp["x_sample"])
    cp, cs = f(inp["c_prompt"]), f(inp["c_sample"])
    stc, ckc, cvc = f(inp["state_conv"])[0], f(inp["cache_win_k"])[0], f(inp["cache_win_v"])[0]
    w_pw1 = f(inp["w_pw1"])[0]
    wpw1 = np.ascontiguousarray(np.stack([w_pw1[:, :D].reshape(D, 16, 128), w_pw1[:, D:].reshape(D, 16, 128)], axis=2).reshape(D, 2 * D))
    w_gu = f(inp["w_gu"])
    wgu = np.ascontiguousarray(np.stack([w_gu[:, :, :DFF].reshape(2, D, NFF, 128), w_gu[:, :, DFF:].reshape(2, D, NFF, 128)], axis=3).reshape(2, D, 2 * DFF))
    b_qkv = f(inp["b_qkv"])[0]
    b_pw1 = f(inp["b_pw1"])[0]
    bkd = np.stack([np.concatenate([b_qkv[2048 + 64 * g:2048 + 64 * g + 64]] * 2) for g in range(4)])
    rows = [f(inp["b_mod"]).reshape(192, 128), f(inp["norm_mix"]).reshape(32, 128), f(inp["norm_ffn"]).reshape(32, 128),
            b_pw1[:D].reshape(16, 128), b_pw1[D:].reshape(16, 128), f(inp["b_dw"]).reshape(16, 128),
            f(inp["conv_ln_g"]).reshape(16, 128), f(inp["conv_ln_b"]).reshape(16, 128), f(inp["b_pw2"]).reshape(16, 128),
            b_qkv[:2048].reshape(16, 128), bkd, f(inp["b_o"]).reshape(16, 128), f(inp["norm_out"]).reshape(16, 128)]
    vecs = np.concatenate(rows, axis=0)
    assert vecs.shape[0] == NVEC
    vecsT = np.ascontiguousarray(vecs.T)
    wdwT = np.ascontiguousarray(f(inp["w_dw"])[0].reshape(31, 16, 128).transpose(2, 0, 1).reshape(128, 31 * 16))
    rel = np.arange(255) - 191
    bk = t5_bucket_np(rel)
    ohrel = np.zeros((32, 255), np.float32)
    ohrel[bk, np.arange(255)] = 1.0
    def tile_w(W, nl):
        N = W.shape[1]
        return np.ascontiguousarray(W.reshape(nl, KC, 128, N // 256, 256).transpose(0, 3, 2, 1, 4).reshape(nl * (N // 256), 128, KC, 256))

    w_dn = f(inp["w_down"])
    wdn_t = np.zeros((2, 6, 8, 128, GRP, 256), np.float32)
    for li in range(2):
        for G in range(6):
            j0, j1 = G * GRP, min(NFF, (G + 1) * GRP)
            blk = w_dn[li, j0 * 128:j1 * 128, :].reshape(j1 - j0, 128, 8, 256)
            wdn_t[li, G, :, :, 0:j1 - j0, :] = blk.transpose(2, 1, 0, 3)
    wdn_t = wdn_t.reshape(96, 128, GRP, 256)
    shared = {
        "vecsT": vecsT, "wdwT": wdwT, "bkv": np.ascontiguousarray(b_qkv[2048:].reshape(1, 512)), "ohrel": ohrel,
        "tableT": np.ascontiguousarray(f(inp["rel_bias_table"]).T), "sinks": f(inp["attn_sinks"]).reshape(1, 32),
        "identf": np.eye(128, dtype=np.float32),
        "wmod": tile_w(f(inp["w_mod"]).reshape(2 * D, 6 * D), 2),
        "wpw1": tile_w(wpw1, 1), "wpw2": tile_w(f(inp["w_pw2"])[0], 1), "wqkv": tile_w(f(inp["w_qkv"])[0], 1),
        "wo": tile_w(f(inp["w_o"])[0], 1), "wgu": tile_w(wgu.reshape(2 * D, 2 * DFF), 2), "wdn": wdn_t,
    }
    maps = []
    for i in range(NCORES):
        b, seg = i // 4, i % 4
        t0 = seg * 1024
        xin = np.zeros((NH, D), np.float32)
        lo = t0 - 160
        if lo >= 0:
            xin[0:160] = xp[b, lo:t0]
        xin[160:1184] = xp[b, t0:t0 + 1024]
        xin[1184:1200] = xs[2 * i]
        xin[1200:1216] = xs[2 * i + 1]
        cvec = np.stack([cp[b], cs[2 * i], cs[2 * i + 1]])
        m = dict(shared)
        m["xT"] = np.ascontiguousarray(xin.T)
        m["cT"] = np.ascontiguousarray(cvec.reshape(3, 16, 128).transpose(2, 1, 0))
        m["flag"] = np.full((128, 1), 1.0 if seg > 0 else 0.0, np.float32)
        m["stT"] = np.ascontiguousarray(stc[2 * i:2 * i + 2].transpose(0, 2, 1))
        kk = ckc[2 * i:2 * i + 2]
        kT = kk.transpose(0, 2, 3, 1)
        m["ckT"] = np.ascontiguousarray(np.concatenate([kT, kT], axis=2))
        m["ck"] = np.ascontiguousarray(kk.reshape(2, 128, 256))
        m["cv"] = np.ascontiguousarray(cvc[2 * i:2 * i + 2].reshape(2, 128, 256))
        maps.append(m)
    return maps


def kernel(**inputs):
    if "nc" not in _NC_CACHE:
        _NC_CACHE["nc"] = build_nc()
    nc = _NC_CACHE["nc"]
    maps = _host_inputs(inputs)
    res = run_bass_kernel_spmd(nc, maps, core_ids=list(range(NCORES)))
    R = res.results
    stc = np.asarray(inputs["state_conv"], dtype=np.float32)
    y_prompt = np.zeros((2, SEQ, D), np.float32)
    y_sample = np.zeros((16, 16, D), np.float32)
    conv_prompt = np.zeros((1, 2, 30, D), np.float32)
    wk_p = np.zeros((1, 2, 128, 4, 64), np.float32)
    wv_p = np.zeros((1, 2, 128, 4, 64), np.float32)
    conv_sample = np.zeros((1, 16, 30, D), np.float32)
    wk_s = np.zeros((1, 16, 128, 4, 64), np.float32)
    wv_s = np.zeros((1, 16, 128, 4, 64), np.float32)
    for i in range(NCORES):
        b, seg = i // 4, i % 4
        r = R[i]
        yT = np.asarray(r["yT"])
        y_prompt[b, seg * 1024:(seg + 1) * 1024] = yT[:, 0:1024].T
        y_sample[2 * i] = yT[:, 1024:1040].T
        y_sample[2 * i + 1] = yT[:, 1040:1056].T
        uc = np.asarray(r["ucT"])
        hd = np.asarray(r["cs_head"])
        for j in range(2):
            conv_sample[0, 2 * i + j, 0:14] = hd[j].T
            conv_sample[0, 2 * i + j, 14:30] = uc[:, 30 + 16 * j:46 + 16 * j].T
            wk_s[0, 2 * i + j] = np.asarray(r["ks_o"])[j].reshape(128, 4, 64)
            wv_s[0, 2 * i + j] = np.asarray(r["vs_o"])[j].reshape(128, 4, 64)
        if seg == 3:
            conv_prompt[0, b] = uc[:, 0:30].T
            wk_p[0, b] = np.asarray(r["kp_o"]).reshape(128, 4, 64)
            wv_p[0, b] = np.asarray(r["vp_o"]).reshape(128, 4, 64)
    return (y_prompt, y_sample, conv_prompt, wk_p, wv_p, conv_sample, wk_s, wv_s)
```

```python
import numpy as np
import concourse.bass as bass
import concourse.mybir as mybir
from concourse.bass_utils import run_bass_kernel_spmd
from contextlib import ExitStack, suppress

F32 = mybir.dt.float32
BF16 = mybir.dt.bfloat16
U8 = mybir.dt.uint8
AF = mybir.ActivationFunctionType
ALU = mybir.AluOpType

D = 2048
KC = 16
DFF = 5632
NFF = 44
SEQ = 4096
NCORES = 8
EPS = 1e-6
NX = 1184
NH = 1216
NU = 1276
NQ = 1056
NKT = 1440
GRP = 8

VO = {}
_o = 0
for _n, _c in [("bmod", 192), ("nmix", 32), ("nffn", 32), ("bpa", 16), ("bpg", 16), ("bdw", 16), ("lng", 16),
               ("lnb", 16), ("bpw2", 16), ("bq", 16), ("bkd", 4), ("bo", 16), ("nout", 16)]:
    VO[_n] = _o
    _o += _c
NVEC = _o


def t5_bucket_np(rel):
    nb = 16
    max_exact = 8
    ret = np.where(rel > 0, nb, 0)
    n = np.abs(rel)
    nf = np.maximum(n, 1).astype(np.float32)
    large = max_exact + (np.log(nf / max_exact) / np.float32(np.log(128 / max_exact)) * (nb - max_exact)).astype(np.int32)
    large = np.minimum(large, nb - 1)
    return ret + np.where(n < max_exact, n, large)


STOP = None


class _StopBuild(Exception):
    pass


class Eng:
    def __init__(self, nc, eng, name, es):
        self.e = eng
        self.sem = es.enter_context(nc.semaphore(name))
        self.n = 0
        self.last = None
        self.marked = True
        self.seen = {}

    def emit(self, inst):
        self.last = inst
        self.marked = False
        return inst

    def tick(self):
        if not self.marked:
            self.n += 1
            self.last.then_inc(self.sem, 1)
            self.marked = True
        return (self.sem, self.n)

    def wait(self, *ts):
        for t in ts:
            if t is None:
                continue
            sem, v = t
            if v <= 0:
                continue
            k = id(sem)
            if self.seen.get(k, 0) >= v:
                continue
            self.e.wait_ge(sem, v)
            self.seen[k] = v


def build_nc():
    nc = bass.Bass("TRN2", target_bir_lowering=False)
    dt_in = lambda n, s: nc.dram_tensor(n, list(s), F32, kind="ExternalInput").ap()
    dt_out = lambda n, s: nc.dram_tensor(n, list(s), F32, kind="ExternalOutput").ap()
    xT = dt_in("xT", (D, NH))
    cT = dt_in("cT", (128, KC, 3))
    flag_d = dt_in("flag", (128, 1))
    stT = dt_in("stT", (2, D, 30))
    ckT = dt_in("ckT", (2, 4, 128, 128))
    ck = dt_in("ck", (2, 128, 256))
    cv = dt_in("cv", (2, 128, 256))
    vecsT = dt_in("vecsT", (128, NVEC))
    wdwT = dt_in("wdwT", (128, 31 * 16))
    bkv_d = dt_in("bkv", (1, 512))
    ohrel = dt_in("ohrel", (32, 255))
    tableT = dt_in("tableT", (32, 32))
    sinks_d = dt_in("sinks", (1, 32))
    identf = dt_in("identf", (128, 128))
    wmod = dt_in("wmod", (96, 128, KC, 256))
    wpw1 = dt_in("wpw1", (16, 128, KC, 256))
    wpw2 = dt_in("wpw2", (8, 128, KC, 256))
    wqkv = dt_in("wqkv", (10, 128, KC, 256))
    wo = dt_in("wo", (8, 128, KC, 256))
    wgu = dt_in("wgu", (88, 128, KC, 256))
    wdn = dt_in("wdn", (96, 128, GRP, 256))

    yT = dt_out("yT", (D, NQ))
    ucT = dt_out("ucT", (D, 62))
    cs_head = dt_out("cs_head", (2, D, 14))
    kp_o = dt_out("kp_o", (128, 256))
    vp_o = dt_out("vp_o", (128, 256))
    ks_o = dt_out("ks_o", (2, 128, 256))
    vs_o = dt_out("vs_o", (2, 128, 256))

    es = ExitStack()
    with es, suppress(_StopBuild):
        sb = lambda n, s, d: es.enter_context(nc.sbuf_tensor(n, list(s), d))
        X = sb("X", (128, KC, NX), F32)
        HBraw = sb("HBraw", (128, KC * NH * 2), U8)
        UBraw = sb("UBraw", (128, KC * NU * 2), U8)
        RING = sb("RING", (128, 3, KC, 256), BF16)
        SCR = sb("SCR", (128, 23808), U8)
        VTs = sb("VTs", (128, NVEC), F32)
        MOD = sb("MOD", (128, 2, 96, 3), F32)
        AM = sb("AM", (128, KC, 3), F32)
        G1B = sb("G1B", (128, KC, 3), F32)
        IDB = sb("IDB", (128, 128), BF16)
        IDF = sb("IDF", (128, 128), F32)
        ONESB = sb("ONESB", (128, 128), BF16)
        FLAG = sb("FLAG", (128, 1), F32)
        FLAGB = sb("FLAGB", (128, 64), BF16)
        CTs = sb("CTs", (128, KC, 3), F32)
        SCT = sb("SCT", (128, KC, 3), BF16)
        UFs = sb("UFs", (128, 2, 62), F32)
        ESK = sb("ESK", (1, 2, 32), BF16)
        ESF = sb("ESF", (1, 3, 32), F32)
        ES64 = sb("ES64", (128, 32), F32)
        FLAGH = sb("FLAGH", (128, 64), BF16)
        ONE1 = sb("ONE1", (1, 64), BF16)
        EPSC = sb("EPSC", (128, 1), F32)
        PS = [es.enter_context(nc.psum_tensor(f"ps{i}", [128, 512], F32)) for i in range(8)]

        def carve(raw, off, shape, dtype):
            n = int(np.prod(shape[1:]))
            bs = 2 if dtype == BF16 else 4
            v = raw[:, off:off + n * bs].bitcast(dtype)
            if len(shape) == 3:
                v = v.rearrange("p (a b) -> p a b", b=shape[2])
            elif len(shape) == 4:
                v = v.rearrange("p (a b c) -> p a b c", b=shape[2], c=shape[3])
            return v

        HB = carve(HBraw, 0, (128, KC, NH), BF16)
        UB = carve(UBraw, 0, (128, KC, NU), BF16)
        QT = carve(UBraw, 0, (128, KC, NQ), BF16)
        ACTR = carve(UBraw, 0, (128, 16, NX), BF16)
        KVST = carve(UBraw, 33792, (128, 2, 256), F32)
        BKVB = carve(UBraw, 33792 + 2048, (128, 512), F32)
        XP = carve(SCR, 0, (128, KC, 32), F32)
        RSTD = carve(SCR, 2048, (128, NH), F32)
        T1 = carve(SCR, 6912, (128, NH), F32)
        T1B = carve(SCR, 16640, (128, NH), F32)
        SQ = carve(SCR, 11776, (128, 2, NH), BF16)
        SG = carve(SCR, 16640, (128, 2, 512), F32)
        TMPE = carve(SCR, 20736, (128, 512), F32)
        KT = carve(SCR, 0, (128, 4, NKT), BF16)
        VTM = carve(SCR, 11520, (128, 22, 256), BF16)
        YST = carve(SCR, 0, (128, 2, NQ), F32)
        Z = carve(HBraw, 0, (128, KC, 416), F32)
        ZB = carve(HBraw, 26624, (128, 2, 416), BF16)
        ZSQ = carve(HBraw, 28288, (128, 2, 416), BF16)
        MEAN = carve(SCR, 0, (128, 416), F32)
        MSQ = carve(SCR, 1664, (128, 416), F32)
        RSC = carve(SCR, 3328, (128, 416), F32)
        WDW = carve(SCR, 4992, (128, 31, 16), F32)
        DG = carve(SCR, 7168, (128, 2, 31, 128), BF16)
        BIAS = carve(HBraw, 0, (128, 4, 2, 512), F32)
        PT = carve(HBraw, 16384, (128, 6, 512), BF16)
        SS = carve(HBraw, 22528, (128, 3, 512), F32)
        RDEN = carve(HBraw, 28672, (128, 2, 256), F32)
        OHS = carve(HBraw, 30720, (128, 255), F32)
        TBT = carve(HBraw, 31744, (128, 32), F32)

        PE = Eng(nc, nc.tensor, "s_pe", es)
        ACT = Eng(nc, nc.scalar, "s_act", es)
        DVE = Eng(nc, nc.vector, "s_dve", es)
        SP = Eng(nc, nc.sync, "s_sp", es)
        POOL = Eng(nc, nc.gpsimd, "s_pool", es)
        ld_sem = es.enter_context(nc.semaphore("ld"))
        ld2_sem = es.enter_context(nc.semaphore("ld2"))
        st_sem = es.enter_context(nc.semaphore("st"))
        kvst_sem = [es.enter_context(nc.semaphore(f"kvst{i}")) for i in range(2)]
        uf_sem = [es.enter_context(nc.semaphore(f"uf{i}")) for i in range(2)]
        yst_sem = [es.enter_context(nc.semaphore(f"yst{i}")) for i in range(2)]
        slot_sem = [es.enter_context(nc.semaphore(f"slot{i}")) for i in range(3)]
        cnt = {}

        def dma(engw, out, in_, sem):
            engw.e.dma_start(out=out, in_=in_).then_inc(sem, 16)
            cnt[id(sem)] = cnt.get(id(sem), 0) + 1
            return (sem, 16 * cnt[id(sem)])

        def mm(out, lhsT, rhs, start, stop, **kw):
            return PE.emit(nc.tensor.matmul(out, lhsT=lhsT, rhs=rhs, start=start, stop=stop, **kw))

        def act(out, in_, func, **kw):
            return ACT.emit(nc.scalar.activation(out=out, in_=in_, func=func, **kw))

        def barrier():
            ts = [PE.tick(), ACT.tick(), DVE.tick()]
            for e in (PE, ACT, DVE):
                e.wait(*ts)
            return ts

        def checkpoint(stage):
            if STOP == stage:
                barrier()
                SP.wait(PE.tick(), ACT.tick(), DVE.tick())
                POOL.wait(PE.tick())
                for sem in [st_sem, ld_sem, ld2_sem] + uf_sem + kvst_sem + yst_sem + slot_sem:
                    if cnt.get(id(sem)):
                        SP.wait((sem, 16 * cnt[id(sem)]))
                raise _StopBuild()

        plan = []

        def wblock(name, Wt, blk, nk):
            plan.append((name, nk, 256, Wt[blk, :, 0:nk, :]))

        for b in range(16):
            wblock(f"mod0_{b}", wmod, b, KC)
        for c in range(16):
            wblock(f"pw1_{c}", wpw1, c, KC)
            wblock(f"mod0_{16 + c}", wmod, 16 + c, KC)
        for b in range(32, 48):
            wblock(f"mod0_{b}", wmod, b, KC)
        for b in range(8):
            wblock(f"pw2_{b}", wpw2, b, KC)
        for li in range(2):
            if li == 1:
                wblock("kv_k", wqkv, 8, KC)
                wblock("kv_v", wqkv, 9, KC)
                for b in range(8):
                    wblock(f"q_{b}", wqkv, b, KC)
                for b in range(8):
                    wblock(f"wo_{b}", wo, b, KC)
            ngrp = (NFF + GRP - 1) // GRP
            for G in range(ngrp):
                j0, j1 = G * GRP, min(NFF, (G + 1) * GRP)
                for j in range(j0, j1):
                    wblock(f"gu{li}_{j}", wgu, 44 * li + j, KC)
                    if li == 0:
                        wblock(f"mod1_{j}", wmod, 48 + j, KC)
                        if j == NFF - 1:
                            for b in range(NFF, 48):
                                wblock(f"mod1_{b}", wmod, 48 + b, KC)
                for b in range(8):
                    wblock(f"dn{li}_{G}_{b}", wdn, (li * 6 + G) * 8 + b, j1 - j0)

        ring = {"next_pf": 0, "next_use": 0, "free": [None, None, None], "load": {}, "hold": None}

        def ring_prefetch(i):
            name, nk, ncols, src = plan[i]
            s = i % 3
            POOL.wait(ring["free"][s])
            if ring["hold"] is not None and name == "kv_k":
                POOL.wait(*ring["hold"])
            ring["load"][i] = dma(POOL, RING[:, s, 0:nk, 0:ncols], src, slot_sem[s])

        def ring_use(name):
            i = ring["next_use"]
            assert plan[i][0] == name, (plan[i][0], name)
            while ring["next_pf"] <= min(i + 2, len(plan) - 1):
                ring_prefetch(ring["next_pf"])
                ring["next_pf"] += 1
            PE.wait(ring["load"][i])
            ring["next_use"] += 1
            return RING[:, i % 3], i

        def ring_release(i):
            ring["free"][i % 3] = PE.tick()

        xv = xT.rearrange("(c p) t -> p c t", p=128)
        for q in range(4):
            dma(SP, X[:, 4 * q:4 * q + 4, :], xv[:, 4 * q:4 * q + 4, 32:NH], ld_sem)
        dma(SP, XP, xv[:, :, 0:32], ld_sem)
        dma(SP, CTs[:], cT, ld_sem)
        dma(SP, FLAG[:], flag_d, ld_sem)
        dma(SP, VTs[:], vecsT, ld_sem)
        dma(SP, IDF[:], identf, ld_sem)
        t_ld = (ld_sem, 16 * cnt[id(ld_sem)])
        stv = stT.rearrange("b (c p) w -> b p c w", p=128)
        for b in range(2):
            dma(SP, cs_head[b], stT[b, :, 16:30], st_sem)
        for b in range(2):
            dma(SP, ks_o[b, 0:112, :], ck[b, 16:128, :], st_sem)
            dma(SP, vs_o[b, 0:112, :], cv[b, 16:128, :], st_sem)

        DVE.wait(t_ld)
        ACT.wait(t_ld)
        DVE.emit(nc.vector.memset(ONESB[:], 1.0))
        DVE.emit(nc.vector.memset(ONE1[:], 1.0))
        DVE.emit(nc.vector.memset(EPSC[:], EPS))
        DVE.emit(nc.vector.tensor_copy(out=IDB[:], in_=IDF[:]))
        DVE.emit(nc.vector.tensor_copy(out=FLAGB[:], in_=FLAG[:, 0:1].to_broadcast([128, 64])))
        DVE.emit(nc.vector.memset(FLAGH[64:128, :], 1.0))
        DVE.emit(nc.vector.tensor_copy(out=FLAGH[0:64, :], in_=FLAG[0:64, 0:1].to_broadcast([64, 64])))
        act(SCT[:], CTs[:], AF.Silu)
        barrier()
        checkpoint(0)

        def mod_block(li, b, pbase=4, defer=False):
            slot, bi = ring_use(f"mod{li}_{b}")
            bank = PS[pbase + (mod_block.k % 2)]
            PE.wait(mod_block.free[mod_block.k % 2])
            for h in range(2):
                for k in range(KC):
                    mm(bank[:, 4 * h:4 * h + 3], slot[:, k, 128 * h:128 * h + 128], SCT[:, k, :], k == 0, k == KC - 1)
            ring_release(bi)
            tp = PE.tick()
            kslot = mod_block.k % 2
            mod_block.k += 1

            def evac():
                DVE.wait(tp)
                for h in range(2):
                    n = 2 * b + h
                    DVE.emit(nc.vector.tensor_scalar_add(out=MOD[:, li, n, :], in0=bank[:, 4 * h:4 * h + 3],
                                                         scalar1=VTs[:, VO["bmod"] + 96 * li + n:VO["bmod"] + 96 * li + n + 1]))
                mod_block.free[kslot] = DVE.tick()
            if defer:
                return evac
            evac()
            return None
        mod_block.k = 0
        mod_block.free = [None, None]

        SEGH = [(0, 1184, 0), (1184, 1200, 1), (1200, 1216, 2)]

        def segs(lo, hi, off=0):
            out = []
            for (a, b_, bi) in [(0, 1152, 0), (1152, 1168, 1), (1168, 1184, 2)]:
                a2, b2 = max(lo, a + off if False else a), min(hi, b_)
                if a2 < b2:
                    out.append((a2, b2, bi))
            return out

        def rms_modulate(li, which, with_pre, part="all", banks=None):
            nv = VO["nmix"] if which == 0 else VO["nffn"]
            sh0, sc0 = (0, 16) if which == 0 else (48, 64)
            c0 = 0 if with_pre else 32
            tiles = [(c0, 512), (512, 1024), (1024, NH)]
            banks = banks or [PS[5], PS[6], PS[7]]
            for c in (range(KC) if part != "apply" else []):
                sqb = SQ[:, c % 2, :]
                ACT.wait(rms_modulate.sq_free[c % 2])
                if with_pre:
                    act(sqb[:, 0:32], XP[:, c, :], AF.Square)
                act(sqb[:, 32:NH], X[:, c, :], AF.Square)
                ta = ACT.tick()
                PE.wait(ta)
                for ti, (a, b_) in enumerate(tiles):
                    mm(banks[ti][:, 0:b_ - a], ONESB[:], sqb[:, a:b_], c == 0, c == KC - 1)
                rms_modulate.sq_free[c % 2] = PE.tick()
            if part != "apply":
                tp = PE.tick()
                ACT.wait(tp)
                for ti, (a, b_) in enumerate(tiles):
                    act(RSTD[:, a:b_], banks[ti][:, 0:b_ - a], AF.Sqrt, bias=EPSC[:, 0:1], scale=1.0 / D)
                ta = ACT.tick()
                DVE.wait(ta)
                DVE.emit(nc.vector.reciprocal(out=RSTD[:, c0:NH], in_=RSTD[:, c0:NH]))
            if part == "stats":
                return
            for c in range(KC):
                DVE.emit(nc.vector.tensor_scalar(out=AM[:, c, :], in0=MOD[:, li, sc0 + c, :], scalar1=1.0,
                                                 scalar2=VTs[:, nv + 16 * li + c:nv + 16 * li + c + 1],
                                                 op0=ALU.add, op1=ALU.mult))
            td = DVE.tick()
            DVE.wait(td)
            ACT.wait(td)
            tprev = [None, None]
            for c in range(KC):
                Tc = T1 if c % 2 == 0 else T1B
                DVE.wait(tprev[c % 2])
                if with_pre:
                    DVE.emit(nc.vector.tensor_tensor(out=Tc[:, 0:32], in0=XP[:, c, :], in1=RSTD[:, 0:32], op=ALU.mult))
                DVE.emit(nc.vector.tensor_tensor(out=Tc[:, 32:NH], in0=X[:, c, :], in1=RSTD[:, 32:NH], op=ALU.mult))
                td = DVE.tick()
                ACT.wait(td)
                for (a, b_, bi) in SEGH:
                    a2 = max(a, c0)
                    act(HB[:, c, a2:b_], Tc[:, a2:b_], AF.Identity, scale=AM[:, c, bi:bi + 1],
                        bias=MOD[:, li, sh0 + c, bi:bi + 1])
                tprev[c % 2] = ACT.tick()
        rms_modulate.sq_free = [None, None]

        def resid_epilogue(bank, ncols, n, xlo, gsrc, bvec):
            ACT.wait(resid_epilogue.tfree)
            for (a, b_, bi) in segs(xlo, xlo + ncols):
                if bvec is None:
                    act(TMPE[:, a - xlo:b_ - xlo], bank[:, a - xlo:b_ - xlo], AF.Identity, scale=gsrc[:, n, bi:bi + 1])
                else:
                    act(TMPE[:, a - xlo:b_ - xlo], bank[:, a - xlo:b_ - xlo], AF.Identity,
                        scale=gsrc[:, n, bi:bi + 1], bias=bvec[:, n, bi:bi + 1])
            ta = ACT.tick()
            DVE.wait(ta)
            DVE.emit(nc.vector.tensor_tensor(out=X[:, n, xlo:xlo + ncols], in0=X[:, n, xlo:xlo + ncols],
                                             in1=TMPE[:, 0:ncols], op=ALU.add))
            resid_epilogue.tfree = DVE.tick()
            return resid_epilogue.tfree
        resid_epilogue.tfree = None

        def proj_resid(wname, rhs_of, tiles, gch, bname, li):
            for n in range(KC):
                DVE.emit(nc.vector.tensor_scalar_mul(out=G1B[:, n, :], in0=MOD[:, li, gch + n, :],
                                                     scalar1=VTs[:, VO[bname] + n:VO[bname] + n + 1]))
            tg = DVE.tick()
            ACT.wait(tg)
            bfree = [None, None]
            kk = 0
            for b in range(8):
                slot, bi = ring_use(f"{wname}_{b}")
                for h in range(2):
                    n = 2 * b + h
                    for (xlo, ncols) in tiles:
                        bank = PS[kk % 2]
                        PE.wait(bfree[kk % 2])
                        for k in range(KC):
                            mm(bank[:, 0:ncols], slot[:, k, 128 * h:128 * h + 128], rhs_of(k, xlo, ncols), k == 0, k == KC - 1)
                        tp = PE.tick()
                        ACT.wait(tp)
                        resid_epilogue(bank, ncols, n, xlo, MOD[:, li, gch:gch + 16, :], G1B)
                        bfree[kk % 2] = ACT.tick()
                        kk += 1
                ring_release(bi)

        def ffn(li, xlo_all, tiles):
            gch = 80
            ngrp = (NFF + GRP - 1) // GRP
            gfree = [None, None]
            ufree = [None, None]
            sgfree = [None, None]
            dfree = [None, None]
            kk = 0
            dk = 0
            for G in range(ngrp):
                j0, j1 = G * GRP, min(NFF, (G + 1) * GRP)
                for j in range(j0, j1):
                    slot, bi = ring_use(f"gu{li}_{j}")
                    aslot = (G % 2) * GRP + (j - j0)
                    for (xlo, ncols) in tiles:
                        bg, bu = PS[kk % 2], PS[2 + kk % 2]
                        PE.wait(gfree[kk % 2], ufree[kk % 2])
                        for k in range(KC):
                            mm(bg[:, 0:ncols], slot[:, k, 0:128], HB[:, k, 32 + xlo:32 + xlo + ncols], k == 0, k == KC - 1)
                        tpg = PE.tick()
                        for k in range(KC):
                            mm(bu[:, 0:ncols], slot[:, k, 128:256], HB[:, k, 32 + xlo:32 + xlo + ncols], k == 0, k == KC - 1)
                        tpu = PE.tick()
                        ACT.wait(tpg, sgfree[kk % 2])
                        act(SG[:, kk % 2, 0:ncols], bg[:, 0:ncols], AF.Silu)
                        gfree[kk % 2] = ACT.tick()
                        DVE.wait(gfree[kk % 2], tpu)
                        DVE.emit(nc.vector.tensor_tensor(out=ACTR[:, aslot, xlo:xlo + ncols], in0=bu[:, 0:ncols],
                                                         in1=SG[:, kk % 2, 0:ncols], op=ALU.mult))
                        ufree[kk % 2] = DVE.tick()
                        sgfree[kk % 2] = ufree[kk % 2]
                        kk += 1
                    ring_release(bi)
                    if li == 0:
                        mod_block(1, j)
                        if j == NFF - 1:
                            for b in range(NFF, 48):
                                mod_block(1, b)
                t_act = DVE.tick()
                PE.wait(t_act)
                for b in range(8):
                    slot, bi = ring_use(f"dn{li}_{G}_{b}")
                    for h in range(2):
                        n = 2 * b + h
                        for (xlo, ncols) in tiles:
                            bank = PS[6 + dk % 2]
                            PE.wait(dfree[dk % 2])
                            for jj in range(j1 - j0):
                                mm(bank[:, 0:ncols], slot[:, jj, 128 * h:128 * h + 128],
                                   ACTR[:, (G % 2) * GRP + jj, xlo:xlo + ncols], jj == 0, jj == j1 - j0 - 1)
                            tp = PE.tick()
                            ACT.wait(tp)
                            resid_epilogue(bank, ncols, n, xlo, MOD[:, li, gch:gch + 16, :], None)
                            dfree[dk % 2] = ACT.tick()
                            dk += 1
                    ring_release(bi)

        rms_modulate(0, 0, True, part="stats", banks=[PS[1], PS[2], PS[3]])
        for b in range(16):
            mod_block(0, b)
        barrier()
        checkpoint(1)
        rms_modulate(0, 0, True, part="apply")
        barrier()
        checkpoint(2)

        t_st = None
        for b in range(2):
            t_st = dma(POOL, UB[:, :, 1184 + 46 * b:1184 + 46 * b + 30], stv[b], ld2_sem)
        tiles1 = [(0, 512), (512, 1024), (1024, NH)]
        afree = [None, None]
        gfree = [None, None]
        sgfree = [None, None]
        uf_t = [None, None]
        kk = 0
        for c in range(KC):
            slot, bi = ring_use(f"pw1_{c}")
            for (a, b_) in tiles1:
                ncols = b_ - a
                ba, bgt = PS[kk % 2], PS[2 + kk % 2]
                PE.wait(afree[kk % 2], gfree[kk % 2])
                for k in range(KC):
                    mm(ba[:, 0:ncols], slot[:, k, 0:128], HB[:, k, a:b_], k == 0, k == KC - 1)
                tpa = PE.tick()
                for k in range(KC):
                    mm(bgt[:, 0:ncols], slot[:, k, 128:256], HB[:, k, a:b_], k == 0, k == KC - 1)
                tpg = PE.tick()
                ACT.wait(tpg, sgfree[kk % 2])
                act(SG[:, kk % 2, 0:ncols], bgt[:, 0:ncols], AF.Sigmoid, bias=VTs[:, VO["bpg"] + c:VO["bpg"] + c + 1])
                gfree[kk % 2] = ACT.tick()
                DVE.wait(gfree[kk % 2], tpa)
                ba_s = VTs[:, VO["bpa"] + c:VO["bpa"] + c + 1]

                def glu(out, lo, hi):
                    DVE.emit(nc.vector.scalar_tensor_tensor(out=out, in0=ba[:, lo:hi], scalar=ba_s,
                                                            in1=SG[:, kk % 2, lo:hi], op0=ALU.add, op1=ALU.mult))
                if a < 1024:
                    glu(UB[:, c, a:b_], 0, ncols)
                else:
                    glu(UB[:, c, 1024:1184], 0, 160)
                    glu(UB[:, c, 1214:1230], 160, 176)
                    glu(UB[:, c, 1260:1276], 176, 192)
                    DVE.wait(uf_t[c % 2])
                    glu(UFs[:, c % 2, 0:30], 130, 160)
                    glu(UFs[:, c % 2, 30:62], 160, 192)
                    tu = DVE.tick()
                    SP.wait(tu)
                    uf_t[c % 2] = dma(SP, ucT[128 * c:128 * c + 128, :], UFs[:, c % 2, :], uf_sem[c % 2])
                afree[kk % 2] = DVE.tick()
                sgfree[kk % 2] = afree[kk % 2]
                if a == 0:
                    DVE.wait(afree[kk % 2])
                    DVE.emit(nc.vector.tensor_scalar_mul(out=UB[:, c, 0:160], in0=UB[:, c, 0:160], scalar1=FLAG[:, 0:1]))
                kk += 1
            ring_release(bi)
            mod_block(0, 16 + c)
        DVE.wait(t_st)
        barrier()
        checkpoint(3)

        SP.wait(PE.tick(), ACT.tick(), DVE.tick())
        tw = dma(SP, WDW, wdwT.rearrange("p (w c) -> p w c", c=16), ld_sem)
        DVE.wait(tw)
        ACT.wait(tw)
        ctiles = [(0, 384), (384, 384), (768, 416)]
        units = [(zlo, zn, c) for (zlo, zn) in ctiles for c in range(KC)]
        dgfree = [None, None]
        cfree = [None, None, None, None]
        zdone = [None] * KC
        zbfree = [None, None]
        conv_t = {}
        st_read = [None]
        NDV = 26

        def emit_conv(u):
            zlo, zn, c = units[u]
            b = u % 2
            samp = zn > 384
            DVE.wait(dgfree[b])
            ACT.wait(dgfree[b])
            DVE.emit(nc.vector.tensor_tensor(out=DG[:, b, 0:NDV, :], in0=IDB[:].unsqueeze(1).to_broadcast([128, NDV, 128]),
                                             in1=WDW[:, 0:NDV, c:c + 1].to_broadcast([128, NDV, 128]), op=ALU.mult))
            td = DVE.tick()
            for w in range(NDV, 31):
                act(DG[:, b, w, :], IDB[:], AF.Identity, scale=WDW[:, w, c:c + 1])
            ta = ACT.tick()
            bank = PS[u % 4]
            PE.wait(td, ta, cfree[u % 4])
            for w in range(31):
                mm(bank[:, 0:384], DG[:, b, w, :], UB[:, c, zlo + 2 + w:zlo + 2 + w + 384], w == 0, w == 30)
            if samp:
                for w in range(31):
                    mm(bank[:, 384:416], DG[:, b, w, :],
                       UB[:, c, 1184:1276].rearrange("p (b t) -> p b t", t=46)[:, :, w:w + 16], w == 0, w == 30)
            conv_t[u] = PE.tick()
            dgfree[b] = conv_t[u]

        def emit_post(u):
            zlo, zn, c = units[u]
            b = u % 2
            bank = PS[u % 4]
            bdw = VTs[:, VO["bdw"] + c:VO["bdw"] + c + 1]
            ACT.wait(conv_t[u], zdone[c])
            act(Z[:, c, 0:zn], bank[:, 0:zn], AF.Identity, bias=bdw)
            tz = ACT.tick()
            cfree[u % 4] = tz
            DVE.wait(tz, zbfree[b])
            DVE.emit(nc.vector.tensor_copy(out=ZB[:, b, 0:zn], in_=Z[:, c, 0:zn]))
            td = DVE.tick()
            ACT.wait(tz, zbfree[b])
            act(ZSQ[:, b, 0:zn], Z[:, c, 0:zn], AF.Square)
            ta = ACT.tick()
            PE.wait(td, ta)
            if c == 0:
                PE.wait(st_read[0])
            mm(PS[4][:, 0:zn], ONESB[:], ZB[:, b, 0:zn], c == 0, c == KC - 1)
            mm(PS[5][:, 0:zn], ONESB[:], ZSQ[:, b, 0:zn], c == 0, c == KC - 1)
            zbfree[b] = PE.tick()
            if c != KC - 1:
                return
            ts = zbfree[b]
            DVE.wait(ts)
            DVE.emit(nc.vector.tensor_scalar_mul(out=MEAN[:, 0:zn], in0=PS[4][:, 0:zn], scalar1=1.0 / D))
            t1_ = DVE.tick()
            DVE.wait(t1_)
            DVE.emit(nc.vector.tensor_tensor(out=MSQ[:, 0:zn], in0=MEAN[:, 0:zn], in1=MEAN[:, 0:zn], op=ALU.mult))
            t2_ = DVE.tick()
            DVE.wait(t2_)
            DVE.emit(nc.vector.scalar_tensor_tensor(out=RSC[:, 0:zn], in0=PS[5][:, 0:zn], scalar=1.0 / D, in1=MSQ[:, 0:zn],
                                                    op0=ALU.mult, op1=ALU.subtract))
            t3_ = DVE.tick()
            st_read[0] = t3_
            ACT.wait(t3_)
            act(RSC[:, 0:zn], RSC[:, 0:zn], AF.Sqrt, bias=EPSC[:, 0:1], scale=1.0)
            t4_ = ACT.tick()
            DVE.wait(t4_)
            DVE.emit(nc.vector.reciprocal(out=RSC[:, 0:zn], in_=RSC[:, 0:zn]))
            t5_ = DVE.tick()
            DVE.wait(t5_)
            for cc in range(KC):
                DVE.emit(nc.vector.tensor_tensor(out=Z[:, cc, 0:zn], in0=Z[:, cc, 0:zn], in1=MEAN[:, 0:zn], op=ALU.subtract))
                tq = DVE.tick()
                DVE.wait(tq)
                DVE.emit(nc.vector.tensor_tensor(out=Z[:, cc, 0:zn], in0=Z[:, cc, 0:zn], in1=RSC[:, 0:zn], op=ALU.mult))
                tq = DVE.tick()
                ACT.wait(tq)
                act(UB[:, cc, zlo:zlo + zn], Z[:, cc, 0:zn], AF.Silu, scale=VTs[:, VO["lng"] + cc:VO["lng"] + cc + 1],
                    bias=VTs[:, VO["lnb"] + cc:VO["lnb"] + cc + 1])
                zdone[cc] = ACT.tick()

        emit_conv(0)
        nmod = 32
        pending = None
        for u in range(len(units)):
            if u + 1 < len(units):
                emit_conv(u + 1)
            if pending is not None:
                pending()
                pending = None
            emit_post(u)
            if u % 2 == 1 and nmod < 48:
                pending = mod_block(0, nmod, pbase=6, defer=True)
                nmod += 1
        if pending is not None:
            pending()
        while nmod < 48:
            mod_block(0, nmod, pbase=6)
            nmod += 1
        barrier()
        checkpoint(4)

        tilesX = [(0, 512), (512, 512), (1024, 160)]
        proj_resid("pw2", lambda k, xlo, n_: UB[:, k, xlo:xlo + n_], tilesX, 32, "bpw2", 0)
        barrier()
        checkpoint(5)

        rms_modulate(0, 1, False)
        barrier()
        ffn(0, 0, tilesX)
        barrier()
        checkpoint(6)

        rms_modulate(1, 0, False)
        tb = barrier()
        checkpoint(7)
        ring["hold"] = tb
        POOL.wait(*tb)
        t_c = None
        for b in range(2):
            dma(POOL, KT[:, :, 1152 + 144 * b:1152 + 144 * b + 128], ckT[b].rearrange("g p k -> p g k"), ld2_sem)
            t_c = dma(POOL, VTM[:, 18 + 2 * b, :], cv[b], ld2_sem)
        SP.wait(*tb)
        t_b = dma(SP, BKVB[:, :], bkv_d.partition_broadcast(128), ld_sem)

        tilesK = [(0, 512), (512, 512), (1024, 160)]
        slot, bi = ring_use("kv_k")
        kfree = [None, None]
        kk = 0

        def ktcol(xlo):
            return xlo if xlo < 1152 else (1280 if xlo < 1168 else 1424)
        for g in range(4):
            for (xlo, ncols) in tilesK:
                bank = PS[kk % 2]
                PE.wait(kfree[kk % 2])
                for half in range(2):
                    for k in range(KC):
                        mm(bank[64 * half:64 * half + 64, 0:ncols], slot[:, k, 64 * g:64 * g + 64],
                           HB[:, k, 32 + xlo:32 + xlo + ncols], k == 0, k == KC - 1, tile_position=(0, 64 * half))
                tp = PE.tick()
                ACT.wait(tp)
                bk = VTs[:, VO["bkd"] + g:VO["bkd"] + g + 1]
                if xlo < 1024:
                    act(KT[:, g, xlo:xlo + ncols], bank[:, 0:ncols], AF.Identity, bias=bk)
                else:
                    act(KT[:, g, 1024:1152], bank[:, 0:128], AF.Identity, bias=bk)
                    act(KT[:, g, 1280:1296], bank[:, 128:144], AF.Identity, bias=bk)
                    act(KT[:, g, 1424:1440], bank[:, 144:160], AF.Identity, bias=bk)
                kfree[kk % 2] = ACT.tick()
                kk += 1
        DVE.wait(t_b)
        kv_t = [None, None]
        so = 0

        def tm_out(slot_, which, items):
            nonlocal so
            for (hcol, rows, dst) in items:
                bank = PS[2 + so % 2]
                PE.wait(tm_out.bfree[so % 2])
                for k in range(KC):
                    mm(bank[0:rows, 0:256], HB[:, k, hcol:hcol + rows], slot_[:, k, 0:256], k == 0, k == KC - 1)
                tp = PE.tick()
                DVE.wait(tp, kv_t[so % 2])
                DVE.emit(nc.vector.tensor_tensor(out=KVST[0:rows, so % 2, :], in0=bank[0:rows, 0:256],
                                                 in1=BKVB[0:rows, 256 * which:256 * which + 256], op=ALU.add))
                td = DVE.tick()
                tm_out.bfree[so % 2] = td
                SP.wait(td)
                kv_t[so % 2] = dma(SP, dst, KVST[0:rows, so % 2, :], kvst_sem[so % 2])
                so += 1
        tm_out.bfree = [None, None]
        tm_out(slot, 0, [(32 + 1024, 64, kp_o[0:64, :]), (32 + 1088, 64, kp_o[64:128, :]),
                         (32 + 1152, 16, ks_o[0, 112:128, :]), (32 + 1168, 16, ks_o[1, 112:128, :])])
        ring_release(bi)
        slot, bi = ring_use("kv_v")
        vfree = [None, None]
        vchunks = [(32 + 64 * i, 128 if i < 17 else 64, i) for i in range(18)] + [(32 + 1152, 16, 19), (32 + 1168, 16, 21)]
        for vi, (hcol, rows, vs) in enumerate(vchunks):
            bank = PS[4 + vi % 2]
            PE.wait(vfree[vi % 2])
            for k in range(KC):
                mm(bank[0:rows, 0:256], HB[:, k, hcol:hcol + rows], slot[:, k, 0:256], k == 0, k == KC - 1)
            tp = PE.tick()
            DVE.wait(tp)
            DVE.emit(nc.vector.tensor_tensor(out=VTM[0:rows, vs, :], in0=bank[0:rows, 0:256], in1=BKVB[0:rows, 256:512],
                                             op=ALU.add))
            if vs < 2:
                mr = 128 if vs == 0 else 64
                tq = DVE.tick()
                DVE.wait(tq)
                DVE.emit(nc.vector.tensor_scalar_mul(out=VTM[0:mr, vs, :], in0=VTM[0:mr, vs, :], scalar1=FLAG[0:mr, 0:1]))
            vfree[vi % 2] = DVE.tick()
        tm_out(slot, 1, [(32 + 1024, 64, vp_o[0:64, :]), (32 + 1088, 64, vp_o[64:128, :]),
                         (32 + 1152, 16, vs_o[0, 112:128, :]), (32 + 1168, 16, vs_o[1, 112:128, :])])
        ring_release(bi)
        barrier()
        checkpoint(8)
        tilesQ = [(128, 352), (480, 352), (832, 352)]
        qfree = [None, None]
        kk = 0
        for b in range(8):
            slot, bi = ring_use(f"q_{b}")
            for h in range(2):
                n = 2 * b + h
                for (xlo, ncols) in tilesQ:
                    bank = PS[kk % 2]
                    PE.wait(qfree[kk % 2])
                    for k in range(KC):
                        mm(bank[:, 0:ncols], slot[:, k, 128 * h:128 * h + 128], HB[:, k, 32 + xlo:32 + xlo + ncols],
                           k == 0, k == KC - 1)
                    tp = PE.tick()
                    ACT.wait(tp)
                    act(QT[:, n, xlo - 128:xlo - 128 + ncols], bank[:, 0:ncols], AF.Identity,
                        bias=VTs[:, VO["bq"] + n:VO["bq"] + n + 1])
                    qfree[kk % 2] = ACT.tick()
                    kk += 1
            ring_release(bi)
        barrier()

        checkpoint(9)
        SP.wait(PE.tick(), ACT.tick(), DVE.tick())
        t1 = dma(SP, OHS[0:32, :], ohrel, ld_sem)
        t2 = dma(SP, TBT[0:32, :], tableT, ld_sem)
        t3 = dma(SP, ES64[:, :], sinks_d.partition_broadcast(128), ld_sem)
        ACT.wait(t3)
        act(ES64[:, :], ES64[:, :], AF.Exp)
        ta = ACT.tick()
        DVE.wait(ta)
        PE.wait(t3, t_c)

        qblocks = []
        for qc in range(16):
            m01 = FLAGB if qc == 0 else (FLAGH if qc == 1 else ONESB)
            qblocks.append((64 * qc, 64, [(64 * qc, qc, 128, m01), (64 * qc + 128, qc + 2, 64, ONESB)]))
        for b in range(2):
            qblocks.append((1024 + 16 * b, 16, [(1152 + 144 * b, 18 + 2 * b, 128, ONESB),
                                               (1280 + 144 * b, 19 + 2 * b, 16, ONESB)]))
        barrier()
        for kt, (j0, nj) in enumerate([(0, 128), (128, 64)]):
            for q in range(64):
                mm(PS[q // 16][0:nj, 32 * (q % 16):32 * (q % 16) + 32], OHS[0:32, j0 + 63 - q:j0 + 63 - q + nj],
                   TBT[0:32, 0:32], True, True)
            tp = PE.tick()
            DVE.wait(tp)
            for g in range(4):
                for bq in range(4):
                    DVE.emit(nc.vector.tensor_copy(
                        out=BIAS[0:nj, g, kt, :].rearrange("j (par pair q) -> j par pair q", par=2, pair=4)[:, :, :, 16 * bq:16 * bq + 16],
                        in_=PS[bq][0:nj, :].rearrange("j (q h) -> j q h", h=32)[:, :, 8 * g:8 * g + 8]
                            .rearrange("j q (pair par) -> j par pair q", par=2)))
            td = DVE.tick()
            PE.wait(td)
        barrier()
        sfree = [None, None, None]
        ofree = [None, None]
        ssfree = [None, None, None]
        ptfree = {}
        exp_t = {}
        gi = 0
        for g in range(4):
            DVE.wait(PE.tick())
            for par in range(2):
                for b3 in range(3):
                    DVE.emit(nc.vector.tensor_copy(
                        out=PT[64:65, 3 * par + b3, 256:512].rearrange("o (a q) -> o a q", q=64),
                        in_=ES64[64:65, 8 * g + par:8 * g + 8:2].unsqueeze(2).to_broadcast([1, 4, 64])))
            tsk = DVE.tick()
            PE.wait(tsk)
            checkpoint(50)
            its = [(qcol, nq, keys, par) for (qcol, nq, keys) in qblocks for par in range(2)]

            st = {}

            def S_mm(i):
                qcol, nq, keys, par = its[i]
                b3 = (gi + i) % 3
                pp = slice(64 * par, 64 * par + 64)
                bS = PS[b3]
                PE.wait(sfree[b3])
                for kt, (kcol, vs, nk, msk) in enumerate(keys):
                    mm(bS[0:nk, 256 * kt:256 * kt + 4 * nq], KT[pp, g, kcol:kcol + nk], QT[pp, 4 * g:4 * g + 4, qcol:qcol + nq],
                       True, True, tile_position=(64 * par, 0))
                st[("s", i)] = PE.tick()

            def S_bias(i):
                qcol, nq, keys, par = its[i]
                b3 = (gi + i) % 3
                bS = PS[b3]
                DVE.wait(st[("s", i)], ssfree[b3])
                for kt, (kcol, vs, nk, msk) in enumerate(keys):
                    DVE.emit(nc.vector.scalar_tensor_tensor(
                        out=SS[0:nk, b3, 256 * kt:256 * kt + 4 * nq].rearrange("j (a q) -> j a q", q=nq),
                        in0=bS[0:nk, 256 * kt:256 * kt + 4 * nq].rearrange("j (a q) -> j a q", q=nq), scalar=0.125,
                        in1=BIAS[0:nk, g, kt, 256 * par:256 * par + 256].rearrange("j (a q) -> j a q", q=64)[:, :, 0:nq],
                        op0=ALU.mult, op1=ALU.add))
                td = DVE.tick()
                sfree[b3] = td
                st[("b", i)] = td

            def S_exp(i):
                qcol, nq, keys, par = its[i]
                b3 = (gi + i) % 3
                pb = 3 * par + b3
                ACT.wait(st[("b", i)], ptfree.get(pb))
                for kt, (kcol, vs, nk, msk) in enumerate(keys):
                    act(PT[0:nk, pb, 256 * kt:256 * kt + 4 * nq], SS[0:nk, b3, 256 * kt:256 * kt + 4 * nq], AF.Exp)
                ta = ACT.tick()
                ssfree[b3] = ta
                exp_t[i] = ta

            def PV_mm(i):
                qcol, nq, keys, par = its[i]
                b3 = (gi + i) % 3
                pb = 3 * par + b3
                b2 = (gi + i) % 2
                pp = slice(64 * par, 64 * par + 64)
                bO = PS[3 + b2]
                PE.wait(exp_t[i], ofree[b2])
                for kt, (kcol, vs, nk, msk) in enumerate(keys):
                    mm(bO[pp, 0:4 * nq], VTM[0:nk, vs, 64 * g:64 * g + 64], PT[0:nk, pb, 256 * kt:256 * kt + 4 * nq],
                       kt == 0, kt == 1, tile_position=(0, 64 * par))
                (kcol, vs, nk, msk) = keys[0]
                mm(bO[pp, 256:256 + 4 * nq], msk[0:nk, 0:64], PT[0:nk, pb, 0:4 * nq], True, False, tile_position=(0, 64 * par))
                (kcol, vs, nk, msk) = keys[1]
                if nq == 64:
                    mm(bO[pp, 256:512], ONESB[0:65, 0:64], PT[0:65, pb, 256:512], False, True, tile_position=(0, 64 * par))
                else:
                    mm(bO[pp, 256:256 + 4 * nq], ONESB[0:nk, 0:64], PT[0:nk, pb, 256:256 + 4 * nq], False, True,
                       tile_position=(0, 64 * par))
                tp = PE.tick()
                ptfree[pb] = tp
                st[("pv", i)] = tp

            def PV_ln(i):
                qcol, nq, keys, par = its[i]
                b2 = (gi + i) % 2
                pp = slice(64 * par, 64 * par + 64)
                bO = PS[3 + b2]
                tp = st[("pv", i)]
                if nq == 64:
                    ACT.wait(tp, ofree[b2])
                    act(RDEN[pp, b2, 0:4 * nq], bO[pp, 256:256 + 4 * nq], AF.Ln)
                else:
                    DVE.wait(tp, ofree[b2])
                    DVE.emit(nc.vector.tensor_tensor(
                        out=RDEN[pp, b2, 0:4 * nq].rearrange("d (a q) -> d a q", q=nq),
                        in0=bO[pp, 256:256 + 4 * nq].rearrange("d (a q) -> d a q", q=nq),
                        in1=ES64[pp, 8 * g + par:8 * g + 8:2].unsqueeze(2).to_broadcast([64, 4, nq]), op=ALU.add))
                    tdd = DVE.tick()
                    ACT.wait(tdd)
                    act(RDEN[pp, b2, 0:4 * nq], RDEN[pp, b2, 0:4 * nq], AF.Ln)
                ta = ACT.tick()
                ACT.wait(ta)
                act(RDEN[pp, b2, 0:4 * nq], RDEN[pp, b2, 0:4 * nq], AF.Exp, scale=-1.0)
                st[("ln", i)] = ACT.tick()

            def PV_mult(i):
                qcol, nq, keys, par = its[i]
                b2 = (gi + i) % 2
                pp = slice(64 * par, 64 * par + 64)
                bO = PS[3 + b2]
                DVE.wait(st[("ln", i)], st[("pv", i)])
                DVE.emit(nc.vector.tensor_tensor(
                    out=QT[pp, 4 * g:4 * g + 4, qcol:qcol + nq],
                    in0=bO[pp, 0:4 * nq].rearrange("d (a q) -> d a q", q=nq),
                    in1=RDEN[pp, b2, 0:4 * nq].rearrange("d (a q) -> d a q", q=nq), op=ALU.mult))
                ofree[b2] = DVE.tick()

            n_it = len(its)
            for i in range(2):
                S_mm(i)
                S_bias(i)
                S_exp(i)
            for i in range(n_it):
                PV_mm(i)
                nxt = i + 2 < n_it
                if nxt:
                    S_mm(i + 2)
                    S_bias(i + 2)
                PV_ln(i)
                PV_mult(i)
                if nxt:
                    S_exp(i + 2)
            gi += n_it
        barrier()

        checkpoint(10)
        proj_resid("wo", lambda k, xlo, n_: QT[:, k, xlo - 128:xlo - 128 + n_], tilesQ, 32, "bo", 1)
        barrier()

        checkpoint(11)
        rms_modulate(1, 1, False)
        barrier()
        DVE.wait(kv_t[0], kv_t[1])
        ffn(1, 128, tilesQ)
        barrier()

        checkpoint(12)
        tilesF = [(128, 512), (640, 512), (1152, 32)]
        banks = [PS[5], PS[6], PS[7]]
        sqf = [None, None]
        for c in range(KC):
            ACT.wait(sqf[c % 2])
            act(SQ[:, c % 2, 32:NH], X[:, c, :], AF.Square)
            ta = ACT.tick()
            PE.wait(ta)
            for ti, (a, n_) in enumerate(tilesF):
                mm(banks[ti][:, 0:n_], ONESB[:], SQ[:, c % 2, 32 + a:32 + a + n_], c == 0, c == KC - 1)
            sqf[c % 2] = PE.tick()
        tp = PE.tick()
        ACT.wait(tp)
        for ti, (a, n_) in enumerate(tilesF):
            act(RSTD[:, 32 + a:32 + a + n_], banks[ti][:, 0:n_], AF.Sqrt, bias=EPSC[:, 0:1], scale=1.0 / D)
        ta = ACT.tick()
        DVE.wait(ta)
        DVE.emit(nc.vector.reciprocal(out=RSTD[:, 160:NH], in_=RSTD[:, 160:NH]))
        td = DVE.tick()
        barrier()
        YS = carve(SCR, 11776, (128, 2, NQ), F32)
        y_t = [None, None]
        for c in range(KC):
            DVE.wait(y_t[c % 2])
            DVE.emit(nc.vector.scalar_tensor_tensor(out=YS[:, c % 2, :], in0=X[:, c, 128:NX],
                                                    scalar=VTs[:, VO["nout"] + c:VO["nout"] + c + 1],
                                                    in1=RSTD[:, 160:NH], op0=ALU.mult, op1=ALU.mult))
            td = DVE.tick()
            SP.wait(td)
            y_t[c % 2] = dma(SP, yT[128 * c:128 * c + 128, :], YS[:, c % 2, :], yst_sem[c % 2])
        SP.wait(y_t[0], y_t[1], uf_t[0], uf_t[1], kv_t[0], kv_t[1], (st_sem, 16 * cnt[id(st_sem)]))
    return nc


_NC_CACHE = {}


def _host_inputs(inp):
    f = lambda a: np.ascontiguousarray(np.asarray(a, dtype=np.float32))
    xp, xs = f(inp["x_prompt"]), f(inp["x_sample"])
    cp, cs = f(inp["c_prompt"]), f(inp["c_sample"])
    stc, ckc, cvc = f(inp["state_conv"])[0], f(inp["cache_win_k"])[0], f(inp["cache_win_v"])[0]
    w_pw1 = f(inp["w_pw1"])[0]
    wpw1 = np.ascontiguousarray(np.stack([w_pw1[:, :D].reshape(D, 16, 128), w_pw1[:, D:].reshape(D, 16, 128)], axis=2).reshape(D, 2 * D))
    w_gu = f(inp["w_gu"])
    wgu = np.ascontiguousarray(np.stack([w_gu[:, :, :DFF].reshape(2, D, NFF, 128), w_gu[:, :, DFF:].reshape(2, D, NFF, 128)], axis=3).reshape(2, D, 2 * DFF))
    b_qkv = f(inp["b_qkv"])[0]
    b_pw1 = f(inp["b_pw1"])[0]
    bkd = np.stack([np.concatenate([b_qkv[2048 + 64 * g:2048 + 64 * g + 64]] * 2) for g in range(4)])
    rows = [f(inp["b_mod"]).reshape(192, 128), f(inp["norm_mix"]).reshape(32, 128), f(inp["norm_ffn"]).reshape(32, 128),
            b_pw1[:D].reshape(16, 128), b_pw1[D:].reshape(16, 128), f(inp["b_dw"]).reshape(16, 128),
            f(inp["conv_ln_g"]).reshape(16, 128), f(inp["conv_ln_b"]).reshape(16, 128), f(inp["b_pw2"]).reshape(16, 128),
            b_qkv[:2048].reshape(16, 128), bkd, f(inp["b_o"]).reshape(16, 128), f(inp["norm_out"]).reshape(16, 128)]
    vecs = np.concatenate(rows, axis=0)
    assert vecs.shape[0] == NVEC
    vecsT = np.ascontiguousarray(vecs.T)
    wdwT = np.ascontiguousarray(f(inp["w_dw"])[0].reshape(31, 16, 128).transpose(2, 0, 1).reshape(128, 31 * 16))
    rel = np.arange(255) - 191
    bk = t5_bucket_np(rel)
    ohrel = np.zeros((32, 255), np.float32)
    ohrel[bk, np.arange(255)] = 1.0
    def tile_w(W, nl):
        N = W.shape[1]
        return np.ascontiguousarray(W.reshape(nl, KC, 128, N // 256, 256).transpose(0, 3, 2, 1, 4).reshape(nl * (N // 256), 128, KC, 256))

    w_dn = f(inp["w_down"])
    wdn_t = np.zeros((2, 6, 8, 128, GRP, 256), np.float32)
    for li in range(2):
        for G in range(6):
            j0, j1 = G * GRP, min(NFF, (G + 1) * GRP)
            blk = w_dn[li, j0 * 128:j1 * 128, :].reshape(j1 - j0, 128, 8, 256)
            wdn_t[li, G, :, :, 0:j1 - j0, :] = blk.transpose(2, 1, 0, 3)
    wdn_t = wdn_t.reshape(96, 128, GRP, 256)
    shared = {
        "vecsT": vecsT, "wdwT": wdwT, "bkv": np.ascontiguousarray(b_qkv[2048:].reshape(1, 512)), "ohrel": ohrel,
        "tableT": np.ascontiguousarray(f(inp["rel_bias_table"]).T), "sinks": f(inp["attn_sinks"]).reshape(1, 32),
        "identf": np.eye(128, dtype=np.float32),
        "wmod": tile_w(f(inp["w_mod"]).reshape(2 * D, 6 * D), 2),
        "wpw1": tile_w(wpw1, 1), "wpw2": tile_w(f(inp["w_pw2"])[0], 1), "wqkv": tile_w(f(inp["w_qkv"])[0], 1),
        "wo": tile_w(f(inp["w_o"])[0], 1), "wgu": tile_w(wgu.reshape(2 * D, 2 * DFF), 2), "wdn": wdn_t,
    }
    maps = []
    for i in range(NCORES):
        b, seg = i // 4, i % 4
        t0 = seg * 1024
        xin = np.zeros((NH, D), np.float32)
        lo = t0 - 160
        if lo >= 0:
            xin[0:160] = xp[b, lo:t0]
        xin[160:1184] = xp[b, t0:t0 + 1024]
        xin[1184:1200] = xs[2 * i]
        xin[1200:1216] = xs[2 * i + 1]
        cvec = np.stack([cp[b], cs[2 * i], cs[2 * i + 1]])
        m = dict(shared)
        m["xT"] = np.ascontiguousarray(xin.T)
        m["cT"] = np.ascontiguousarray(cvec.reshape(3, 16, 128).transpose(2, 1, 0))
        m["flag"] = np.full((128, 1), 1.0 if seg > 0 else 0.0, np.float32)
        m["stT"] = np.ascontiguousarray(stc[2 * i:2 * i + 2].transpose(0, 2, 1))
        kk = ckc[2 * i:2 * i + 2]
        kT = kk.transpose(0, 2, 3, 1)
        m["ckT"] = np.ascontiguousarray(np.concatenate([kT, kT], axis=2))
        m["ck"] = np.ascontiguousarray(kk.reshape(2, 128, 256))
        m["cv"] = np.ascontiguousarray(cvc[2 * i:2 * i + 2].reshape(2, 128, 256))
        maps.append(m)
    return maps


def kernel(**inputs):
    if "nc" not in _NC_CACHE:
        _NC_CACHE["nc"] = build_nc()
    nc = _NC_CACHE["nc"]
    maps = _host_inputs(inputs)
    res = run_bass_kernel_spmd(nc, maps, core_ids=list(range(NCORES)))
    R = res.results
    stc = np.asarray(inputs["state_conv"], dtype=np.float32)
    y_prompt = np.zeros((2, SEQ, D), np.float32)
    y_sample = np.zeros((16, 16, D), np.float32)
    conv_prompt = np.zeros((1, 2, 30, D), np.float32)
    wk_p = np.zeros((1, 2, 128, 4, 64), np.float32)
    wv_p = np.zeros((1, 2, 128, 4, 64), np.float32)
    conv_sample = np.zeros((1, 16, 30, D), np.float32)
    wk_s = np.zeros((1, 16, 128, 4, 64), np.float32)
    wv_s = np.zeros((1, 16, 128, 4, 64), np.float32)
    for i in range(NCORES):
        b, seg = i // 4, i % 4
        r = R[i]
        yT = np.asarray(r["yT"])
        y_prompt[b, seg * 1024:(seg + 1) * 1024] = yT[:, 0:1024].T
        y_sample[2 * i] = yT[:, 1024:1040].T
        y_sample[2 * i + 1] = yT[:, 1040:1056].T
        uc = np.asarray(r["ucT"])
        hd = np.asarray(r["cs_head"])
        for j in range(2):
            conv_sample[0, 2 * i + j, 0:14] = hd[j].T
            conv_sample[0, 2 * i + j, 14:30] = uc[:, 30 + 16 * j:46 + 16 * j].T
            wk_s[0, 2 * i + j] = np.asarray(r["ks_o"])[j].reshape(128, 4, 64)
            wv_s[0, 2 * i + j] = np.asarray(r["vs_o"])[j].reshape(128, 4, 64)
        if seg == 3:
            conv_prompt[0, b] = uc[:, 0:30].T
            wk_p[0, b] = np.asarray(r["kp_o"]).reshape(128, 4, 64)
            wv_p[0, b] = np.asarray(r["vp_o"]).reshape(128, 4, 64)
    return (y_prompt, y_sample, conv_prompt, wk_p, wv_p, conv_sample, wk_s, wv_s)
```
